# Optimizing a Trainium2 kernel written in Bass

```python
import math
import jax, jax.numpy as jnp
from jax import lax
import numpy as np

D_MODEL = 1024
BATCH = 8
SEQ = 2048
DEPTH = 2
DEC_BATCH = 128
DEC_SEQ = 4
PAST_LEN = 16384
PAGE_SIZE = 128

SSD_INNER = D_MODEL
SSD_HEAD_DIM = 64
SSD_HEADS = SSD_INNER // SSD_HEAD_DIM
SSD_GROUPS = 2
SSD_STATE = 64
SSD_CONV = 4
SSD_CONV_CH = SSD_INNER + 2 * SSD_GROUPS * SSD_STATE
MLSTM_INNER = D_MODEL
MLSTM_HEADS = 4
MLSTM_HEAD_DIM = MLSTM_INNER // MLSTM_HEADS
CHUNK = 128
D_FF = ((8 * D_MODEL // 3 + 255) // 256) * 256
FFN_CONV = 3
IN_SPLITS = (SSD_INNER, SSD_CONV_CH, SSD_HEADS, MLSTM_INNER, MLSTM_INNER, MLSTM_INNER,
             2 * MLSTM_HEADS, MLSTM_INNER, 2 * D_MODEL)
N_IN = SSD_INNER + SSD_CONV_CH + SSD_HEADS + 4 * MLSTM_INNER + 2 * MLSTM_HEADS + 2 * D_MODEL
ALPHA = (2 * DEPTH) ** 0.25
BETA = (8 * DEPTH) ** -0.25
EPS = 1e-5

kernel_name = "hybrid_ssd_mlstm_convffn_step"


def _layer_norm(x, g, b):
    xf = x.astype(jnp.float32)
    mu = jnp.mean(xf, axis=-1, keepdims=True)
    var = jnp.mean(jnp.square(xf - mu), axis=-1, keepdims=True)
    y = (xf - mu) * lax.rsqrt(var + EPS) * g.astype(jnp.float32) + b.astype(jnp.float32)
    return y.astype(x.dtype)


def _rms_norm(xf, w):
    return xf * lax.rsqrt(jnp.mean(jnp.square(xf), axis=-1, keepdims=True) + EPS) * w.astype(jnp.float32)


def _causal_dwconv(x, buf, w, b):
    length = x.shape[1]
    width = w.shape[0]
    xp = jnp.concatenate([buf.astype(x.dtype), x], axis=1)
    y = xp[:, 0:length] * w[0]
    for j in range(1, width):
        y = y + xp[:, j:j + length] * w[j]
    return y + b, xp[:, length:]


def _chunk_len(length):
    q = min(length, CHUNK)
    if length % q:
        q = math.gcd(length, CHUNK)
    return q


def _ssd_chunked(xh, dt, a, bm, cm, h0):
    bsz, length, n_heads, p_dim = xh.shape
    n_groups, n_state = bm.shape[2], bm.shape[3]
    hpg = n_heads // n_groups
    q = _chunk_len(length)
    nc = length // q
    xdt = (xh * dt[..., None]).reshape(bsz, nc, q, n_groups, hpg, p_dim)
    acs = jnp.cumsum((dt * a).reshape(bsz, nc, q, n_groups, hpg), axis=2)
    bc = bm.reshape(bsz, nc, q, n_groups, n_state)
    cc = cm.reshape(bsz, nc, q, n_groups, n_state)
    seg = acs[:, :, :, None] - acs[:, :, None, :]
    causal = jnp.tril(jnp.ones((q, q), dtype=bool))[:, :, None, None]
    decay = jnp.exp(jnp.where(causal, seg, -jnp.inf))
    cb = jnp.einsum('bclgn,bcsgn->bclsg', cc, bc)
    y_diag = jnp.einsum('bclsg,bclsgk,bcsgkp->bclgkp', cb, decay, xdt)
    decay_end = jnp.exp(acs[:, :, -1:] - acs)
    local = jnp.einsum('bclgn,bclgk,bclgkp->bcgkpn', bc, decay_end, xdt)
    chunk_decay = jnp.exp(acs[:, :, -1])

    def step(h, inp):
        loc, dec = inp
        return h * dec[..., None, None] + loc, h

    h_last, h_in = lax.scan(step, h0.reshape(bsz, n_groups, hpg, p_dim, n_state),
                            (jnp.moveaxis(local, 1, 0), jnp.moveaxis(chunk_decay, 1, 0)))
    h_in = jnp.moveaxis(h_in, 0, 1)
    y_off = jnp.einsum('bclgn,bcgkpn,bclgk->bclgkp', cc, h_in, jnp.exp(acs))
    y = (y_diag + y_off).reshape(bsz, length, n_heads, p_dim)
    return y, h_last.reshape(bsz, n_heads, p_dim, n_state)


def _mlstm_chunked(q, k, v, i_pre, logf, c0, n0, m0):
    bsz, length, n_heads, d_head = q.shape
    qn = _chunk_len(length)
    nc = length // qn

    def vec_chunks(t):
        return t.reshape(bsz, nc, qn, n_heads, d_head).transpose(1, 0, 3, 2, 4)

    def gate_chunks(t):
        return t.reshape(bsz, nc, qn, n_heads).transpose(1, 0, 3, 2)

    causal = jnp.tril(jnp.ones((qn, qn), dtype=bool))

    def step(carry, inp):
        c, n, m = carry
        qc, kc, vc, ic, fc = inp
        bcum = jnp.cumsum(fc, axis=-1)
        dmat = jnp.where(causal, bcum[..., :, None] - bcum[..., None, :] + ic[..., None, :], -jnp.inf)
        inter = bcum + m[..., None]
        m_t = jnp.maximum(inter, jnp.max(dmat, axis=-1))
        w_intra = jnp.exp(dmat - m_t[..., None])
        w_inter = jnp.exp(inter - m_t)
        s = jnp.einsum('bhtd,bhsd->bhts', qc, kc) * w_intra
        num = jnp.einsum('bhts,bhse->bhte', s, vc) + w_inter[..., None] * jnp.einsum('bhtd,bhde->bhte', qc, c)
        den = jnp.sum(s, axis=-1) + w_inter * jnp.einsum('bhtd,bhd->bht', qc, n)
        h = num / jnp.maximum(jnp.abs(den), jnp.exp(-m_t))[..., None]
        m_end = m_t[..., -1]
        w_c = jnp.exp(bcum[..., -1] + m - m_end)
        w_s = jnp.exp(bcum[..., -1:] - bcum + ic - m_end[..., None])
        c_new = w_c[..., None, None] * c + jnp.einsum('bhs,bhsd,bhse->bhde', w_s, kc, vc)
        n_new = w_c[..., None] * n + jnp.einsum('bhs,bhsd->bhd', w_s, kc)
        return (c_new, n_new, m_end), h

    (c_last, n_last, m_last), hs = lax.scan(
        step, (c0, n0, m0),
        (vec_chunks(q), vec_chunks(k), vec_chunks(v), gate_chunks(i_pre), gate_chunks(logf)))
    hs = hs.transpose(1, 0, 3, 2, 4).reshape(bsz, length, n_heads, d_head)
    return hs, (c_last, n_last, m_last)


def _mixer(x, st_ssd, st_ssd_conv, st_c, st_n, st_m, w_in, ssd_conv_w, ssd_conv_b, ssd_dt_bias,
           ssd_a_log, ssd_d, ssd_norm_w, mlstm_gate_b, mlstm_norm_w, w_branch_a, w_branch_b, w_out):
    f32 = jnp.float32
    bsz, length, _ = x.shape
    proj = x @ w_in
    cuts = np.cumsum(IN_SPLITS)[:-1].tolist()
    z, xbc, dt, q, k, v, if_pre, o_pre, gates = jnp.split(proj, cuts, axis=-1)
    xbc, new_ssd_conv = _causal_dwconv(xbc, st_ssd_conv, ssd_conv_w, ssd_conv_b)
    xbc = jax.nn.silu(xbc.astype(f32))
    xs, bm, cm = jnp.split(xbc, [SSD_INNER, SSD_INNER + SSD_GROUPS * SSD_STATE], axis=-1)
    xh = xs.reshape(bsz, length, SSD_HEADS, SSD_HEAD_DIM)
    dt = jax.nn.softplus(dt.astype(f32) + ssd_dt_bias.astype(f32))
    a = -jnp.exp(ssd_a_log.astype(f32))
    y, new_ssd = _ssd_chunked(xh, dt, a,
                              bm.reshape(bsz, length, SSD_GROUPS, SSD_STATE),
                              cm.reshape(bsz, length, SSD_GROUPS, SSD_STATE),
                              st_ssd.astype(f32))
    y = y + ssd_d.astype(f32)[:, None] * xh
    y = _rms_norm(y.reshape(bsz, length, SSD_INNER) * jax.nn.silu(z.astype(f32)), ssd_norm_w)
    branch_a = y.astype(x.dtype) @ w_branch_a
    hd = (bsz, length, MLSTM_HEADS, MLSTM_HEAD_DIM)
    qh = q.astype(f32).reshape(hd)
    kh = k.astype(f32).reshape(hd) * (MLSTM_HEAD_DIM ** -0.5)
    vh = v.astype(f32).reshape(hd)
    i_pre, f_pre = jnp.split(if_pre.astype(f32) + mlstm_gate_b.astype(f32), 2, axis=-1)
    hm, (new_c, new_n, new_m) = _mlstm_chunked(qh, kh, vh, i_pre, jax.nn.log_sigmoid(f_pre),
                                               st_c.astype(f32), st_n.astype(f32), st_m.astype(f32))
    hm = _rms_norm(hm, mlstm_norm_w.reshape(MLSTM_HEADS, MLSTM_HEAD_DIM))
    hm = hm.reshape(bsz, length, MLSTM_INNER) * jax.nn.sigmoid(o_pre.astype(f32))
    branch_b = hm.astype(x.dtype) @ w_branch_b
    g_a, g_b = jnp.split(jax.nn.sigmoid(gates), 2, axis=-1)
    out = (g_a * branch_a + g_b * branch_b) @ w_out
    dty = x.dtype
    return out, (new_ssd.astype(dty), new_ssd_conv.astype(dty), new_c.astype(dty),
                 new_n.astype(dty), new_m.astype(dty))


def _conv_ffn(x, st_conv, w_up, conv_w, conv_b, w_down):
    up = x @ w_up
    up, new_conv = _causal_dwconv(up, st_conv, conv_w, conv_b)
    gate, val = jnp.split(up, 2, axis=-1)
    return (jax.nn.silu(gate) * val) @ w_down, new_conv.astype(x.dtype)


def _layer(x, states, params):
    st_ssd, st_ssd_conv, st_c, st_n, st_m, st_ffn = states
    (w_in, ssd_conv_w, ssd_conv_b, ssd_dt_bias, ssd_a_log, ssd_d, ssd_norm_w, mlstm_gate_b,
     mlstm_norm_w, w_branch_a, w_branch_b, w_out, ln1_g, ln1_b, ffn_w_up, ffn_conv_w,
     ffn_conv_b, ffn_w_down, ln2_g, ln2_b) = params
    mix, (n_ssd, n_ssd_conv, n_c, n_n, n_m) = _mixer(
        x, st_ssd, st_ssd_conv, st_c, st_n, st_m, w_in, ssd_conv_w, ssd_conv_b, ssd_dt_bias,
        ssd_a_log, ssd_d, ssd_norm_w, mlstm_gate_b, mlstm_norm_w, w_branch_a, w_branch_b, w_out)
    x = _layer_norm(ALPHA * x + mix, ln1_g, ln1_b)
    ffn, n_ffn = _conv_ffn(x, st_ffn, ffn_w_up, ffn_conv_w, ffn_conv_b, ffn_w_down)
    x = _layer_norm(ALPHA * x + ffn, ln2_g, ln2_b)
    return x, (n_ssd, n_ssd_conv, n_c, n_n, n_m, n_ffn)


def _trunk(x, states, params):
    new = []
    for layer in range(DEPTH):
        x, st = _layer(x, tuple(s[layer] for s in states), tuple(p[layer] for p in params))
        new.append(st)
    stacked = tuple(jnp.stack([st[i] for st in new]) for i in range(len(states)))
    return x, stacked


def setup_inputs(seed: int = 0) -> dict:
    key = jax.random.key(seed)
    ks = jax.random.split(key, 32)
    f = jnp.float32

    def nrm(k, shape, scale):
        return jax.random.normal(k, shape, f) * scale

    x_prompt = nrm(ks[0], (BATCH, SEQ, D_MODEL), 1.0)
    x_sample = nrm(ks[1], (DEC_BATCH, DEC_SEQ, D_MODEL), 1.0)
    state_ssd = nrm(ks[2], (DEPTH, DEC_BATCH, SSD_HEADS, SSD_HEAD_DIM, SSD_STATE), 0.1)
    state_ssd_conv = nrm(ks[3], (DEPTH, DEC_BATCH, SSD_CONV - 1, SSD_CONV_CH), 1.0)
    state_mlstm_c = nrm(ks[4], (DEPTH, DEC_BATCH, MLSTM_HEADS, MLSTM_HEAD_DIM, MLSTM_HEAD_DIM), 0.1)
    state_mlstm_n = nrm(ks[5], (DEPTH, DEC_BATCH, MLSTM_HEADS, MLSTM_HEAD_DIM), 0.1)
    state_mlstm_m = jax.random.uniform(ks[6], (DEPTH, DEC_BATCH, MLSTM_HEADS), f, 0.0, 1.0)
    state_ffn_conv = nrm(ks[7], (DEPTH, DEC_BATCH, FFN_CONV - 1, 2 * D_FF), 1.0)
    w_in = nrm(ks[8], (DEPTH, D_MODEL, N_IN), D_MODEL ** -0.5)
    ssd_conv_w = nrm(ks[9], (DEPTH, SSD_CONV, SSD_CONV_CH), SSD_CONV ** -0.5)
    ssd_conv_b = nrm(ks[10], (DEPTH, SSD_CONV_CH), 0.01)
    dt0 = jnp.exp(jax.random.uniform(ks[11], (DEPTH, SSD_HEADS), f, math.log(1e-3), math.log(1e-1)))
    ssd_dt_bias = dt0 + jnp.log(-jnp.expm1(-dt0))
    ssd_a_log = jnp.log(jax.random.uniform(ks[12], (DEPTH, SSD_HEADS), f, 1.0, 16.0))
    ssd_d = 1.0 + nrm(ks[13], (DEPTH, SSD_HEADS), 0.01)
    ssd_norm_w = 1.0 + nrm(ks[14], (DEPTH, SSD_INNER), 0.01)
    f_bias = jnp.linspace(3.0, 6.0, MLSTM_HEADS, dtype=f)
    mlstm_gate_b = jnp.concatenate([nrm(ks[15], (DEPTH, MLSTM_HEADS), 0.1),
                                    f_bias + nrm(ks[16], (DEPTH, MLSTM_HEADS), 0.01)], axis=-1)
    mlstm_norm_w = 1.0 + nrm(ks[17], (DEPTH, MLSTM_INNER), 0.01)
    w_branch_a = nrm(ks[18], (DEPTH, SSD_INNER, D_MODEL), SSD_INNER ** -0.5)
    w_branch_b = nrm(ks[19], (DEPTH, MLSTM_INNER, D_MODEL), MLSTM_INNER ** -0.5)
    w_out = nrm(ks[20], (DEPTH, D_MODEL, D_MODEL), BETA * D_MODEL ** -0.5)
    ln1_g = 1.0 + nrm(ks[21], (DEPTH, D_MODEL), 0.01)
    ln1_b = nrm(ks[22], (DEPTH, D_MODEL), 0.01)
    ffn_w_up = nrm(ks[23], (DEPTH, D_MODEL, 2 * D_FF), D_MODEL ** -0.5)
    ffn_conv_w = nrm(ks[24], (DEPTH, FFN_CONV, 2 * D_FF), FFN_CONV ** -0.5)
    ffn_conv_b = nrm(ks[25], (DEPTH, 2 * D_FF), 0.01)
    ffn_w_down = nrm(ks[26], (DEPTH, D_FF, D_MODEL), BETA * D_FF ** -0.5)
    ln2_g = 1.0 + nrm(ks[27], (DEPTH, D_MODEL), 0.01)
    ln2_b = nrm(ks[28], (DEPTH, D_MODEL), 0.01)
    return {"x_prompt": x_prompt, "x_sample": x_sample,
            "state_ssd": state_ssd, "state_ssd_conv": state_ssd_conv,
            "state_mlstm_c": state_mlstm_c, "state_mlstm_n": state_mlstm_n,
            "state_mlstm_m": state_mlstm_m, "state_ffn_conv": state_ffn_conv,
            "w_in": w_in, "ssd_conv_w": ssd_conv_w, "ssd_conv_b": ssd_conv_b,
            "ssd_dt_bias": ssd_dt_bias, "ssd_a_log": ssd_a_log, "ssd_d": ssd_d,
            "ssd_norm_w": ssd_norm_w, "mlstm_gate_b": mlstm_gate_b, "mlstm_norm_w": mlstm_norm_w,
            "w_branch_a": w_branch_a, "w_branch_b": w_branch_b, "w_out": w_out,
            "ln1_g": ln1_g, "ln1_b": ln1_b, "ffn_w_up": ffn_w_up, "ffn_conv_w": ffn_conv_w,
            "ffn_conv_b": ffn_conv_b, "ffn_w_down": ffn_w_down, "ln2_g": ln2_g, "ln2_b": ln2_b}


def reference(x_prompt, x_sample, state_ssd, state_ssd_conv, state_mlstm_c, state_mlstm_n,
              state_mlstm_m, state_ffn_conv, w_in, ssd_conv_w, ssd_conv_b, ssd_dt_bias, ssd_a_log,
              ssd_d, ssd_norm_w, mlstm_gate_b, mlstm_norm_w, w_branch_a, w_branch_b, w_out,
              ln1_g, ln1_b, ffn_w_up, ffn_conv_w, ffn_conv_b, ffn_w_down, ln2_g, ln2_b):
    params = (w_in, ssd_conv_w, ssd_conv_b, ssd_dt_bias, ssd_a_log, ssd_d, ssd_norm_w, mlstm_gate_b,
              mlstm_norm_w, w_branch_a, w_branch_b, w_out, ln1_g, ln1_b, ffn_w_up, ffn_conv_w,
              ffn_conv_b, ffn_w_down, ln2_g, ln2_b)
    bp = x_prompt.shape[0]
    dtp = x_prompt.dtype
    prompt_states = (jnp.zeros((DEPTH, bp) + state_ssd.shape[2:], dtp),
                     jnp.zeros((DEPTH, bp) + state_ssd_conv.shape[2:], dtp),
                     jnp.zeros((DEPTH, bp) + state_mlstm_c.shape[2:], dtp),
                     jnp.zeros((DEPTH, bp) + state_mlstm_n.shape[2:], dtp),
                     jnp.zeros((DEPTH, bp) + state_mlstm_m.shape[2:], dtp),
                     jnp.zeros((DEPTH, bp) + state_ffn_conv.shape[2:], dtp))
    y_prompt, (ssd_p, ssd_conv_p, c_p, n_p, m_p, ffn_p) = _trunk(x_prompt, prompt_states, params)
    sample_states = (state_ssd, state_ssd_conv, state_mlstm_c, state_mlstm_n, state_mlstm_m,
                     state_ffn_conv)
    y_sample, (ssd_s, ssd_conv_s, c_s, n_s, m_s, ffn_s) = _trunk(x_sample, sample_states, params)
    return (y_prompt, y_sample, ssd_p, ssd_s, ssd_conv_p, ssd_conv_s, c_p, c_s, n_p, n_s,
            m_p, m_s, ffn_p, ffn_s)
```

```python
import contextlib
import os
import numpy as np
import concourse.bass as bass
import concourse.mybir as mybir
from concourse.bass_utils import run_bass_kernel_spmd

F32 = mybir.dt.float32
BF16 = mybir.dt.bfloat16
AF = mybir.ActivationFunctionType
ALU = mybir.AluOpType
AX = mybir.AxisListType

NCORES = 8
D = 1024
SEQ = 2048
DEPTH = 2
NS = 16
SL = 4
NIN = 8472
DFF = 2816
ALPHA = (2 * DEPTH) ** 0.25
EPS = 1e-5
NEGV = -30000.0
ENGS = ("pe", "act", "dve", "pool", "sp")


class _Cut(Exception):
    pass


KT = os.environ.get('KT', '')
KL = int(os.environ.get('KL', '2'))
KCUT = os.environ.get('KCUT', '')


def cut(name):
    if KCUT == name:
        raise _Cut()


class Op:
    __slots__ = ("eng", "fn", "deps", "dma", "sem", "val", "waits", "snap", "needed")

    def __init__(self, eng, fn, dma):
        self.eng = eng
        self.fn = fn
        self.deps = set()
        self.dma = dma
        self.sem = None
        self.val = 0
        self.waits = []
        self.snap = None
        self.needed = False


class Buf:
    __slots__ = ("ap", "toks", "meta")

    def __init__(self, ap, toks, meta=None):
        self.ap = ap
        self.toks = toks
        self.meta = meta

    def __getitem__(self, k):
        return Buf(self.ap[k], self.toks, self.meta)

    def r(self, pat, **kw):
        return Buf(self.ap.rearrange(pat, **kw), self.toks, self.meta)

    def us(self, axis):
        return Buf(self.ap.unsqueeze(axis), self.toks, self.meta)

    def bc(self, shape):
        return Buf(self.ap.to_broadcast(list(shape)), self.toks, self.meta)

    def bitcast(self, dt):
        return Buf(self.ap.bitcast(dt), self.toks, self.meta)


def _toks(lst):
    out = []
    for b in lst:
        if b is None or isinstance(b, (int, float)):
            continue
        if isinstance(b, Buf):
            out.extend(b.toks)
        else:
            out.append(b)
    return out


def _ap(x):
    return x.ap if isinstance(x, Buf) else x


class Prog:
    def __init__(self, nc, n_dma_sems=12):
        self.nc = nc
        self.ops = {e: [] for e in ENGS}
        self.order = []
        self.last_write = {}
        self.readers = {}
        self.n_dma_sems = n_dma_sems
        self.checker = None

    def rec(self, eng, fn, reads=(), writes=(), dma=False):
        op = Op(eng, fn, dma)
        if self.checker is not None:
            for b in list(reads) + list(writes):
                if isinstance(b, Buf) and b.meta is not None:
                    self.checker(b)
        rt = _toks(reads)
        wt = _toks(writes)
        if any(isinstance(t, tuple) and t[0] == "ps" for t in rt):
            wt = wt + [t for t in rt if isinstance(t, tuple) and t[0] == "ps"]
            rt = [t for t in rt if not (isinstance(t, tuple) and t[0] == "ps")]
        for t in rt:
            w = self.last_write.get(t)
            if w is not None:
                op.deps.add(w)
        for t in wt:
            w = self.last_write.get(t)
            if w is not None:
                op.deps.add(w)
            for r in self.readers.get(t, ()):
                op.deps.add(r)
        for t in rt:
            lst = self.readers.setdefault(t, [])
            if not dma:
                for i, r in enumerate(lst):
                    if (not r.dma) and r.eng == eng:
                        lst[i] = op
                        break
                else:
                    lst.append(op)
            else:
                lst.append(op)
        for t in wt:
            self.last_write[t] = op
            self.readers[t] = []
        op.deps.discard(op)
        self.ops[eng].append(op)
        self.order.append(op)
        return op

    def emit(self):
        nc = self.nc
        for op in self.order:
            if op.eng == "pe" and not op.dma:
                op.deps = {d for d in op.deps if not (d.eng == "pe" and not d.dma)}
        dma_cnt = {}
        dma_last = {}
        for op in self.order:
            if op.dma:
                k = dma_cnt.get(op.eng, 0)
                dma_cnt[op.eng] = k + 1
                slot = (op.eng, k % self.n_dma_sems)
                prev = dma_last.get(slot)
                if prev is not None:
                    op.deps.add(prev)
                dma_last[slot] = op
                op.sem = slot
                op.val = 16 * (k // self.n_dma_sems + 1)
        for op in self.order:
            for d in op.deps:
                d.needed = True
        final_ops = list(dma_last.values())
        cnt = {e: 0 for e in ENGS}
        for op in self.order:
            if not op.dma and op.needed:
                cnt[op.eng] += 1
                op.sem = ("c", op.eng)
                op.val = cnt[op.eng]
        know = {e: {} for e in ENGS}
        for op in self.order:
            kn = know[op.eng]
            wd = {}
            for d in sorted(op.deps, key=lambda d: -d.val):
                if d.sem is None:
                    continue
                if kn.get(d.sem, 0) >= d.val:
                    continue
                if wd.get(d.sem, 0) < d.val:
                    wd[d.sem] = d.val
                for s, v in d.snap.items():
                    if kn.get(s, 0) < v:
                        kn[s] = v
            op.waits = list(wd.items())
            if op.sem is not None:
                sn = dict(kn)
                sn[op.sem] = op.val
                op.snap = sn
        sem_keys = []
        seen = set()
        for op in self.order:
            if op.sem is not None and op.sem not in seen:
                seen.add(op.sem)
                sem_keys.append(op.sem)
        self.stats = {e: len(self.ops[e]) for e in ENGS}
        self.stats["sems"] = len(sem_keys)
        self.stats["waits"] = sum(len(op.waits) for op in self.order)
        self.stats["cnt"] = dict(cnt)
        self.stats["dma"] = dict(dma_cnt)
        sems = {}
        with contextlib.ExitStack() as st:
            for i, k in enumerate(sem_keys):
                sems[k] = st.enter_context(nc.semaphore("s%d" % i))
            block = st.enter_context(nc.Block())
            fin = [(sems[d.sem], d.val) for d in final_ops]
            ops = self.ops

            def run(eng_name, eng):
                for op in ops[eng_name]:
                    for s, v in op.waits:
                        eng.wait_ge(sems[s], v)
                    ins = op.fn(eng)
                    if op.sem is not None:
                        ins.then_inc(sems[op.sem], 16 if op.dma else 1)
                if eng_name == "sp":
                    for s, v in fin:
                        eng.wait_ge(s, v)

            @block.tensor
            def _(e):
                run("pe", e)

            @block.scalar
            def _(e):
                run("act", e)

            @block.vector
            def _(e):
                run("dve", e)

            @block.gpsimd
            def _(e):
                run("pool", e)

            @block.sync
            def _(e):
                run("sp", e)


def _const_layout():
    lay = {}
    off = 0

    def add(name, n):
        nonlocal off
        lay[name] = (off, n)
        off += n

    add("ident", 128)
    add("ones", 128)
    for sfx, q in (("p", 128), ("s", 64)):
        add("L" + sfx, q)
        add("U" + sfx, q)
        add("SELLAST" + sfx, q)
        add("NEG" + sfx, 512)
        add("NEGT" + sfx, 4 * q)
    add("SELROWp", 1)
    add("SELROWs", NS)
    add("bms", NS)
    add("bmTs", 64)
    return lay, off


CL, NCONST = _const_layout()


def _make_consts():
    c = np.zeros((128, NCONST), np.float32)

    def put(name, arr):
        o, n = CL[name]
        a = np.asarray(arr, np.float32)
        c[: a.shape[0], o:o + a.shape[1]] = a

    put("ident", np.eye(128))
    put("ones", np.ones((128, 128)))
    for sfx, q, sl in (("p", 128, 128), ("s", 64, SL)):
        i = np.arange(q)
        seq = i // sl
        same = seq[:, None] == seq[None, :]
        put("L" + sfx, same & (i[:, None] <= i[None, :]))
        put("U" + sfx, same & (i[:, None] > i[None, :]))
        last = seq * sl + sl - 1
        put("SELLAST" + sfx, i[:, None] == last[None, :])
        neg = np.where(same & (i[None, :] >= i[:, None]), 0.0, NEGV)
        put("NEG" + sfx, np.tile(neg, (1, 512 // q)))
        negt = np.where(same & (i[None, :] <= i[:, None]), 0.0, NEGV)
        put("NEGT" + sfx, np.tile(negt, (1, 4)))
    sr = np.zeros((128, 1))
    sr[127, 0] = 1
    put("SELROWp", sr)
    i = np.arange(64)
    put("SELROWs", (i[:, None] == (np.arange(NS) * SL + SL - 1)[None, :]))
    bm = (i[:, None] // SL) == np.arange(NS)[None, :]
    put("bms", bm)
    put("bmTs", bm.T)
    return c


PR = {"dtb": (0, 16), "alog": (16, 16), "D": (32, 16), "gb": (48, 8), "snw": (64, 1024), "mnw": (1088, 1024),
      "l1g": (2112, 1024), "l1b": (3136, 1024), "l2g": (4160, 1024), "l2b": (5184, 1024)}
NPR = 6208

def _wblocks():
    blks = [("w_in", 0, 8, [(0, 1024)]), ("w_in", 0, 8, [(1024, 1024)]), ("w_in", 0, 8, [(2048, 272)]),
            ("w_in", 0, 8, [(2320, 1024)]), ("w_in", 0, 8, [(3344, 1024)]), ("w_in", 0, 8, [(4368, 1032)]),
            ("w_in", 0, 8, [(5400, 1024)]), ("w_a", 0, 8, [(0, 1024)]), ("w_in", 0, 8, [(6424, 1024)]),
            ("w_b", 0, 8, [(0, 1024)]), ("w_in", 0, 8, [(7448, 1024)]), ("w_out", 0, 8, [(0, 1024)])]
    for bb in range(6):
        n = 512 if bb < 5 else 256
        blks.append(("w_up", 0, 8, [(bb * 512, n), (DFF + bb * 512, n)]))
    for cb in range(2):
        for kh in range(2):
            blks.append(("w_down", kh * 11 * 128, 11, [(cb * 512, 512)]))
    return blks


WBLK = _wblocks()
NWB = len(WBLK)
WCAP = 8 * 1032
NWBUF = 2
USE_WSC = True


def build_program():
    nc = bass.Bass("TRN2", target_bir_lowering=False)
    dr = {}

    def din(name, shape):
        dr[name] = nc.dram_tensor(name, list(shape), F32, kind="ExternalInput").ap()
        return dr[name]

    def dout(name, shape):
        dr[name] = nc.dram_tensor(name, list(shape), F32, kind="ExternalOutput").ap()
        return dr[name]

    din("xp", (SEQ, D)); din("xs", (NS * SL, D))
    din("s_ssd", (DEPTH, NS, 1024, 64)); din("s_sconv", (DEPTH, NS * 3, 1280))
    din("s_c", (DEPTH, NS, 4, 256, 256)); din("s_n", (DEPTH, NS * 4, 256)); din("s_m", (DEPTH, NS, 4))
    din("s_fconv", (DEPTH, NS * 2, 2 * DFF))
    din("w_in", (DEPTH, D, NIN)); din("w_a", (DEPTH, D, D)); din("w_b", (DEPTH, D, D)); din("w_out", (DEPTH, D, D))
    din("w_up", (DEPTH, D, 2 * DFF)); din("w_down", (DEPTH, DFF, D))
    din("prow", (DEPTH, NPR)); din("cw_s", (DEPTH, 128, 40)); din("cb_s", (DEPTH, 128, 10))
    din("cw_f", (DEPTH, 128, 132)); din("cb_f", (DEPTH, 128, 44)); din("consts", (128, NCONST))
    dout("y_p", (SEQ, D)); dout("y_s", (NS * SL, D))
    dout("ssd_p", (DEPTH, 1024, 64)); dout("ssd_s", (DEPTH, NS, 1024, 64))
    dout("sconv_p", (DEPTH, 3, 1280)); dout("sconv_s", (DEPTH, NS, 3, 1280))
    dout("c_p", (DEPTH, 4, 256, 256)); dout("c_s", (DEPTH, NS, 4, 256, 256))
    dout("n_p", (DEPTH, 4, 256)); dout("n_s", (DEPTH, NS * 4, 256))
    dout("m_p", (DEPTH, 1, 4)); dout("m_s", (DEPTH, NS, 4))
    dout("fconv_p", (DEPTH, 2, 2 * DFF)); dout("fconv_s", (DEPTH, NS, 2, 2 * DFF))

    wsc = nc.dram_tensor("wsc", [DEPTH * NWB, 128, WCAP], BF16, kind="Internal").ap()
    st = contextlib.ExitStack()
    with st:
        P = Prog(nc)
        ARENA_BYTES = 211456
        arena_t = st.enter_context(nc.sbuf_tensor("arena", [128, ARENA_BYTES // 4], F32))
        PS_t = st.enter_context(nc.psum_tensor("PS", [128, 4096], F32))
        astate = {"off": 0}

        def alloc(free, dt=F32):
            if isinstance(free, int):
                free = (free,)
            n = int(np.prod(free))
            nb = n * (4 if dt == F32 else 2)
            nb = (nb + 127) // 128 * 128
            o = astate["off"]
            astate["off"] = o + nb
            astate["peak"] = max(astate.get("peak", 0), astate["off"])
            assert astate["off"] <= ARENA_BYTES, ("arena overflow", astate["off"])
            base = arena_t[:, o // 4:(o + nb) // 4]
            ap = base if dt == F32 else base.bitcast(BF16)
            ap = ap[:, 0:n]
            if len(free) == 2:
                ap = ap.rearrange("p (a b) -> p a b", a=free[0])
            elif len(free) == 3:
                ap = ap.rearrange("p (a b c) -> p a b c", a=free[0], b=free[1])
            elif len(free) == 4:
                ap = ap.rearrange("p (a b c d) -> p a b c d", a=free[0], b=free[1], c=free[2])
            toks = [("sb", k) for k in range(o // 128, (o + nb) // 128)]
            return Buf(ap, toks)

        def mark():
            return astate["off"]

        def release(m):
            astate["off"] = m

        ps_use = [0] * 8
        ps_clock = [0]
        ps_gen = [0] * 8

        ring_gen = {}

        def ps_check(b):
            for kind, idx, g in b.meta:
                if kind == "ps":
                    assert ps_gen[idx] == g, ("stale PSUM buffer used", idx, g, ps_gen[idx])
                else:
                    assert ring_gen.get(idx, 0) == g, ("stale ring buffer used", idx, g, ring_gen.get(idx, 0))

        def make_ring(nbytes):
            base = (astate["off"] + 511) // 512 * 512
            nbytes = nbytes // 512 * 512
            astate["off"] = base + nbytes
            astate["peak"] = max(astate.get("peak", 0), astate["off"])
            assert astate["off"] <= ARENA_BYTES, ("arena overflow (ring)", astate["off"])
            assert base % 512 == 0 or True
            st_ = {"o": 0}

            def ralloc(free, dt=F32):
                if isinstance(free, int):
                    free = (free,)
                n = int(np.prod(free))
                nb = n * (4 if dt == F32 else 2)
                nb = (nb + 511) // 512 * 512
                assert nb <= nbytes
                if st_["o"] + nb > nbytes:
                    st_["o"] = 0
                o = base + st_["o"]
                st_["o"] += nb
                o4 = (o + 3) // 4 * 4
                bs_ = arena_t[:, o4 // 4:(o4 + nb) // 4 if o4 + nb <= ARENA_BYTES else ARENA_BYTES // 4]
                ap = bs_ if dt == F32 else bs_.bitcast(BF16)
                ap = ap[:, 0:n]
                if len(free) == 2:
                    ap = ap.rearrange("p (a b) -> p a b", a=free[0])
                elif len(free) == 3:
                    ap = ap.rearrange("p (a b c) -> p a b c", a=free[0], b=free[1])
                slots = list(range(o // 512, (o + nb) // 512))
                meta = []
                for k in slots:
                    ring_gen[k] = ring_gen.get(k, 0) + 1
                    meta.append(("ring", k, ring_gen[k]))
                return Buf(ap, [("sb", k) for k in range(o // 128, (o + nb) // 128)], tuple(meta))

            return ralloc

        P.checker = ps_check

        ps_allowed = [list(range(8))]

        def run_streams(streams):
            alive = list(streams)
            while alive:
                for item in list(alive):
                    g, banks = item
                    ps_allowed[0] = banks
                    try:
                        next(g)
                    except StopIteration:
                        alive.remove(item)
            ps_allowed[0] = list(range(8))

        def psum(nb=1):
            best, bs = None, None
            for s in range(0, 8, nb):
                if any(b not in ps_allowed[0] for b in range(s, s + nb)):
                    continue
                sc = max(ps_use[s:s + nb])
                if best is None or sc < best:
                    best, bs = sc, s
            ps_clock[0] += 1
            for b in range(bs, bs + nb):
                ps_use[b] = ps_clock[0]
                ps_gen[b] += 1
            return Buf(PS_t[:, bs * 512:(bs + nb) * 512], [("ps", b) for b in range(bs, bs + nb)],
                       tuple(("ps", b, ps_gen[b]) for b in range(bs, bs + nb)))

        def MM(out, lhsT, rhs, start=True, stop=True):
            P.rec("pe", lambda e: e.matmul(out.ap, lhsT.ap, rhs.ap, start=start, stop=stop),
                  reads=[lhsT, rhs], writes=[out])

        def TR(out, in_, ident):
            P.rec("pe", lambda e: e.transpose(out.ap, in_.ap, ident.ap), reads=[in_, ident], writes=[out])

        def ACT(out, in_, func, scale=1.0, bias=None, accum=None):
            kw = {}
            if bias is not None:
                kw["bias"] = _ap(bias)
            if accum is not None:
                kw["accum_out"] = accum.ap
            sc = _ap(scale)
            P.rec("act", lambda e: e.activation(out=out.ap, in_=in_.ap, func=func, scale=sc, **kw),
                  reads=[in_, scale, bias], writes=[out, accum])

        def TT(eng, out, in0, in1, op):
            P.rec(eng, lambda e: e.tensor_tensor(out=out.ap, in0=in0.ap, in1=in1.ap, op=op),
                  reads=[in0, in1], writes=[out])

        def TS(eng, out, in0, s1, s2=None, op0=ALU.mult, op1=None):
            a1, a2 = _ap(s1), _ap(s2)
            if op1 is None:
                P.rec(eng, lambda e: e.tensor_scalar(out=out.ap, in0=in0.ap, scalar1=a1, scalar2=None, op0=op0),
                      reads=[in0, s1], writes=[out])
            else:
                P.rec(eng, lambda e: e.tensor_scalar(out=out.ap, in0=in0.ap, scalar1=a1, scalar2=a2, op0=op0, op1=op1),
                      reads=[in0, s1, s2], writes=[out])

        def STT(out, in0, scalar, in1, op0, op1):
            sc = _ap(scalar)
            P.rec("dve", lambda e: e.scalar_tensor_tensor(out=out.ap, in0=in0.ap, scalar=sc, in1=in1.ap, op0=op0, op1=op1),
                  reads=[in0, scalar, in1], writes=[out])

        def CP(eng, out, in_):
            if eng == "act":
                ACT(out, in_, AF.Copy)
            else:
                P.rec(eng, lambda e: e.tensor_copy(out=out.ap, in_=in_.ap), reads=[in_], writes=[out])

        def MEMSET(eng, out, v):
            P.rec(eng, lambda e: e.memset(out.ap, v), writes=[out])

        def DMA(q, out, in_, reads=(), writes=()):
            oa, ia = _ap(out), _ap(in_)
            P.rec(q, lambda e: e.dma_start(out=oa, in_=ia), reads=list(reads) + ([in_] if isinstance(in_, Buf) else []),
                  writes=list(writes) + ([out] if isinstance(out, Buf) else []), dma=True)

        def RED(out, in_, op):
            P.rec("dve", lambda e: e.tensor_reduce(out=out.ap, in_=in_.ap, axis=AX.X, op=op), reads=[in_], writes=[out])

        cp_rr = [0]

        def CPX(out, in_):
            cp_rr[0] ^= 1
            CP("act" if cp_rr[0] else "dve", out, in_)

        cf = alloc(NCONST, F32)
        cb_ = alloc(NCONST, BF16)
        DMA("sp", cf, dr["consts"])
        DMA("pool", cb_, dr["consts"])

        def CF(name, rows=128, sub=None):
            o, n = CL[name]
            if sub is not None:
                o, n = o + sub[0], sub[1]
            return cf[0:rows, o:o + n]

        def CB(name, rows=128, sub=None):
            o, n = CL[name]
            if sub is not None:
                o, n = o + sub[0], sub[1]
            return cb_[0:rows, o:o + n]

        mhalf = alloc(1, F32)
        MEMSET("pool", mhalf, -0.5)
        wbufs = [alloc(WCAP, BF16) for _ in range(NWBUF)]
        x_tok = alloc((4, D), F32)
        prm_small = alloc(64, F32)
        cws = alloc((10, 4), F32); cbs = alloc(10, F32); cwf = alloc((44, 3), F32); cbf = alloc(44, F32)
        a_bc = alloc(16, F32)
        pst = []
        for l in range(DEPTH):
            s = {"hT_f": alloc(512, F32), "hT_b": alloc(512, BF16), "C_f": alloc((2, 4, 256), F32),
                 "C_b": alloc((2, 4, 256), BF16), "nT_f": alloc((2, 4, 1), F32), "nT_b": alloc((2, 4, 1), BF16),
                 "m_st": alloc(4, F32), "tl_s": alloc((10, 3), F32), "tl_f": alloc((44, 2), F32)}
            for k in ("hT_f", "hT_b", "C_f", "C_b", "nT_f", "nT_b", "m_st", "tl_s", "tl_f"):
                MEMSET("pool", s[k], 0.0)
            pst.append(s)

        wstate = {"issued": 0, "gs": None, "lst": None}

        def wbuf_of(g):
            if wstate["gs"] is not None and g >= wstate["gs"]:
                lst = wstate["lst"]
                return lst[(g - wstate["gs"]) % len(lst)]
            return wbufs[g % NWBUF]

        def wdepth(g):
            if wstate["gs"] is not None and g >= wstate["gs"]:
                return len(wstate["lst"]) - 1
            return 1

        def wissue(g):
            l = (g // NWB) % DEPTH
            name, row0, nk, segs = WBLK[g % NWB]
            wb = wbuf_of(g)
            tot = sum(n for _, n in segs)
            view = wb[:, 0:nk * tot].r("p (k n) -> p k n", k=nk)
            lk = g % (DEPTH * NWB)
            if g >= DEPTH * NWB and USE_WSC:
                DMA("pool", wb[:, 0:nk * tot], wsc[lk, :, 0:nk * tot], reads=[("dram", lk)])
                return
            src = dr[name][l]
            o = 0
            for c0, n in segs:
                sap = src[row0:row0 + nk * 128, c0:c0 + n].rearrange("(k p) n -> p k n", p=128)
                DMA("pool", view[:, :, o:o + n], sap)
                o += n
            if USE_WSC:
                DMA("sp", wsc[lk, :, 0:nk * tot], wb[:, 0:nk * tot], writes=[("dram", lk)])

        def wget(g, total, prefetch=True):
            while wstate["issued"] <= min(g + (wdepth(g) if prefetch else 0), total - 1):
                nx = wstate["issued"]
                if nx >= SAMPLE_G0 and wstate["gs"] is None and nx > g:
                    break
                wissue(nx)
                wstate["issued"] += 1
            name, row0, nk, segs = WBLK[g % NWB]
            tot = sum(n for _, n in segs)
            return wbuf_of(g)[:, 0:nk * tot].r("p (k n) -> p k n", k=nk)

        tiles = [("p", i) for i in range(4)] + [("s", 0)]
        if KT:
            tiles = [(t[0], int(t[1:] or 0)) for t in KT.split(",")]
        total_blocks = len(tiles) * DEPTH * NWB
        SAMPLE_G0 = 10 ** 9
        for _i, (_k, _t) in enumerate(tiles):
            if _k == "s":
                SAMPLE_G0 = _i * DEPTH * NWB
        gctr = [0]

        def nextw(prefetch=True):
            g = gctr[0]
            gctr[0] += 1
            return wget(g, total_blocks, prefetch)

        identf = CF("ident")
        identb = CB("ident")
        onesf = CF("ones")
        onesb = CB("ones")

        def rsqrt_(out, in_, scale, rows, n):
            TS("dve", out, in_, scale, EPS, ALU.mult, ALU.add)
            TT("pool", out, out, mhalf[0:rows, 0:1].bc([rows, n]), ALU.pow)

        for kind, ti in tiles:
            prompt = kind == "p"
            Q = 128 if prompt else 64
            NCH = 4 if prompt else 1
            TTK = Q * NCH
            nseq = 1 if prompt else NS
            Lq = TTK if prompt else SL
            sfx = "p" if prompt else "s"
            Lm = CF("L" + sfx, Q); Um = CF("U" + sfx, Q); SELLAST = CF("SELLAST" + sfx, Q)
            NEGb = CB("NEG" + sfx, Q); NEGTb = CB("NEGT" + sfx, Q)
            SELROW = CF("SELROW" + sfx, Q)
            if prompt:
                bm = onesf[0:Q, 0:1]; bmb = onesb[0:Q, 0:1]; bmT = onesf[0:1, 0:Q]
            else:
                bm = CF("bms", Q); bmb = CB("bms", Q); bmT = CF("bmTs", NS)
            idq_f = identf[0:Q, 0:Q]; idq_b = identb[0:Q, 0:Q]
            last_tile = prompt and ti == 3
            want_rows = last_tile or not prompt

            if not prompt and tiles.index((kind, ti)) * DEPTH * NWB == SAMPLE_G0:
                extra = []
                wstate["gs"] = max(SAMPLE_G0, wstate["issued"])
                wstate["lst"] = extra + [wbufs[(wstate["gs"] + k) % NWBUF] for k in range(NWBUF)]
            if prompt:
                DMA("sp", x_tok, dr["xp"][ti * 512:(ti + 1) * 512, :].rearrange("(c p) d -> p c d", p=128))
            else:
                DMA("sp", x_tok[0:Q, 0, :], dr["xs"])

            def layer_body(l):
                S = pst[l]
                m0 = mark()
                DMA("sp", prm_small, dr["prow"][l, 0:64].partition_broadcast(128))
                DMA("sp", cws, dr["cw_s"][l].rearrange("p (a b) -> p a b", a=10))
                DMA("sp", cbs, dr["cb_s"][l])
                DMA("sp", cwf, dr["cw_f"][l].rearrange("p (a b) -> p a b", a=44))
                DMA("sp", cbf, dr["cb_f"][l])
                ACT(a_bc, prm_small[:, 16:32], AF.Exp)
                TS("dve", a_bc, a_bc, -1.0)
                dtb = prm_small[:, 0:16]; Dp = prm_small[:, 32:48]; gbp = prm_small[:, 48:56]

                def prow(name, rows):
                    o, n = PR[name]
                    b = alloc(n, F32)
                    DMA("sp", b, dr["prow"][l, o:o + n].partition_broadcast(128))
                    return b[0:rows, :]

                xT = alloc((8, TTK), BF16)
                yT = alloc((8, TTK), BF16)
                qT = alloc((8, TTK), BF16)
                kT = alloc((8, TTK), BF16)
                k_tok = [alloc(D, BF16) for _ in range(NCH)]
                gi = alloc((NCH, 4), F32)
                logf = alloc((NCH, 4), F32)
                mS = mark()

                def to_featmajor(src_tok, dstT):
                    for c in range(NCH):
                        for half in range(2):
                            ps = psum(1)
                            for j in range(4):
                                kc = half * 4 + j
                                TR(ps[:, j * Q:(j + 1) * Q], src_tok[0:Q, c, kc * 128:(kc + 1) * 128], idq_f)
                            CPX(dstT[:, half * 4:half * 4 + 4, c * Q:(c + 1) * Q],
                                ps[:, 0:4 * Q].r("p (j q) -> p j q", j=4))

                to_featmajor(x_tok, xT)

                def proj_tok(src, c, wb, col0, ncols, evac, rows=None):
                    r0, r1 = (c * Q, (c + 1) * Q) if rows is None else rows
                    M = r1 - r0
                    for cb0 in range(0, ncols, 512):
                        n = min(512, ncols - cb0)
                        ps = psum(1)
                        for kc in range(8):
                            MM(ps[0:M, 0:n], src[:, kc, r0:r1], wb[:, kc, col0 + cb0:col0 + cb0 + n], kc == 0, kc == 7)
                        evac(ps[0:M, 0:n], cb0, n)

                def proj_feat(src, wb, col0, evac, m=128):
                    ps = psum(1)
                    for kc in range(8):
                        MM(ps[0:m, 0:TTK], wb[:, kc, col0:col0 + m], src[:, kc, 0:TTK], kc == 0, kc == 7)
                    evac(ps[0:m, 0:TTK])

                if prompt:
                    row_rng = (TTK - 32, TTK)
                else:
                    row_rng = (0, 64)
                RM = row_rng[1] - row_rng[0]

                def emit_rows(src, wb, col0, ncols, dst_p, dst_s, nrow, dcol0):
                    def ev(ps, cb0, n):
                        t = cur_ring[0](512, F32)
                        CPX(t[0:RM, 0:n], ps)
                        if prompt:
                            DMA("sp", dst_p[l, :, dcol0 + cb0:dcol0 + cb0 + n], t[RM - nrow:RM, 0:n])
                        else:
                            for tt in range(nrow):
                                tok = SL - nrow + tt
                                DMA("sp", dst_s[l, :, tt, dcol0 + cb0:dcol0 + cb0 + n], t[tok:64:SL, 0:n])
                    proj_tok(src, 0, wb, col0, ncols, ev, rows=row_rng)

                cut('P0')
                z_tok = [alloc(D, BF16) for _ in range(NCH)]
                xsT = alloc((8, TTK), BF16)
                BT = alloc(TTK, BF16)
                CT = alloc(TTK, BF16)
                dt_ = alloc((NCH, 16), F32)
                dta = alloc((NCH, 16), F32)
                wb = nextw()
                for c in range(NCH):
                    proj_tok(xT, c, wb, 0, 1024,
                             lambda ps, cb0, n, c=c: ACT(z_tok[c][0:Q, cb0:cb0 + n], ps, AF.Silu))
                histS = None
                if not prompt:
                    histS = alloc((10, NS * 3), F32)
                    mm = mark()
                    nat = alloc(1280, F32)
                    DMA("sp", nat[0:NS * 3, :], dr["s_sconv"][l])
                    for cc in range(10):
                        ps = psum(1)
                        TR(ps[:, 0:NS * 3], nat[0:NS * 3, cc * 128:(cc + 1) * 128], identf[0:NS * 3, 0:NS * 3])
                        CPX(histS[:, cc, :], ps[:, 0:NS * 3])
                    release(mm)

                mR = mark()
                cur_ring = [make_ring(20 * 1024)]

                def conv_from_psum(ps, cc, cw, cb, tails, W):
                    K = W - 1
                    acc = cur_ring[0]((1, Lq), F32)
                    xh = cur_ring[0](2 * K, F32)
                    a2 = acc[:, 0, :]
                    ACT(a2[:, K:Lq], ps[:, K:Lq], AF.Identity, scale=cw[:, cc, K:K + 1], bias=cb[:, cc:cc + 1])
                    CP("act", xh[:, 0:K], tails[:, cc, :])
                    CP("act", xh[:, K:2 * K], ps[:, 0:K])
                    CP("act", tails[:, cc, :], ps[:, Lq - K:Lq])
                    for j in range(K - 1, -1, -1):
                        STT(a2[:, K:Lq], ps[:, j:Lq - K + j], cw[:, cc, j:j + 1], a2[:, K:Lq], ALU.mult, ALU.add)
                    TS("dve", a2[:, 0:K], xh[:, K:2 * K], cw[:, cc, K:K + 1], cb[:, cc:cc + 1], ALU.mult, ALU.add)
                    for j in range(K - 1, -1, -1):
                        STT(a2[:, 0:K], xh[:, j:j + K], cw[:, cc, j:j + 1], a2[:, 0:K], ALU.mult, ALU.add)
                    return acc

                def conv_chunk(ps, cc):
                    if prompt:
                        acc = conv_from_psum(ps, cc, cws, cbs, S["tl_s"], 4)
                        dst = xsT[:, cc, :] if cc < 8 else (BT if cc == 8 else CT)
                        ACT(dst.r("p (b t) -> p b t", b=nseq), acc, AF.Silu)
                        return
                    xp = cur_ring[0]((nseq, 3 + Lq), F32)
                    acc = cur_ring[0]((nseq, Lq), F32)
                    if prompt:
                        CP("act", xp[:, 0, 0:3], S["tl_s"][:, cc, :])
                    else:
                        CP("act", xp[:, :, 0:3], histS[:, cc, :].r("p (b j) -> p b j", j=3))
                    CP("act", xp[:, :, 3:3 + Lq], ps.r("p (b t) -> p b t", b=nseq))
                    ACT(acc, ps.r("p (b t) -> p b t", b=nseq), AF.Identity, scale=cws[:, cc, 3:4], bias=cbs[:, cc:cc + 1])
                    for j in (2, 1, 0):
                        STT(acc, xp[:, :, j:j + Lq], cws[:, cc, j:j + 1], acc, ALU.mult, ALU.add)
                    if cc < 8:
                        dst = xsT[:, cc, :]
                    elif cc == 8:
                        dst = BT
                    else:
                        dst = CT
                    ACT(dst.r("p (b t) -> p b t", b=nseq), acc, AF.Silu)
                    if prompt:
                        CP("act", S["tl_s"][:, cc, :], xp[:, 0, Lq:Lq + 3])

                wb = nextw()
                for cc in range(8):
                    proj_feat(xT, wb, cc * 128, lambda ps, cc=cc: conv_chunk(ps, cc))
                if want_rows:
                    emit_rows(xT, wb, 0, 1024, dr["sconv_p"], dr["sconv_s"], 3, 0)
                wb = nextw()
                for cc in (8, 9):
                    proj_feat(xT, wb, (cc - 8) * 128, lambda ps, cc=cc: conv_chunk(ps, cc))
                if want_rows:
                    emit_rows(xT, wb, 0, 256, dr["sconv_p"], dr["sconv_s"], 3, 1024)
                psd = psum(1)
                for c in range(NCH):
                    for kc in range(8):
                        MM(psd[0:Q, c * 16:(c + 1) * 16], xT[:, kc, c * Q:(c + 1) * Q], wb[:, kc, 256:272], kc == 0, kc == 7)
                mm = mark()
                tx = alloc((NCH, 16), F32); ta = alloc((NCH, 16), F32)
                TT("dve", tx[0:Q], psd[0:Q, 0:NCH * 16].r("p (c h) -> p c h", c=NCH), dtb[0:Q].us(1).bc([Q, NCH, 16]), ALU.add)
                STT(ta[0:Q], tx[0:Q], -1.0, tx[0:Q], ALU.mult, ALU.max)
                ACT(ta[0:Q], ta[0:Q], AF.Exp, scale=-1.0)
                ACT(ta[0:Q], ta[0:Q], AF.Ln, bias=1.0)
                STT(dt_[0:Q], tx[0:Q], 0.0, ta[0:Q], ALU.max, ALU.add)
                TT("dve", dta[0:Q], dt_[0:Q], a_bc[0:Q].us(1).bc([Q, NCH, 16]), ALU.mult)
                release(mm)

                release(mR)
                cut('S1')
                snw = prow("snw", Q)
                gx = alloc((NCH, 8), F32); gta = alloc((NCH, 4), F32)
                s2b = []
                for _p in range(2 if NCH > 1 else 1):
                    s2b.append({"xs_tok": alloc(D, BF16), "xdt": alloc(D, BF16), "B_tok": alloc(128, BF16),
                                "GT": alloc((16, Q), BF16), "eac": alloc(32, F32), "cdb": alloc((nseq, 8), F32),
                                "xdtw": alloc(D, BF16)})
                R = alloc((16, Q), F32); expT = alloc((16, Q), BF16); CBs = alloc((2, Q), BF16)
                Xs = None if prompt else alloc((2, NS, 8), F32)
                yo = alloc(D, F32); xsD = alloc(D, F32); ss = alloc(2, F32); y_n = alloc(D, BF16)
                if not prompt:
                    sq = [{"hT_f": alloc(512, F32), "hT_b": alloc(512, BF16),
                           "xw": alloc(D, BF16), "hn": alloc(512, F32)} for _ in range(2)]
                    nat_l = [alloc((8, 64), F32) for _ in range(6)]
                    natn_l = [alloc((8, 64), F32) for _ in range(6)]

                def s2_front(c, B):
                    cols = slice(c * Q, (c + 1) * Q)
                    psx = psum(1).bitcast(BF16)
                    for kc in range(8):
                        TR(psx[0:Q, kc * 128:(kc + 1) * 128], xsT[:, kc, cols], identb)
                    CP("act", B["xs_tok"][0:Q], psx[0:Q, :])
                    TT("dve", B["xdt"][0:Q].r("p (h e) -> p h e", h=16), psx[0:Q, :].r("p (h e) -> p h e", h=16),
                       dt_[0:Q, c, :].us(2).bc([Q, 16, 64]), ALU.mult)
                    psb = psum(1).bitcast(BF16)
                    TR(psb[0:Q, 0:128], BT[:, cols], identb)
                    CP("act", B["B_tok"][0:Q], psb[0:Q, 0:128])
                    TT("pool", R[0:Q], Lm.us(1).bc([Q, 16, Q]), dta[0:Q, c, :].us(2).bc([Q, 16, Q]), ALU.mult)
                    yield
                    nbk = 16 * Q // 512
                    SEG = psum(nbk)
                    Rf = R[0:Q].r("p h l -> p (h l)")
                    for bk in range(nbk):
                        MM(SEG[0:Q, bk * 512:(bk + 1) * 512], Um, Rf[:, bk * 512:(bk + 1) * 512], True, False)
                        MM(SEG[0:Q, bk * 512:(bk + 1) * 512], idq_b, NEGb, False, True)
                    ACT(expT[0:Q].r("p h l -> p (h l)"), SEG[0:Q, 0:16 * Q], AF.Exp)
                    psc = psum(2)
                    for g in range(2):
                        MM(psc[0:Q, g * 512:g * 512 + Q], BT[g * 64:(g + 1) * 64, cols], CT[g * 64:(g + 1) * 64, cols])
                    pt = psum(1)
                    MM(pt[0:Q, 0:16], Lm, dta[0:Q, c, :])
                    MM(pt[0:Q, 16:32], Um, dta[0:Q, c, :])
                    pcd = psum(1)
                    if prompt:
                        for g in range(2):
                            MM(pcd[g * 64:(g + 1) * 64, 0:8], onesf[0:Q, 0:64], dta[0:Q, c, g * 8:(g + 1) * 8])
                    else:
                        for g in range(2):
                            TT("dve", Xs[0:Q, g], bm.us(2).bc([Q, NS, 8]),
                               dta[0:Q, c, g * 8:(g + 1) * 8].us(1).bc([Q, NS, 8]), ALU.mult)
                            MM(pcd[g * 64:(g + 1) * 64, 0:NS * 8], onesf[0:Q, 0:64], Xs[0:Q, g].r("p b h -> p (b h)"))
                    yield
                    CP("dve", CBs[0:Q], psc[0:Q, :].r("p (g x) -> p g x", g=2)[:, :, 0:Q])
                    ACT(B["eac"][0:Q], pt[0:Q, 0:32], AF.Exp)
                    ACT(B["cdb"].r("p b h -> p (b h)"), pcd[:, 0:nseq * 8], AF.Exp)
                    TT("dve", B["GT"][0:Q].r("p (g k) l -> p g k l", g=2), expT[0:Q].r("p (g k) l -> p g k l", g=2),
                       CBs[0:Q].us(2).bc([Q, 2, 8, Q]), ALU.mult)
                    TT("dve", B["xdtw"][0:Q].r("p (h e) -> p h e", h=16), B["xdt"][0:Q].r("p (h e) -> p h e", h=16),
                       B["eac"][0:Q, 16:32].us(2).bc([Q, 16, 64]), ALU.mult)
                    yield

                def s2_back(c, B):
                    cols = slice(c * Q, (c + 1) * Q)
                    eac, cdb, xdtw, xdt, GT, xs_tok, B_tok = (B["eac"], B["cdb"], B["xdtw"], B["xdt"], B["GT"],
                                                             B["xs_tok"], B["B_tok"])
                    def load_nat(b2):
                        nv = nat_l[b2 % 6].r("p (j g) n -> p j g n", g=2)
                        for g in range(2):
                            DMA("sp", nv[:, :, g, :],
                                dr["s_ssd"][l, b2][g * 512:(g + 1) * 512, :].rearrange("(j q) n -> q j n", q=128))

                    if not prompt:
                        for b2 in range(3):
                            load_nat(b2)
                    for b in range(nseq):
                        if prompt:
                            hT_f, hT_b = S["hT_f"], S["hT_b"]
                        else:
                            if b + 3 < nseq:
                                load_nat(b + 3)
                            sb_ = sq[b % 2]
                            nat, hT_f, hT_b = nat_l[b % 6], sb_["hT_f"], sb_["hT_b"]
                            natv = nat.r("p (j g) n -> p j g n", g=2)
                            pn = psum(1)
                            for jj in range(4):
                                TR(pn[:, jj * 128:(jj + 1) * 128], natv[:, jj].r("p g n -> p (g n)"), identf)
                            CP("dve", hT_f, pn)
                            CP("act", hT_b, pn)
                        YO = psum(2)
                        for g in range(2):
                            MM(YO[0:Q, g * 512:(g + 1) * 512], CT[g * 64:(g + 1) * 64, cols], hT_b[g * 64:(g + 1) * 64, :])
                        if prompt:
                            TT("dve", yo[0:Q].r("p (h e) -> p h e", h=16), YO[0:Q, :].r("p (h e) -> p h e", h=16),
                               eac[0:Q, 0:16].us(2).bc([Q, 16, 64]), ALU.mult)
                            xw = xdtw
                        else:
                            if b == 0:
                                TS("dve", yo[0:Q], YO[0:Q, :], bm[:, b:b + 1])
                            else:
                                STT(yo[0:Q], YO[0:Q, :], bm[:, b:b + 1], yo[0:Q], ALU.mult, ALU.add)
                            xw = sb_["xw"]
                            TS("dve", xw[0:Q], xdtw[0:Q], bm[:, b:b + 1])
                        HL = psum(1)
                        for g in range(2):
                            MM(HL[g * 64:(g + 1) * 64, :], B_tok[0:Q, g * 64:(g + 1) * 64], xw[0:Q, g * 512:(g + 1) * 512])
                        hn = hT_f if prompt else sb_["hn"]
                        TT("dve", hn.r("p (h e) -> p h e", h=8), hT_f.r("p (h e) -> p h e", h=8),
                           cdb[:, b, :].us(2).bc([128, 8, 64]), ALU.mult)
                        TT("dve", hn, hn, HL, ALU.add)
                        if prompt:
                            CP("act", hT_b, hn)
                        else:
                            po = psum(1)
                            for jj in range(4):
                                TR(po[:, jj * 128:(jj + 1) * 128], hn[:, jj * 128:(jj + 1) * 128], identf)
                            natn = natn_l[b % 6]
                            CP("act", natn.r("p k n -> p (k n)"), po)
                            natnv = natn.r("p (j g) n -> p j g n", g=2)
                            for g in range(2):
                                DMA("sp", dr["ssd_s"][l, b][g * 512:(g + 1) * 512, :].rearrange("(j q) n -> q j n", q=128),
                                    natnv[:, :, g, :])
                        yield
                    if not prompt:
                        TT("dve", yo[0:Q].r("p (h e) -> p h e", h=16), yo[0:Q].r("p (h e) -> p h e", h=16),
                           eac[0:Q, 0:16].us(2).bc([Q, 16, 64]), ALU.mult)
                    TT("pool", xsD[0:Q].r("p (h e) -> p h e", h=16), xs_tok[0:Q].r("p (h e) -> p h e", h=16),
                       Dp[0:Q].us(2).bc([Q, 16, 64]), ALU.mult)
                    Y = psum(2)
                    for h in range(16):
                        MM(Y[0:Q, h * 64:(h + 1) * 64], GT[0:Q, h, :], xdt[0:Q, h * 64:(h + 1) * 64])
                    TT("dve", yo[0:Q], yo[0:Q], Y[0:Q, :], ALU.add)
                    TT("dve", yo[0:Q], yo[0:Q], xsD[0:Q], ALU.add)
                    TT("dve", yo[0:Q], yo[0:Q], z_tok[c][0:Q], ALU.mult)
                    ACT(xsD[0:Q], yo[0:Q], AF.Square, accum=ss[0:Q, 0:1])
                    rsqrt_(ss[0:Q, 1:2], ss[0:Q, 0:1], 1.0 / 1024, Q, 1)
                    STT(y_n[0:Q], yo[0:Q], ss[0:Q, 1:2], snw, ALU.mult, ALU.mult)
                    yield
                    pyt = psum(1).bitcast(BF16)
                    for kc in range(8):
                        TR(pyt[:, kc * Q:(kc + 1) * Q], y_n[0:Q, kc * 128:(kc + 1) * 128], idq_b)
                    CP("act", yT[:, :, cols], pyt[:, 0:8 * Q].r("p (k q) -> p k q", k=8))
                    yield

                def s2_stream():
                    yield from s2_front(0, s2b[0])
                    for c in range(NCH):
                        if c + 1 < NCH:
                            yield from s2_front(c + 1, s2b[(c + 1) % 2])
                        yield from s2_back(c, s2b[c % 2])
                    if last_tile:
                        po = psum(2)
                        for blk in range(8):
                            g, jj = blk // 4, blk % 4
                            TR(po[:, g * 512 + jj * 64:g * 512 + (jj + 1) * 64], S["hT_f"][g * 64:(g + 1) * 64, jj * 128:(jj + 1) * 128],
                               identf[g * 64:(g + 1) * 64, g * 64:(g + 1) * 64])
                        natn = yo.r("p (k n) -> p k n", k=16)[:, 0:8, :]
                        CP("act", natn.r("p (g j) n -> p g (j n)", g=2), po.r("p (g x) -> p g x", g=2)[:, :, 0:256])
                        DMA("sp", dr["ssd_p"][l].rearrange("(k q) n -> q k n", q=128), natn)

                def m1_stream():
                    wb = nextw()
                    for cc in range(8):
                        proj_feat(xT, wb, cc * 128, lambda ps, cc=cc: CP("act", qT[:, cc, :], ps))
                        yield
                    wb = nextw()
                    for cc in range(8):
                        proj_feat(xT, wb, cc * 128, lambda ps, cc=cc: ACT(kT[:, cc, :], ps, AF.Copy, scale=0.0625))
                        yield
                    for c in range(NCH):
                        for hf in range(2):
                            proj_tok(xT, c, wb, hf * 512, 512,
                                     lambda ps, cb0, n, c=c, hf=hf: ACT(k_tok[c][0:Q, hf * 512:hf * 512 + n], ps, AF.Copy, scale=0.0625))
                            yield
                    wbv[0] = nextw()
                    wb = wbv[0]
                    psg = psum(1)
                    for c in range(NCH):
                        for kc in range(8):
                            MM(psg[0:Q, c * 8:(c + 1) * 8], xT[:, kc, c * Q:(c + 1) * Q], wb[:, kc, 1024:1032], kc == 0, kc == 7)
                    TT("dve", gx[0:Q], psg[0:Q, 0:NCH * 8].r("p (c h) -> p c h", c=NCH), gbp[0:Q].us(1).bc([Q, NCH, 8]), ALU.add)
                    yield
                    CP("dve", gi[0:Q], gx[0:Q, :, 0:4])
                    STT(gta[0:Q], gx[0:Q, :, 4:8], -1.0, gx[0:Q, :, 4:8], ALU.mult, ALU.max)
                    ACT(gta[0:Q], gta[0:Q], AF.Exp, scale=-1.0)
                    ACT(gta[0:Q], gta[0:Q], AF.Ln, bias=1.0)
                    STT(logf[0:Q], gx[0:Q, :, 4:8], 0.0, gta[0:Q], ALU.min, ALU.subtract)
                    yield

                wbv = [None]
                run_streams([(s2_stream(), list(range(6))), (m1_stream(), [6, 7])])
                release(mS)
                cut('M1')
                hmT = alloc((8, TTK), BF16)
                v_tok = [alloc(D, BF16) for _ in range(NCH)]
                o_tok = [alloc(D, BF16) for _ in range(NCH)]
                mnw = prow("mnw", Q)
                TS("dve", mnw, mnw, 0.5)
                wb = wbv[0]
                for c in range(NCH):
                    proj_tok(xT, c, wb, 0, 1024, lambda ps, cb0, n, c=c: CP("act", v_tok[c][0:Q, cb0:cb0 + n], ps))
                wb = nextw()
                for c in range(NCH):
                    proj_tok(xT, c, wb, 0, 1024,
                             lambda ps, cb0, n, c=c: ACT(o_tok[c][0:Q, cb0:cb0 + n], ps, AF.Tanh, scale=0.5))

                mT = alloc((8, TTK), BF16)
                sgA = [alloc(TTK, BF16) for _ in range(2)]
                mM2 = mark()
                if prompt:
                    nT_f, nT_b, m_st = S["nT_f"], S["nT_b"], S["m_st"][0:1, :]
                else:
                    nT_f = alloc((2, 4, NS), F32); nT_b = alloc((2, 4, NS), BF16); m_st = alloc(4, F32)[0:NS, :]
                    mm = mark()
                    nat = alloc(256, F32)
                    DMA("sp", nat[0:64, :], dr["s_n"][l])
                    DMA("sp", m_st, dr["s_m"][l])
                    for dc in range(2):
                        ps = psum(1)
                        TR(ps[:, 0:64], nat[0:64, dc * 128:(dc + 1) * 128], identf[0:64, 0:64])
                        CP("dve", nT_f[:, dc].r("p h b -> p b h"), ps[:, 0:64].r("p (b h) -> p b h", h=4))
                    CP("act", nT_b.r("p a h b -> p (a h b)"), nT_f.r("p a h b -> p (a h b)"))
                    release(mm)
                m2b = []
                for _p in range(2 if NCH > 1 else 1):
                    m2b.append({"sm": alloc(48, F32), "gs": alloc(8, F32), "SwT": alloc((4, Q), BF16), "kw": alloc(D, BF16),
                                "wcb": alloc((nseq, 4), F32), "mprev": alloc(4, F32), "pes": alloc(12, F32)})
                R2 = alloc((4, Q), F32); Wt = alloc((4, Q), BF16); Sw = alloc((4, Q), BF16)
                ws = alloc(4, F32); wcs = alloc(4, F32); Z = alloc((nseq, 4), F32)
                ni = alloc(D, F32); denI = alloc(4, F32); emt = alloc(4, F32); junk = alloc(256, BF16)
                ow = alloc(D, F32); hmn = alloc(D, BF16)
                tmpd = None if prompt else alloc((4, NS), F32)
                if not prompt:
                    cq = [{"C_b": alloc((2, 4, 256), BF16), "kwm": alloc(D, BF16)} for _ in range(2)]
                    Cf_l = [alloc((2, 4, 256), F32) for _ in range(5)]

                def m2_front(c, B):
                    cols = slice(c * Q, (c + 1) * Q)
                    sm, gs, SwT, kw, wcb, mprev, pes = B["sm"], B["gs"], B["SwT"], B["kw"], B["wcb"], B["mprev"], B["pes"]
                    pg = psum(1)
                    MM(pg[0:Q, 0:4], Lm, logf[0:Q, c, :])
                    MM(pg[0:Q, 4:8], bmT, m_st)
                    CP("dve", gs[0:Q], pg[0:Q, 0:8])
                    bcum = gs[0:Q, 0:4]; m_tok = gs[0:Q, 4:8]
                    a_ = sm[0:Q, 0:4]; mloc = sm[0:Q, 4:8]; mxx = sm[0:Q, 8:12]; nmxx = sm[0:Q, 12:16]
                    wint = sm[0:Q, 16:20]; den = sm[0:Q, 20:24]; mt = sm[0:Q, 24:28]
                    TT("dve", a_, gi[0:Q, c, :], bcum, ALU.subtract)
                    TT("pool", R2[0:Q], idq_f.us(1).bc([Q, 4, Q]), a_.us(2).bc([Q, 4, Q]), ALU.mult)
                    yield
                    A = psum(1)
                    MM(A[0:Q, 0:4 * Q], onesf[0:Q, 0:Q], R2[0:Q].r("p h s -> p (h s)"), True, False)
                    MM(A[0:Q, 0:4 * Q], idq_b, NEGTb, False, True)
                    RED(mloc, A[0:Q, 0:4 * Q].r("p (h s) -> p h s", h=4), ALU.max)
                    TT("dve", mxx, mloc, m_tok, ALU.max)
                    TS("dve", nmxx, mxx, -1.0)
                    TT("dve", mt, bcum, mxx, ALU.add)
                    for h in range(4):
                        ACT(Wt[0:Q, h, :], A[0:Q, h * Q:(h + 1) * Q], AF.Exp, bias=nmxx[:, h:h + 1])
                    pe_ = psum(1)
                    MM(pe_[0:Q, 0:4], SELLAST, mxx)
                    MM(pe_[0:nseq, 4:8], SELROW, mt)
                    MM(pe_[0:nseq, 8:12], SELROW, mxx)
                    CP("dve", pes[0:Q, 0:4], pe_[0:Q, 0:4])
                    CP("dve", pes[0:nseq, 4:12], pe_[0:nseq, 4:12])
                    CP("dve", mprev[0:nseq], m_st)
                    CP("dve", m_st, pes[0:nseq, 4:8])
                    yield
                    QK = psum(1)
                    for h in range(4):
                        for dc in range(2):
                            MM(QK[0:Q, h * Q:(h + 1) * Q], qT[:, h * 2 + dc, cols], kT[:, h * 2 + dc, cols], dc == 0, dc == 1)
                    TT("dve", Sw[0:Q].r("p h s -> p (h s)"), Wt[0:Q].r("p h s -> p (h s)"), QK[0:Q, 0:4 * Q], ALU.mult)
                    TT("dve", wint, m_tok, mxx, ALU.subtract)
                    ACT(wint, wint, AF.Exp)
                    TT("dve", ws[0:Q], a_, pes[0:Q, 0:4], ALU.subtract)
                    ACT(ws[0:Q], ws[0:Q], AF.Exp)
                    TT("dve", kw[0:Q].r("p (h e) -> p h e", h=4), k_tok[c][0:Q].r("p (h e) -> p h e", h=4),
                       ws[0:Q].us(2).bc([Q, 4, 256]), ALU.mult)
                    TT("dve", wcs[0:nseq], mprev[0:nseq], pes[0:nseq, 8:12], ALU.subtract)
                    ACT(wcs[0:nseq], wcs[0:nseq], AF.Exp)
                    TT("dve", Z[0:nseq], identf[0:nseq, 0:nseq].us(2).bc([nseq, nseq, 4]),
                       wcs[0:nseq].us(1).bc([nseq, nseq, 4]), ALU.mult)
                    yield
                    pw = psum(1).bitcast(BF16)
                    for h in range(4):
                        TR(pw[0:Q, h * Q:(h + 1) * Q], Sw[0:Q, h, :], idq_b)
                    CP("act", SwT[0:Q].r("p h s -> p (h s)"), pw[0:Q, 0:4 * Q])
                    pwc = psum(1)
                    MM(pwc[:, 0:nseq * 4], onesf[0:nseq, 0:128], Z[0:nseq].r("p b h -> p (b h)"))
                    CP("act", wcb.r("p b h -> p (b h)"), pwc[:, 0:nseq * 4])
                    yield

                def m2_back(c, B):
                    cols = slice(c * Q, (c + 1) * Q)
                    sm, gs, SwT, kw, wcb = B["sm"], B["gs"], B["SwT"], B["kw"], B["wcb"]
                    bcum = gs[0:Q, 0:4]
                    wint = sm[0:Q, 16:20]; den = sm[0:Q, 20:24]; mt = sm[0:Q, 24:28]; r_ = sm[0:Q, 28:32]
                    ssq = sm[0:Q, 32:36]; rstd = sm[0:Q, 36:40]
                    pden = psum(1)
                    for h in range(4):
                        MM(pden[0:Q, h:h + 1], SwT[0:Q, h, :], onesb[0:Q, 0:1])
                    for h in range(4):
                        for dc in range(2):
                            MM(pden[0:Q, 8 + h * nseq:8 + (h + 1) * nseq], qT[:, h * 2 + dc, cols], nT_b[:, dc, h, :], dc == 0, dc == 1)
                    if prompt:
                        CP("dve", denI[0:Q], pden[0:Q, 8:12])
                    else:
                        TT("dve", tmpd[0:Q], pden[0:Q, 8:8 + 4 * NS].r("p (h b) -> p h b", h=4), bm.us(1).bc([Q, 4, NS]), ALU.mult)
                        RED(denI[0:Q], tmpd[0:Q], ALU.add)
                    TT("dve", denI[0:Q], denI[0:Q], wint, ALU.mult)
                    TT("dve", den, denI[0:Q], pden[0:Q, 0:4], ALU.add)
                    def load_C(b2):
                        for a2 in range(2):
                            DMA("sp", Cf_l[b2 % 5][:, a2], dr["s_c"][l, b2][:, a2 * 128:(a2 + 1) * 128, :].rearrange("h q e -> q h e"))

                    if not prompt:
                        for b2 in range(3):
                            load_C(b2)
                    for b in range(nseq):
                        if prompt:
                            C_f, C_b = S["C_f"], S["C_b"]
                            kwm = kw
                        else:
                            if b + 3 < nseq:
                                load_C(b + 3)
                            cb2 = cq[b % 2]
                            C_f, C_b, kwm = Cf_l[b % 5], cb2["C_b"], cb2["kwm"]
                            CP("act", C_b.r("p a h e -> p (a h e)"), C_f.r("p a h e -> p (a h e)"))
                            TS("dve", kwm[0:Q], kw[0:Q], bm[:, b:b + 1])
                        NI = psum(2)
                        for h in range(4):
                            for dc in range(2):
                                MM(NI[0:Q, h * 256:(h + 1) * 256], qT[:, h * 2 + dc, cols], C_b[:, dc, h, :], dc == 0, dc == 1)
                        if prompt:
                            TT("dve", ni[0:Q].r("p (h e) -> p h e", h=4), NI[0:Q, :].r("p (h e) -> p h e", h=4),
                               wint.us(2).bc([Q, 4, 256]), ALU.mult)
                        elif b == 0:
                            TS("dve", ni[0:Q], NI[0:Q, :], bm[:, b:b + 1])
                        else:
                            STT(ni[0:Q], NI[0:Q, :], bm[:, b:b + 1], ni[0:Q], ALU.mult, ALU.add)
                        for dc in range(2):
                            Cn = psum(2)
                            for h in range(4):
                                MM(Cn[:, h * 256:(h + 1) * 256], kwm[0:Q, h * 256 + dc * 128:h * 256 + (dc + 1) * 128],
                                   v_tok[c][0:Q, h * 256:(h + 1) * 256])
                            for h in range(4):
                                STT(C_f[:, dc, h, :], C_f[:, dc, h, :], wcb[:, b, h:h + 1], Cn[:, h * 256:(h + 1) * 256],
                                    ALU.mult, ALU.add)
                        if prompt:
                            CP("act", C_b.r("p a h e -> p (a h e)"), C_f.r("p a h e -> p (a h e)"))
                        else:
                            for a2 in range(2):
                                DMA("sp", dr["c_s"][l, b][:, a2 * 128:(a2 + 1) * 128, :].rearrange("h q e -> q h e"), C_f[:, a2])
                        yield
                    if not prompt:
                        TT("dve", ni[0:Q].r("p (h e) -> p h e", h=4), ni[0:Q].r("p (h e) -> p h e", h=4),
                           wint.us(2).bc([Q, 4, 256]), ALU.mult)
                    pn2 = psum(1)
                    for dc in range(2):
                        for h in range(4):
                            MM(pn2[:, (dc * 4 + h) * nseq:(dc * 4 + h + 1) * nseq],
                               kw[0:Q, h * 256 + dc * 128:h * 256 + (dc + 1) * 128], bmb)
                    for dc in range(2):
                        TT("dve", nT_f[:, dc], nT_f[:, dc], wcb.r("p b h -> p h b"), ALU.mult)
                    TT("dve", nT_f.r("p a h b -> p (a h b)"), nT_f.r("p a h b -> p (a h b)"), pn2[:, 0:8 * nseq], ALU.add)
                    CP("act", nT_b.r("p a h b -> p (a h b)"), nT_f.r("p a h b -> p (a h b)"))
                    NUM = psum(2)
                    for h in range(4):
                        MM(NUM[0:Q, h * 256:(h + 1) * 256], SwT[0:Q, h, :], v_tok[c][0:Q, h * 256:(h + 1) * 256])
                    TT("dve", ni[0:Q], ni[0:Q], NUM[0:Q, :], ALU.add)
                    ACT(emt[0:Q], mt, AF.Exp, scale=-1.0)
                    STT(den, den, -1.0, den, ALU.mult, ALU.max)
                    TT("dve", den, den, emt[0:Q], ALU.max)
                    P.rec("dve", lambda e, o=r_.ap, i=den.ap: e.reciprocal(out=o, in_=i), reads=[den], writes=[r_])
                    TT("dve", ni[0:Q].r("p (h e) -> p h e", h=4), ni[0:Q].r("p (h e) -> p h e", h=4),
                       r_.us(2).bc([Q, 4, 256]), ALU.mult)
                    for h in range(4):
                        ACT(junk[0:Q], ni[0:Q, h * 256:(h + 1) * 256], AF.Square, accum=ssq[:, h:h + 1])
                    rsqrt_(rstd, ssq, 1.0 / 256, Q, 4)
                    STT(ow[0:Q], o_tok[c][0:Q], 1.0, mnw, ALU.add, ALU.mult)
                    for h in range(4):
                        STT(hmn[0:Q, h * 256:(h + 1) * 256], ni[0:Q, h * 256:(h + 1) * 256], rstd[:, h:h + 1],
                            ow[0:Q, h * 256:(h + 1) * 256], ALU.mult, ALU.mult)
                    yield
                    pht = psum(1).bitcast(BF16)
                    for kc in range(8):
                        TR(pht[:, kc * Q:(kc + 1) * Q], hmn[0:Q, kc * 128:(kc + 1) * 128], idq_b)
                    CP("act", hmT[:, :, cols], pht[:, 0:8 * Q].r("p (k q) -> p k q", k=8))
                    yield

                def m2_stream():
                    yield from m2_front(0, m2b[0])
                    for c in range(NCH):
                        if c + 1 < NCH:
                            yield from m2_front(c + 1, m2b[(c + 1) % 2])
                        yield from m2_back(c, m2b[c % 2])
                    if last_tile or not prompt:
                        nrow = 4 * nseq
                        po = psum(1)
                        tmpn = ni.r("p (a x) -> p a x", a=2)[:, :, 0:nseq * 4].r("p a (b h) -> p a b h", h=4)
                        for dc in range(2):
                            CP("dve", tmpn[:, dc], nT_f[:, dc].r("p h b -> p b h"))
                            TR(po[0:nrow, dc * 128:(dc + 1) * 128], tmpn[:, dc].r("p b h -> p (b h)"), identf)
                        natn = ow[:, 0:256]
                        CP("act", natn[0:nrow], po[0:nrow, 0:256])
                        DMA("sp", dr["n_p"][l] if prompt else dr["n_s"][l], natn[0:nrow])
                        DMA("sp", dr["m_p"][l] if prompt else dr["m_s"][l], m_st)
                        if prompt:
                            for a2 in range(2):
                                DMA("sp", dr["c_p"][l][:, a2 * 128:(a2 + 1) * 128, :].rearrange("h q e -> q h e"), S["C_f"][:, a2])

                def ga_stream():
                    wa = nextw()
                    wg = nextw(prefetch=False)
                    for j in range(8):
                        pb = psum(1); pgt = psum(1)
                        for kc in range(8):
                            MM(pgt[:, 0:TTK], wg[:, kc, j * 128:(j + 1) * 128], xT[:, kc, :], kc == 0, kc == 7)
                        for kc in range(8):
                            MM(pb[:, 0:TTK], wa[:, kc, j * 128:(j + 1) * 128], yT[:, kc, :], kc == 0, kc == 7)
                        sg = sgA[j % 2]
                        ACT(sg, pgt[:, 0:TTK], AF.Tanh, scale=0.5)
                        STT(mT[:, j, :], sg, 1.0, pb[:, 0:TTK], ALU.add, ALU.mult)
                        yield

                run_streams([(m2_stream(), list(range(6))), (ga_stream(), [6, 7])])
                release(mM2)
                cut('M2')
                cur_ring[0] = make_ring(12 * 1024)
                wa = nextw()
                wg = nextw(prefetch=False)
                for j in range(8):
                    pb = psum(1); pgt = psum(1)
                    for kc in range(8):
                        MM(pgt[:, 0:TTK], wg[:, kc, j * 128:(j + 1) * 128], xT[:, kc, :], kc == 0, kc == 7)
                    for kc in range(8):
                        MM(pb[:, 0:TTK], wa[:, kc, j * 128:(j + 1) * 128], hmT[:, kc, :], kc == 0, kc == 7)
                    sg = sgA[j % 2]
                    ACT(sg, pgt[:, 0:TTK], AF.Tanh, scale=0.5)
                    t2 = cur_ring[0](TTK, F32)
                    STT(t2, sg, 1.0, pb[:, 0:TTK], ALU.add, ALU.mult)
                    TT("dve", mT[:, j, :], mT[:, j, :], t2, ALU.add)

                def layer_norm(c, pss, g_, b_, mixscale=1.0):
                    st6 = cur_ring[0](12, F32); mv = cur_ring[0](4, F32)
                    for hf in range(2):
                        xv = x_tok[0:Q, c, hf * 512:(hf + 1) * 512]
                        if mixscale == 1.0:
                            STT(xv, xv, ALPHA, pss[hf], ALU.mult, ALU.add)
                        else:
                            TS("dve", xv, xv, ALPHA)
                            STT(xv, pss[hf], mixscale, xv, ALU.mult, ALU.add)
                        P.rec("dve", lambda e, o=st6[0:Q, hf * 6:(hf + 1) * 6].ap, i=xv.ap: e.bn_stats(out=o, in_=i),
                              reads=[xv], writes=[st6])
                    P.rec("dve", lambda e, o=mv[0:Q, 0:2].ap, i=st6[0:Q].ap: e.bn_aggr(out=o, in_=i), reads=[st6], writes=[mv])
                    rsqrt_(mv[0:Q, 2:3], mv[0:Q, 1:2], 1.0, Q, 1)
                    xv = x_tok[0:Q, c, :]
                    TS("dve", xv, xv, mv[0:Q, 0:1], mv[0:Q, 2:3], ALU.subtract, ALU.mult)
                    TT("dve", xv, xv, g_, ALU.mult)
                    TT("dve", xv, xv, b_, ALU.add)

                wb = nextw()
                l1g = prow("l1g", Q); l1b = prow("l1b", Q)
                for c in range(NCH):
                    pss = []
                    for hf in range(2):
                        ps = psum(1)
                        for kc in range(8):
                            MM(ps[0:Q, :], mT[:, kc, c * Q:(c + 1) * Q], wb[:, kc, hf * 512:(hf + 1) * 512], kc == 0, kc == 7)
                        pss.append(ps[0:Q, :])
                    layer_norm(c, pss, l1g, l1b, mixscale=0.5)

                cut('G')
                release(m0)
                m0b = mark()
                x1T = alloc((8, TTK), BF16)
                to_featmajor(x_tok, x1T)
                hT = alloc((22, TTK), BF16)
                histF = None
                if not prompt:
                    histF = alloc((44, NS * 2), F32)
                    for piece in range(4):
                        mm = mark()
                        nat = alloc(1408, F32)
                        DMA("sp", nat[0:NS * 2, :], dr["s_fconv"][l, :, piece * 1408:(piece + 1) * 1408])
                        for k in range(11):
                            cc = piece * 11 + k
                            ps = psum(1)
                            TR(ps[:, 0:NS * 2], nat[0:NS * 2, k * 128:(k + 1) * 128], identf[0:NS * 2, 0:NS * 2])
                            CPX(histF[:, cc, :], ps[:, 0:NS * 2])
                        release(mm)

                cur_ring[0] = make_ring(40 * 1024)

                def ffn_conv(ps, cc, silu):
                    if prompt:
                        acc = conv_from_psum(ps, cc, cwf, cbf, S["tl_f"], 3)
                        if silu:
                            sg = cur_ring[0]((nseq, Lq), BF16)
                            ACT(sg, acc, AF.Silu)
                            return sg
                        return acc
                    xp = cur_ring[0]((nseq, 2 + Lq), F32)
                    acc = cur_ring[0]((nseq, Lq), F32)
                    if prompt:
                        CP("act", xp[:, 0, 0:2], S["tl_f"][:, cc, :])
                    else:
                        CP("act", xp[:, :, 0:2], histF[:, cc, :].r("p (b j) -> p b j", j=2))
                    CP("act", xp[:, :, 2:2 + Lq], ps.r("p (b t) -> p b t", b=nseq))
                    ACT(acc, ps.r("p (b t) -> p b t", b=nseq), AF.Identity, scale=cwf[:, cc, 2:3], bias=cbf[:, cc:cc + 1])
                    for j in (1, 0):
                        STT(acc, xp[:, :, j:j + Lq], cwf[:, cc, j:j + 1], acc, ALU.mult, ALU.add)
                    if prompt:
                        CP("act", S["tl_f"][:, cc, :], xp[:, 0, Lq:Lq + 2])
                    if silu:
                        sg = cur_ring[0]((nseq, Lq), BF16)
                        ACT(sg, acc, AF.Silu)
                        return sg
                    return acc

                for bb in range(6):
                    nj = 4 if bb < 5 else 2
                    n = nj * 128
                    wb = nextw()
                    for jj in range(nj):
                        j = bb * 4 + jj
                        pg_ = psum(1); pv_ = psum(1)
                        for kc in range(8):
                            MM(pg_[:, 0:TTK], wb[:, kc, jj * 128:(jj + 1) * 128], x1T[:, kc, :], kc == 0, kc == 7)
                        for kc in range(8):
                            MM(pv_[:, 0:TTK], wb[:, kc, n + jj * 128:n + (jj + 1) * 128], x1T[:, kc, :], kc == 0, kc == 7)
                        sg = ffn_conv(pg_[:, 0:TTK], j, True)
                        av = ffn_conv(pv_[:, 0:TTK], 22 + j, False)
                        TT("dve", hT[:, j, :].r("p (b t) -> p b t", b=nseq), sg, av, ALU.mult)
                    if want_rows:
                        emit_rows(x1T, wb, 0, n, dr["fconv_p"], dr["fconv_s"], 2, bb * 512)
                        emit_rows(x1T, wb, n, n, dr["fconv_p"], dr["fconv_s"], 2, DFF + bb * 512)
                l2g = prow("l2g", Q); l2b = prow("l2b", Q)
                for cb0 in range(2):
                    pss = [psum(1) for _ in range(NCH)]
                    for kh in range(2):
                        wb = nextw()
                        for c in range(NCH):
                            for k in range(11):
                                kc = kh * 11 + k
                                MM(pss[c][0:Q, :], hT[:, kc, c * Q:(c + 1) * Q], wb[:, k, :], kc == 0, kc == 21)
                    if cb0 == 0:
                        keep = []
                        for c in range(NCH):
                            t = alloc(512, F32)
                            CPX(t[0:Q], pss[c][0:Q, :])
                            keep.append(t[0:Q])
                    else:
                        for c in range(NCH):
                            layer_norm(c, [keep[c], pss[c][0:Q, :]], l2g, l2b)
                release(m0b)
                release(m0)

            for l in range(DEPTH):
                if l >= KL:
                    continue
                base_m = mark()
                try:
                    layer_body(l)
                except _Cut:
                    pass
                release(base_m)
                gctr[0] = (tiles.index((kind, ti)) * DEPTH + l + 1) * NWB
                wstate["issued"] = max(wstate["issued"], gctr[0])
            if prompt:
                DMA("sp", dr["y_p"][ti * 512:(ti + 1) * 512, :].rearrange("(c p) d -> p c d", p=128), x_tok)
            else:
                DMA("sp", dr["y_s"], x_tok[0:Q, 0, :])

        print("final peak", astate.get("peak"))
        P.emit()
        build_program.stats = P.stats
    return nc


_NC_CACHE = {}


def _get_nc():
    if "nc" not in _NC_CACHE:
        _NC_CACHE["nc"] = build_program()
    return _NC_CACHE["nc"]


def kernel(x_prompt, x_sample, state_ssd, state_ssd_conv, state_mlstm_c, state_mlstm_n, state_mlstm_m,
           state_ffn_conv, w_in, ssd_conv_w, ssd_conv_b, ssd_dt_bias, ssd_a_log, ssd_d, ssd_norm_w,
           mlstm_gate_b, mlstm_norm_w, w_branch_a, w_branch_b, w_out, ln1_g, ln1_b, ffn_w_up, ffn_conv_w,
           ffn_conv_b, ffn_w_down, ln2_g, ln2_b):
    f = lambda a: np.ascontiguousarray(np.asarray(a, dtype=np.float32))
    nc = _get_nc()
    prow = np.zeros((DEPTH, NPR), np.float32)
    for name, arr in (("dtb", ssd_dt_bias), ("alog", ssd_a_log), ("D", ssd_d), ("gb", mlstm_gate_b),
                      ("snw", ssd_norm_w), ("mnw", mlstm_norm_w), ("l1g", ln1_g), ("l1b", ln1_b),
                      ("l2g", ln2_g), ("l2b", ln2_b)):
        o, n = PR[name]
        prow[:, o:o + n] = f(arr)
    cw_s = f(ssd_conv_w).reshape(DEPTH, 4, 10, 128).transpose(0, 3, 2, 1).reshape(DEPTH, 128, 40)
    cb_s = f(ssd_conv_b).reshape(DEPTH, 10, 128).transpose(0, 2, 1)
    cw_f = f(ffn_conv_w).reshape(DEPTH, 3, 44, 128).transpose(0, 3, 2, 1).reshape(DEPTH, 128, 132)
    cb_f = f(ffn_conv_b).reshape(DEPTH, 44, 128).transpose(0, 2, 1)
    consts = _make_consts()
    shared = {"w_in": f(w_in), "w_a": f(w_branch_a), "w_b": f(w_branch_b), "w_out": f(w_out), "w_up": f(ffn_w_up),
              "w_down": f(ffn_w_down), "prow": prow, "cw_s": f(cw_s), "cb_s": f(cb_s), "cw_f": f(cw_f),
              "cb_f": f(cb_f), "consts": consts}
    xp = f(x_prompt); xs = f(x_sample)
    s_ssd = f(state_ssd); s_sc = f(state_ssd_conv); s_c = f(state_mlstm_c); s_n = f(state_mlstm_n)
    s_m = f(state_mlstm_m); s_fc = f(state_ffn_conv)
    in_maps = []
    for i in range(NCORES):
        sl = slice(i * NS, (i + 1) * NS)
        m = dict(shared)
        m["xp"] = xp[i]
        m["xs"] = np.ascontiguousarray(xs[sl].reshape(NS * SL, D))
        m["s_ssd"] = np.ascontiguousarray(s_ssd[:, sl].reshape(DEPTH, NS, 1024, 64))
        m["s_sconv"] = np.ascontiguousarray(s_sc[:, sl].reshape(DEPTH, NS * 3, 1280))
        m["s_c"] = np.ascontiguousarray(s_c[:, sl])
        m["s_n"] = np.ascontiguousarray(s_n[:, sl].reshape(DEPTH, NS * 4, 256))
        m["s_m"] = np.ascontiguousarray(s_m[:, sl])
        m["s_fconv"] = np.ascontiguousarray(s_fc[:, sl].reshape(DEPTH, NS * 2, 2 * DFF))
        in_maps.append(m)
    res = run_bass_kernel_spmd(nc, in_maps, core_ids=list(range(NCORES)))
    R = res.results
    cat0 = lambda k: np.stack([R[i][k] for i in range(NCORES)], 0)
    y_p = cat0("y_p")
    y_s = cat0("y_s").reshape(NCORES * NS, SL, D)
    ssd_p = np.stack([R[i]["ssd_p"] for i in range(NCORES)], 1).reshape(DEPTH, NCORES, 16, 64, 64)
    ssd_s = np.concatenate([R[i]["ssd_s"] for i in range(NCORES)], 1).reshape(DEPTH, NCORES * NS, 16, 64, 64)
    sconv_p = np.stack([R[i]["sconv_p"] for i in range(NCORES)], 1)
    sconv_s = np.concatenate([R[i]["sconv_s"] for i in range(NCORES)], 1)
    c_p = np.stack([R[i]["c_p"] for i in range(NCORES)], 1)
    c_s = np.concatenate([R[i]["c_s"] for i in range(NCORES)], 1)
    n_p = np.stack([R[i]["n_p"] for i in range(NCORES)], 1)
    n_s = np.concatenate([R[i]["n_s"].reshape(DEPTH, NS, 4, 256) for i in range(NCORES)], 1)
    m_p = np.stack([R[i]["m_p"].reshape(DEPTH, 4) for i in range(NCORES)], 1)
    m_s = np.concatenate([R[i]["m_s"] for i in range(NCORES)], 1)
    fconv_p = np.stack([R[i]["fconv_p"] for i in range(NCORES)], 1)
    fconv_s = np.concatenate([R[i]["fconv_s"] for i in range(NCORES)], 1)
    outs = (y_p, y_s, ssd_p, ssd_s, sconv_p, sconv_s, c_p, c_s, n_p, n_s, m_p, m_s, fconv_p, fconv_s)
    return tuple(np.ascontiguousarray(o, dtype=np.float32) for o in outs)
```

```python
import contextlib
import os
import numpy as np
import concourse.bass as bass
import concourse.mybir as mybir
from concourse.bass_utils import run_bass_kernel_spmd

F32 = mybir.dt.float32
BF16 = mybir.dt.bfloat16
AF = mybir.ActivationFunctionType
ALU = mybir.AluOpType
AX = mybir.AxisListType

NCORES = 8
D = 1024
SEQ = 2048
DEPTH = 2
NS = 16
SL = 4
NIN = 8472
DFF = 2816
ALPHA = (2 * DEPTH) ** 0.25
EPS = 1e-5
NEGV = -30000.0
ENGS = ("pe", "act", "dve", "pool", "sp")


class _Cut(Exception):
    pass


KT = os.environ.get('KT', '')
KL = int(os.environ.get('KL', '2'))
KCUT = os.environ.get('KCUT', '')


def cut(name):
    if KCUT == name:
        raise _Cut()


class Op:
    __slots__ = ("eng", "fn", "deps", "dma", "sem", "val", "waits", "snap", "needed")

    def __init__(self, eng, fn, dma):
        self.eng = eng
        self.fn = fn
        self.deps = set()
        self.dma = dma
        self.sem = None
        self.val = 0
        self.waits = []
        self.snap = None
        self.needed = False


class Buf:
    __slots__ = ("ap", "toks", "meta")

    def __init__(self, ap, toks, meta=None):
        self.ap = ap
        self.toks = toks
        self.meta = meta

    def __getitem__(self, k):
        return Buf(self.ap[k], self.toks, self.meta)

    def r(self, pat, **kw):
        return Buf(self.ap.rearrange(pat, **kw), self.toks, self.meta)

    def us(self, axis):
        return Buf(self.ap.unsqueeze(axis), self.toks, self.meta)

    def bc(self, shape):
        return Buf(self.ap.to_broadcast(list(shape)), self.toks, self.meta)

    def bitcast(self, dt):
        return Buf(self.ap.bitcast(dt), self.toks, self.meta)


def _toks(lst):
    out = []
    for b in lst:
        if b is None or isinstance(b, (int, float)):
            continue
        if isinstance(b, Buf):
            out.extend(b.toks)
        else:
            out.append(b)
    return out


def _ap(x):
    return x.ap if isinstance(x, Buf) else x


class Prog:
    def __init__(self, nc, n_dma_sems=12):
        self.nc = nc
        self.ops = {e: [] for e in ENGS}
        self.order = []
        self.last_write = {}
        self.readers = {}
        self.n_dma_sems = n_dma_sems
        self.checker = None

    def rec(self, eng, fn, reads=(), writes=(), dma=False):
        op = Op(eng, fn, dma)
        if self.checker is not None:
            for b in list(reads) + list(writes):
                if isinstance(b, Buf) and b.meta is not None:
                    self.checker(b)
        rt = _toks(reads)
        wt = _toks(writes)
        if any(isinstance(t, tuple) and t[0] == "ps" for t in rt):
            wt = wt + [t for t in rt if isinstance(t, tuple) and t[0] == "ps"]
            rt = [t for t in rt if not (isinstance(t, tuple) and t[0] == "ps")]
        for t in rt:
            w = self.last_write.get(t)
            if w is not None:
                op.deps.add(w)
        for t in wt:
            w = self.last_write.get(t)
            if w is not None:
                op.deps.add(w)
            for r in self.readers.get(t, ()):
                op.deps.add(r)
        for t in rt:
            lst = self.readers.setdefault(t, [])
            if not dma:
                for i, r in enumerate(lst):
                    if (not r.dma) and r.eng == eng:
                        lst[i] = op
                        break
                else:
                    lst.append(op)
            else:
                lst.append(op)
        for t in wt:
            self.last_write[t] = op
            self.readers[t] = []
        op.deps.discard(op)
        self.ops[eng].append(op)
        self.order.append(op)
        return op

    def emit(self):
        nc = self.nc
        for op in self.order:
            if op.eng == "pe" and not op.dma:
                op.deps = {d for d in op.deps if not (d.eng == "pe" and not d.dma)}
        dma_cnt = {}
        dma_last = {}
        for op in self.order:
            if op.dma:
                k = dma_cnt.get(op.eng, 0)
                dma_cnt[op.eng] = k + 1
                slot = (op.eng, k % self.n_dma_sems)
                prev = dma_last.get(slot)
                if prev is not None:
                    op.deps.add(prev)
                dma_last[slot] = op
                op.sem = slot
                op.val = 16 * (k // self.n_dma_sems + 1)
        for op in self.order:
            for d in op.deps:
                d.needed = True
        final_ops = list(dma_last.values())
        cnt = {e: 0 for e in ENGS}
        for op in self.order:
            if not op.dma and op.needed:
                cnt[op.eng] += 1
                op.sem = ("c", op.eng)
                op.val = cnt[op.eng]
        know = {e: {} for e in ENGS}
        for op in self.order:
            kn = know[op.eng]
            wd = {}
            for d in sorted(op.deps, key=lambda d: -d.val):
                if d.sem is None:
                    continue
                if kn.get(d.sem, 0) >= d.val:
                    continue
                if wd.get(d.sem, 0) < d.val:
                    wd[d.sem] = d.val
                for s, v in d.snap.items():
                    if kn.get(s, 0) < v:
                        kn[s] = v
            op.waits = list(wd.items())
            if op.sem is not None:
                sn = dict(kn)
                sn[op.sem] = op.val
                op.snap = sn
        sem_keys = []
        seen = set()
        for op in self.order:
            if op.sem is not None and op.sem not in seen:
                seen.add(op.sem)
                sem_keys.append(op.sem)
        self.stats = {e: len(self.ops[e]) for e in ENGS}
        self.stats["sems"] = len(sem_keys)
        self.stats["waits"] = sum(len(op.waits) for op in self.order)
        self.stats["cnt"] = dict(cnt)
        self.stats["dma"] = dict(dma_cnt)
        sems = {}
        with contextlib.ExitStack() as st:
            for i, k in enumerate(sem_keys):
                sems[k] = st.enter_context(nc.semaphore("s%d" % i))
            block = st.enter_context(nc.Block())
            fin = [(sems[d.sem], d.val) for d in final_ops]
            ops = self.ops

            def run(eng_name, eng):
                for op in ops[eng_name]:
                    for s, v in op.waits:
                        eng.wait_ge(sems[s], v)
                    ins = op.fn(eng)
                    if op.sem is not None:
                        ins.then_inc(sems[op.sem], 16 if op.dma else 1)
                if eng_name == "sp":
                    for s, v in fin:
                        eng.wait_ge(s, v)

            @block.tensor
            def _(e):
                run("pe", e)

            @block.scalar
            def _(e):
                run("act", e)

            @block.vector
            def _(e):
                run("dve", e)

            @block.gpsimd
            def _(e):
                run("pool", e)

            @block.sync
            def _(e):
                run("sp", e)


def _const_layout():
    lay = {}
    off = 0

    def add(name, n):
        nonlocal off
        lay[name] = (off, n)
        off += n

    add("ident", 128)
    add("ones", 128)
    for sfx, q in (("p", 128), ("s", 64)):
        add("L" + sfx, q)
        add("U" + sfx, q)
        add("SELLAST" + sfx, q)
        add("NEG" + sfx, 512)
        add("NEGT" + sfx, 4 * q)
    add("SELROWp", 1)
    add("SELROWs", NS)
    add("bms", NS)
    add("bmTs", 64)
    return lay, off


CL, NCONST = _const_layout()


def _make_consts():
    c = np.zeros((128, NCONST), np.float32)

    def put(name, arr):
        o, n = CL[name]
        a = np.asarray(arr, np.float32)
        c[: a.shape[0], o:o + a.shape[1]] = a

    put("ident", np.eye(128))
    put("ones", np.ones((128, 128)))
    for sfx, q, sl in (("p", 128, 128), ("s", 64, SL)):
        i = np.arange(q)
        seq = i // sl
        same = seq[:, None] == seq[None, :]
        put("L" + sfx, same & (i[:, None] <= i[None, :]))
        put("U" + sfx, same & (i[:, None] > i[None, :]))
        last = seq * sl + sl - 1
        put("SELLAST" + sfx, i[:, None] == last[None, :])
        neg = np.where(same & (i[None, :] >= i[:, None]), 0.0, NEGV)
        put("NEG" + sfx, np.tile(neg, (1, 512 // q)))
        negt = np.where(same & (i[None, :] <= i[:, None]), 0.0, NEGV)
        put("NEGT" + sfx, np.tile(negt, (1, 4)))
    sr = np.zeros((128, 1))
    sr[127, 0] = 1
    put("SELROWp", sr)
    i = np.arange(64)
    put("SELROWs", (i[:, None] == (np.arange(NS) * SL + SL - 1)[None, :]))
    bm = (i[:, None] // SL) == np.arange(NS)[None, :]
    put("bms", bm)
    put("bmTs", bm.T)
    return c


PR = {"dtb": (0, 16), "alog": (16, 16), "D": (32, 16), "gb": (48, 8), "snw": (64, 1024), "mnw": (1088, 1024),
      "l1g": (2112, 1024), "l1b": (3136, 1024), "l2g": (4160, 1024), "l2b": (5184, 1024)}
NPR = 6208

def _wblocks():
    blks = [("w_in", 0, 8, [(0, 1024)]), ("w_in", 0, 8, [(1024, 1024)]), ("w_in", 0, 8, [(2048, 272)]),
            ("w_in", 0, 8, [(2320, 1024)]), ("w_in", 0, 8, [(3344, 1024)]), ("w_in", 0, 8, [(4368, 1032)]),
            ("w_in", 0, 8, [(5400, 1024)]), ("w_a", 0, 8, [(0, 1024)]), ("w_in", 0, 8, [(6424, 1024)]),
            ("w_b", 0, 8, [(0, 1024)]), ("w_in", 0, 8, [(7448, 1024)]), ("w_out", 0, 8, [(0, 1024)])]
    for bb in range(6):
        n = 512 if bb < 5 else 256
        blks.append(("w_up", 0, 8, [(bb * 512, n), (DFF + bb * 512, n)]))
    for cb in range(2):
        for kh in range(2):
            blks.append(("w_down", kh * 11 * 128, 11, [(cb * 512, 512)]))
    return blks


WBLK = _wblocks()
NWB = len(WBLK)
WCAP = 8 * 1032
NWBUF = 2
USE_WSC = True


def build_program():
    nc = bass.Bass("TRN2", target_bir_lowering=False)
    dr = {}

    def din(name, shape):
        dr[name] = nc.dram_tensor(name, list(shape), F32, kind="ExternalInput").ap()
        return dr[name]

    def dout(name, shape):
        dr[name] = nc.dram_tensor(name, list(shape), F32, kind="ExternalOutput").ap()
        return dr[name]

    din("xp", (SEQ, D)); din("xs", (NS * SL, D))
    din("s_ssd", (DEPTH, NS, 1024, 64)); din("s_sconv", (DEPTH, NS * 3, 1280))
    din("s_c", (DEPTH, NS, 4, 256, 256)); din("s_n", (DEPTH, NS * 4, 256)); din("s_m", (DEPTH, NS, 4))
    din("s_fconv", (DEPTH, NS * 2, 2 * DFF))
    din("w_in", (DEPTH, D, NIN)); din("w_a", (DEPTH, D, D)); din("w_b", (DEPTH, D, D)); din("w_out", (DEPTH, D, D))
    din("w_up", (DEPTH, D, 2 * DFF)); din("w_down", (DEPTH, DFF, D))
    din("prow", (DEPTH, NPR)); din("cw_s", (DEPTH, 128, 40)); din("cb_s", (DEPTH, 128, 10))
    din("cw_f", (DEPTH, 128, 132)); din("cb_f", (DEPTH, 128, 44)); din("consts", (128, NCONST))
    dout("y_p", (SEQ, D)); dout("y_s", (NS * SL, D))
    dout("ssd_p", (DEPTH, 1024, 64)); dout("ssd_s", (DEPTH, NS, 1024, 64))
    dout("sconv_p", (DEPTH, 3, 1280)); dout("sconv_s", (DEPTH, NS, 3, 1280))
    dout("c_p", (DEPTH, 4, 256, 256)); dout("c_s", (DEPTH, NS, 4, 256, 256))
    dout("n_p", (DEPTH, 4, 256)); dout("n_s", (DEPTH, NS * 4, 256))
    dout("m_p", (DEPTH, 1, 4)); dout("m_s", (DEPTH, NS, 4))
    dout("fconv_p", (DEPTH, 2, 2 * DFF)); dout("fconv_s", (DEPTH, NS, 2, 2 * DFF))

    wsc = nc.dram_tensor("wsc", [DEPTH * NWB, 128, WCAP], BF16, kind="Internal").ap()
    st = contextlib.ExitStack()
    with st:
        P = Prog(nc)
        ARENA_BYTES = 211456
        arena_t = st.enter_context(nc.sbuf_tensor("arena", [128, ARENA_BYTES // 4], F32))
        PS_t = st.enter_context(nc.psum_tensor("PS", [128, 4096], F32))
        astate = {"off": 0}

        def alloc(free, dt=F32):
            if isinstance(free, int):
                free = (free,)
            n = int(np.prod(free))
            nb = n * (4 if dt == F32 else 2)
            nb = (nb + 127) // 128 * 128
            o = astate["off"]
            astate["off"] = o + nb
            astate["peak"] = max(astate.get("peak", 0), astate["off"])
            assert astate["off"] <= ARENA_BYTES, ("arena overflow", astate["off"])
            base = arena_t[:, o // 4:(o + nb) // 4]
            ap = base if dt == F32 else base.bitcast(BF16)
            ap = ap[:, 0:n]
            if len(free) == 2:
                ap = ap.rearrange("p (a b) -> p a b", a=free[0])
            elif len(free) == 3:
                ap = ap.rearrange("p (a b c) -> p a b c", a=free[0], b=free[1])
            elif len(free) == 4:
                ap = ap.rearrange("p (a b c d) -> p a b c d", a=free[0], b=free[1], c=free[2])
            toks = [("sb", k) for k in range(o // 128, (o + nb) // 128)]
            return Buf(ap, toks)

        def mark():
            return astate["off"]

        def release(m):
            astate["off"] = m

        ps_use = [0] * 8
        ps_clock = [0]
        ps_gen = [0] * 8

        ring_gen = {}

        def ps_check(b):
            for kind, idx, g in b.meta:
                if kind == "ps":
                    assert ps_gen[idx] == g, ("stale PSUM buffer used", idx, g, ps_gen[idx])
                else:
                    assert ring_gen.get(idx, 0) == g, ("stale ring buffer used", idx, g, ring_gen.get(idx, 0))

        def make_ring(nbytes):
            base = (astate["off"] + 511) // 512 * 512
            nbytes = nbytes // 512 * 512
            astate["off"] = base + nbytes
            astate["peak"] = max(astate.get("peak", 0), astate["off"])
            assert astate["off"] <= ARENA_BYTES, ("arena overflow (ring)", astate["off"])
            assert base % 512 == 0 or True
            st_ = {"o": 0}

            def ralloc(free, dt=F32):
                if isinstance(free, int):
                    free = (free,)
                n = int(np.prod(free))
                nb = n * (4 if dt == F32 else 2)
                nb = (nb + 511) // 512 * 512
                assert nb <= nbytes
                if st_["o"] + nb > nbytes:
                    st_["o"] = 0
                o = base + st_["o"]
                st_["o"] += nb
                o4 = (o + 3) // 4 * 4
                bs_ = arena_t[:, o4 // 4:(o4 + nb) // 4 if o4 + nb <= ARENA_BYTES else ARENA_BYTES // 4]
                ap = bs_ if dt == F32 else bs_.bitcast(BF16)
                ap = ap[:, 0:n]
                if len(free) == 2:
                    ap = ap.rearrange("p (a b) -> p a b", a=free[0])
                elif len(free) == 3:
                    ap = ap.rearrange("p (a b c) -> p a b c", a=free[0], b=free[1])
                slots = list(range(o // 512, (o + nb) // 512))
                meta = []
                for k in slots:
                    ring_gen[k] = ring_gen.get(k, 0) + 1
                    meta.append(("ring", k, ring_gen[k]))
                return Buf(ap, [("sb", k) for k in range(o // 128, (o + nb) // 128)], tuple(meta))

            return ralloc

        P.checker = ps_check

        ps_allowed = [list(range(8))]

        def run_streams(streams):
            alive = list(streams)
            while alive:
                for item in list(alive):
                    g, banks = item
                    ps_allowed[0] = banks
                    try:
                        next(g)
                    except StopIteration:
                        alive.remove(item)
            ps_allowed[0] = list(range(8))

        def psum(nb=1):
            best, bs = None, None
            for s in range(0, 8, nb):
                if any(b not in ps_allowed[0] for b in range(s, s + nb)):
                    continue
                sc = max(ps_use[s:s + nb])
                if best is None or sc < best:
                    best, bs = sc, s
            ps_clock[0] += 1
            for b in range(bs, bs + nb):
                ps_use[b] = ps_clock[0]
                ps_gen[b] += 1
            return Buf(PS_t[:, bs * 512:(bs + nb) * 512], [("ps", b) for b in range(bs, bs + nb)],
                       tuple(("ps", b, ps_gen[b]) for b in range(bs, bs + nb)))

        def MM(out, lhsT, rhs, start=True, stop=True):
            P.rec("pe", lambda e: e.matmul(out.ap, lhsT.ap, rhs.ap, start=start, stop=stop),
                  reads=[lhsT, rhs], writes=[out])

        def TR(out, in_, ident):
            P.rec("pe", lambda e: e.transpose(out.ap, in_.ap, ident.ap), reads=[in_, ident], writes=[out])

        def ACT(out, in_, func, scale=1.0, bias=None, accum=None):
            kw = {}
            if bias is not None:
                kw["bias"] = _ap(bias)
            if accum is not None:
                kw["accum_out"] = accum.ap
            sc = _ap(scale)
            P.rec("act", lambda e: e.activation(out=out.ap, in_=in_.ap, func=func, scale=sc, **kw),
                  reads=[in_, scale, bias], writes=[out, accum])

        def TT(eng, out, in0, in1, op):
            P.rec(eng, lambda e: e.tensor_tensor(out=out.ap, in0=in0.ap, in1=in1.ap, op=op),
                  reads=[in0, in1], writes=[out])

        def TS(eng, out, in0, s1, s2=None, op0=ALU.mult, op1=None):
            a1, a2 = _ap(s1), _ap(s2)
            if op1 is None:
                P.rec(eng, lambda e: e.tensor_scalar(out=out.ap, in0=in0.ap, scalar1=a1, scalar2=None, op0=op0),
                      reads=[in0, s1], writes=[out])
            else:
                P.rec(eng, lambda e: e.tensor_scalar(out=out.ap, in0=in0.ap, scalar1=a1, scalar2=a2, op0=op0, op1=op1),
                      reads=[in0, s1, s2], writes=[out])

        def STT(out, in0, scalar, in1, op0, op1):
            sc = _ap(scalar)
            P.rec("dve", lambda e: e.scalar_tensor_tensor(out=out.ap, in0=in0.ap, scalar=sc, in1=in1.ap, op0=op0, op1=op1),
                  reads=[in0, scalar, in1], writes=[out])

        def CP(eng, out, in_):
            if eng == "act":
                ACT(out, in_, AF.Copy)
            else:
                P.rec(eng, lambda e: e.tensor_copy(out=out.ap, in_=in_.ap), reads=[in_], writes=[out])

        def MEMSET(eng, out, v):
            P.rec(eng, lambda e: e.memset(out.ap, v), writes=[out])

        def DMA(q, out, in_, reads=(), writes=()):
            oa, ia = _ap(out), _ap(in_)
            P.rec(q, lambda e: e.dma_start(out=oa, in_=ia), reads=list(reads) + ([in_] if isinstance(in_, Buf) else []),
                  writes=list(writes) + ([out] if isinstance(out, Buf) else []), dma=True)

        def RED(out, in_, op):
            P.rec("dve", lambda e: e.tensor_reduce(out=out.ap, in_=in_.ap, axis=AX.X, op=op), reads=[in_], writes=[out])

        cp_rr = [0]

        def CPX(out, in_):
            cp_rr[0] ^= 1
            CP("act" if cp_rr[0] else "dve", out, in_)

        cf = alloc(NCONST, F32)
        cb_ = alloc(NCONST, BF16)
        DMA("sp", cf, dr["consts"])
        DMA("pool", cb_, dr["consts"])

        def CF(name, rows=128, sub=None):
            o, n = CL[name]
            if sub is not None:
                o, n = o + sub[0], sub[1]
            return cf[0:rows, o:o + n]

        def CB(name, rows=128, sub=None):
            o, n = CL[name]
            if sub is not None:
                o, n = o + sub[0], sub[1]
            return cb_[0:rows, o:o + n]

        mhalf = alloc(1, F32)
        MEMSET("pool", mhalf, -0.5)
        wbufs = [alloc(WCAP, BF16) for _ in range(NWBUF)]
        x_tok = alloc((4, D), F32)
        prm_small = alloc(64, F32)
        cws = alloc((10, 4), F32); cbs = alloc(10, F32); cwf = alloc((44, 3), F32); cbf = alloc(44, F32)
        a_bc = alloc(16, F32)
        pst = []
        for l in range(DEPTH):
            s = {"hT_f": alloc(512, F32), "hT_b": alloc(512, BF16), "C_f": alloc((2, 4, 256), F32),
                 "C_b": alloc((2, 4, 256), BF16), "nT_f": alloc((2, 4, 1), F32), "nT_b": alloc((2, 4, 1), BF16),
                 "m_st": alloc(4, F32), "tl_s": alloc((10, 3), F32), "tl_f": alloc((44, 2), F32)}
            for k in ("hT_f", "hT_b", "C_f", "C_b", "nT_f", "nT_b", "m_st", "tl_s", "tl_f"):
                MEMSET("pool", s[k], 0.0)
            pst.append(s)

        wstate = {"issued": 0, "gs": None, "lst": None}

        def wbuf_of(g):
            if wstate["gs"] is not None and g >= wstate["gs"]:
                lst = wstate["lst"]
                return lst[(g - wstate["gs"]) % len(lst)]
            return wbufs[g % NWBUF]

        def wdepth(g):
            if wstate["gs"] is not None and g >= wstate["gs"]:
                return len(wstate["lst"]) - 1
            return 1

        def wissue(g):
            l = (g // NWB) % DEPTH
            name, row0, nk, segs = WBLK[g % NWB]
            wb = wbuf_of(g)
            tot = sum(n for _, n in segs)
            view = wb[:, 0:nk * tot].r("p (k n) -> p k n", k=nk)
            lk = g % (DEPTH * NWB)
            if g >= DEPTH * NWB and USE_WSC:
                DMA("pool", wb[:, 0:nk * tot], wsc[lk, :, 0:nk * tot], reads=[("dram", lk)])
                return
            src = dr[name][l]
            o = 0
            for c0, n in segs:
                sap = src[row0:row0 + nk * 128, c0:c0 + n].rearrange("(k p) n -> p k n", p=128)
                DMA("pool", view[:, :, o:o + n], sap)
                o += n
            if USE_WSC:
                DMA("sp", wsc[lk, :, 0:nk * tot], wb[:, 0:nk * tot], writes=[("dram", lk)])

        def wget(g, total, prefetch=True):
            while wstate["issued"] <= min(g + (wdepth(g) if prefetch else 0), total - 1):
                nx = wstate["issued"]
                if nx >= SAMPLE_G0 and wstate["gs"] is None and nx > g:
                    break
                wissue(nx)
                wstate["issued"] += 1
            name, row0, nk, segs = WBLK[g % NWB]
            tot = sum(n for _, n in segs)
            return wbuf_of(g)[:, 0:nk * tot].r("p (k n) -> p k n", k=nk)

        tiles = [("p", i) for i in range(4)] + [("s", 0)]
        if KT:
            tiles = [(t[0], int(t[1:] or 0)) for t in KT.split(",")]
        total_blocks = len(tiles) * DEPTH * NWB
        SAMPLE_G0 = 10 ** 9
        for _i, (_k, _t) in enumerate(tiles):
            if _k == "s":
                SAMPLE_G0 = _i * DEPTH * NWB
        gctr = [0]

        def nextw(prefetch=True):
            g = gctr[0]
            gctr[0] += 1
            return wget(g, total_blocks, prefetch)

        identf = CF("ident")
        identb = CB("ident")
        onesf = CF("ones")
        onesb = CB("ones")

        def rsqrt_(out, in_, scale, rows, n):
            TS("dve", out, in_, scale, EPS, ALU.mult, ALU.add)
            TT("pool", out, out, mhalf[0:rows, 0:1].bc([rows, n]), ALU.pow)

        for kind, ti in tiles:
            prompt = kind == "p"
            Q = 128 if prompt else 64
            NCH = 4 if prompt else 1
            TTK = Q * NCH
            nseq = 1 if prompt else NS
            Lq = TTK if prompt else SL
            sfx = "p" if prompt else "s"
            Lm = CF("L" + sfx, Q); Um = CF("U" + sfx, Q); SELLAST = CF("SELLAST" + sfx, Q)
            NEGb = CB("NEG" + sfx, Q); NEGTb = CB("NEGT" + sfx, Q)
            SELROW = CF("SELROW" + sfx, Q)
            if prompt:
                bm = onesf[0:Q, 0:1]; bmb = onesb[0:Q, 0:1]; bmT = onesf[0:1, 0:Q]
            else:
                bm = CF("bms", Q); bmb = CB("bms", Q); bmT = CF("bmTs", NS)
            idq_f = identf[0:Q, 0:Q]; idq_b = identb[0:Q, 0:Q]
            last_tile = prompt and ti == 3
            want_rows = last_tile or not prompt

            if not prompt and tiles.index((kind, ti)) * DEPTH * NWB == SAMPLE_G0:
                extra = []
                wstate["gs"] = max(SAMPLE_G0, wstate["issued"])
                wstate["lst"] = extra + [wbufs[(wstate["gs"] + k) % NWBUF] for k in range(NWBUF)]
            if prompt:
                DMA("sp", x_tok, dr["xp"][ti * 512:(ti + 1) * 512, :].rearrange("(c p) d -> p c d", p=128))
            else:
                DMA("sp", x_tok[0:Q, 0, :], dr["xs"])

            def layer_body(l):
                S = pst[l]
                m0 = mark()
                DMA("sp", prm_small, dr["prow"][l, 0:64].partition_broadcast(128))
                DMA("sp", cws, dr["cw_s"][l].rearrange("p (a b) -> p a b", a=10))
                DMA("sp", cbs, dr["cb_s"][l])
                DMA("sp", cwf, dr["cw_f"][l].rearrange("p (a b) -> p a b", a=44))
                DMA("sp", cbf, dr["cb_f"][l])
                ACT(a_bc, prm_small[:, 16:32], AF.Exp)
                TS("dve", a_bc, a_bc, -1.0)
                dtb = prm_small[:, 0:16]; Dp = prm_small[:, 32:48]; gbp = prm_small[:, 48:56]

                def prow(name, rows):
                    o, n = PR[name]
                    b = alloc(n, F32)
                    DMA("sp", b, dr["prow"][l, o:o + n].partition_broadcast(128))
                    return b[0:rows, :]

                xT = alloc((8, TTK), BF16)
                yT = alloc((8, TTK), BF16)
                qT = alloc((8, TTK), BF16)
                kT = alloc((8, TTK), BF16)
                k_tok = [alloc(D, BF16) for _ in range(NCH)]
                gi = alloc((NCH, 4), F32)
                logf = alloc((NCH, 4), F32)
                mS = mark()

                def to_featmajor(src_tok, dstT):
                    for c in range(NCH):
                        for half in range(2):
                            ps = psum(1)
                            for j in range(4):
                                kc = half * 4 + j
                                TR(ps[:, j * Q:(j + 1) * Q], src_tok[0:Q, c, kc * 128:(kc + 1) * 128], idq_f)
                            CPX(dstT[:, half * 4:half * 4 + 4, c * Q:(c + 1) * Q],
                                ps[:, 0:4 * Q].r("p (j q) -> p j q", j=4))

                to_featmajor(x_tok, xT)

                def proj_tok(src, c, wb, col0, ncols, evac, rows=None):
                    r0, r1 = (c * Q, (c + 1) * Q) if rows is None else rows
                    M = r1 - r0
                    for cb0 in range(0, ncols, 512):
                        n = min(512, ncols - cb0)
                        ps = psum(1)
                        for kc in range(8):
                            MM(ps[0:M, 0:n], src[:, kc, r0:r1], wb[:, kc, col0 + cb0:col0 + cb0 + n], kc == 0, kc == 7)
                        evac(ps[0:M, 0:n], cb0, n)

                def proj_feat(src, wb, col0, evac, m=128):
                    ps = psum(1)
                    for kc in range(8):
                        MM(ps[0:m, 0:TTK], wb[:, kc, col0:col0 + m], src[:, kc, 0:TTK], kc == 0, kc == 7)
                    evac(ps[0:m, 0:TTK])

                if prompt:
                    row_rng = (TTK - 32, TTK)
                else:
                    row_rng = (0, 64)
                RM = row_rng[1] - row_rng[0]

                def emit_rows(src, wb, col0, ncols, dst_p, dst_s, nrow, dcol0):
                    def ev(ps, cb0, n):
                        t = cur_ring[0](512, F32)
                        CPX(t[0:RM, 0:n], ps)
                        if prompt:
                            DMA("sp", dst_p[l, :, dcol0 + cb0:dcol0 + cb0 + n], t[RM - nrow:RM, 0:n])
                        else:
                            for tt in range(nrow):
                                tok = SL - nrow + tt
                                DMA("sp", dst_s[l, :, tt, dcol0 + cb0:dcol0 + cb0 + n], t[tok:64:SL, 0:n])
                    proj_tok(src, 0, wb, col0, ncols, ev, rows=row_rng)

                cut('P0')
                z_tok = [alloc(D, BF16) for _ in range(NCH)]
                xsT = alloc((8, TTK), BF16)
                BT = alloc(TTK, BF16)
                CT = alloc(TTK, BF16)
                dt_ = alloc((NCH, 16), F32)
                dta = alloc((NCH, 16), F32)
                wb = nextw()
                for c in range(NCH):
                    proj_tok(xT, c, wb, 0, 1024,
                             lambda ps, cb0, n, c=c: ACT(z_tok[c][0:Q, cb0:cb0 + n], ps, AF.Silu))
                histS = None
                if not prompt:
                    histS = alloc((10, NS * 3), F32)
                    mm = mark()
                    nat = alloc(1280, F32)
                    DMA("sp", nat[0:NS * 3, :], dr["s_sconv"][l])
                    for cc in range(10):
                        ps = psum(1)
                        TR(ps[:, 0:NS * 3], nat[0:NS * 3, cc * 128:(cc + 1) * 128], identf[0:NS * 3, 0:NS * 3])
                        CPX(histS[:, cc, :], ps[:, 0:NS * 3])
                    release(mm)

                mR = mark()
                cur_ring = [make_ring(20 * 1024)]

                def conv_chunk(ps, cc):
                    xp = cur_ring[0]((nseq, 3 + Lq), F32)
                    acc = cur_ring[0]((nseq, Lq), F32)
                    if prompt:
                        CP("act", xp[:, 0, 0:3], S["tl_s"][:, cc, :])
                    else:
                        CP("act", xp[:, :, 0:3], histS[:, cc, :].r("p (b j) -> p b j", j=3))
                    CP("act", xp[:, :, 3:3 + Lq], ps.r("p (b t) -> p b t", b=nseq))
                    ACT(acc, ps.r("p (b t) -> p b t", b=nseq), AF.Identity, scale=cws[:, cc, 3:4], bias=cbs[:, cc:cc + 1])
                    for j in (2, 1, 0):
                        STT(acc, xp[:, :, j:j + Lq], cws[:, cc, j:j + 1], acc, ALU.mult, ALU.add)
                    if cc < 8:
                        dst = xsT[:, cc, :]
                    elif cc == 8:
                        dst = BT
                    else:
                        dst = CT
                    ACT(dst.r("p (b t) -> p b t", b=nseq), acc, AF.Silu)
                    if prompt:
                        CP("act", S["tl_s"][:, cc, :], xp[:, 0, Lq:Lq + 3])

                wb = nextw()
                for cc in range(8):
                    proj_feat(xT, wb, cc * 128, lambda ps, cc=cc: conv_chunk(ps, cc))
                if want_rows:
                    emit_rows(xT, wb, 0, 1024, dr["sconv_p"], dr["sconv_s"], 3, 0)
                wb = nextw()
                for cc in (8, 9):
                    proj_feat(xT, wb, (cc - 8) * 128, lambda ps, cc=cc: conv_chunk(ps, cc))
                if want_rows:
                    emit_rows(xT, wb, 0, 256, dr["sconv_p"], dr["sconv_s"], 3, 1024)
                psd = psum(1)
                for c in range(NCH):
                    for kc in range(8):
                        MM(psd[0:Q, c * 16:(c + 1) * 16], xT[:, kc, c * Q:(c + 1) * Q], wb[:, kc, 256:272], kc == 0, kc == 7)
                mm = mark()
                tx = alloc((NCH, 16), F32); ta = alloc((NCH, 16), F32)
                TT("dve", tx[0:Q], psd[0:Q, 0:NCH * 16].r("p (c h) -> p c h", c=NCH), dtb[0:Q].us(1).bc([Q, NCH, 16]), ALU.add)
                STT(ta[0:Q], tx[0:Q], -1.0, tx[0:Q], ALU.mult, ALU.max)
                ACT(ta[0:Q], ta[0:Q], AF.Exp, scale=-1.0)
                ACT(ta[0:Q], ta[0:Q], AF.Ln, bias=1.0)
                STT(dt_[0:Q], tx[0:Q], 0.0, ta[0:Q], ALU.max, ALU.add)
                TT("dve", dta[0:Q], dt_[0:Q], a_bc[0:Q].us(1).bc([Q, NCH, 16]), ALU.mult)
                release(mm)

                release(mR)
                cut('S1')
                snw = prow("snw", Q)
                gx = alloc((NCH, 8), F32); gta = alloc((NCH, 4), F32)
                s2b = []
                for _p in range(2 if NCH > 1 else 1):
                    s2b.append({"xs_tok": alloc(D, BF16), "xdt": alloc(D, BF16), "B_tok": alloc(128, BF16),
                                "GT": alloc((16, Q), BF16), "eac": alloc(32, F32), "cdb": alloc((nseq, 8), F32),
                                "xdtw": alloc(D, BF16)})
                R = alloc((16, Q), F32); expT = alloc((16, Q), BF16); CBs = alloc((2, Q), BF16)
                Xs = None if prompt else alloc((2, NS, 8), F32)
                yo = alloc(D, F32); xsD = alloc(D, F32); ss = alloc(2, F32); y_n = alloc(D, BF16)
                if not prompt:
                    sq = [{"hT_f": alloc(512, F32), "hT_b": alloc(512, BF16),
                           "xw": alloc(D, BF16), "hn": alloc(512, F32)} for _ in range(2)]
                    nat_l = [alloc((8, 64), F32) for _ in range(6)]
                    natn_l = [alloc((8, 64), F32) for _ in range(6)]

                def s2_front(c, B):
                    cols = slice(c * Q, (c + 1) * Q)
                    psx = psum(1).bitcast(BF16)
                    for kc in range(8):
                        TR(psx[0:Q, kc * 128:(kc + 1) * 128], xsT[:, kc, cols], identb)
                    CP("act", B["xs_tok"][0:Q], psx[0:Q, :])
                    TT("dve", B["xdt"][0:Q].r("p (h e) -> p h e", h=16), psx[0:Q, :].r("p (h e) -> p h e", h=16),
                       dt_[0:Q, c, :].us(2).bc([Q, 16, 64]), ALU.mult)
                    psb = psum(1).bitcast(BF16)
                    TR(psb[0:Q, 0:128], BT[:, cols], identb)
                    CP("act", B["B_tok"][0:Q], psb[0:Q, 0:128])
                    TT("pool", R[0:Q], Lm.us(1).bc([Q, 16, Q]), dta[0:Q, c, :].us(2).bc([Q, 16, Q]), ALU.mult)
                    yield
                    nbk = 16 * Q // 512
                    SEG = psum(nbk)
                    Rf = R[0:Q].r("p h l -> p (h l)")
                    for bk in range(nbk):
                        MM(SEG[0:Q, bk * 512:(bk + 1) * 512], Um, Rf[:, bk * 512:(bk + 1) * 512], True, False)
                        MM(SEG[0:Q, bk * 512:(bk + 1) * 512], idq_b, NEGb, False, True)
                    ACT(expT[0:Q].r("p h l -> p (h l)"), SEG[0:Q, 0:16 * Q], AF.Exp)
                    psc = psum(2)
                    for g in range(2):
                        MM(psc[0:Q, g * 512:g * 512 + Q], BT[g * 64:(g + 1) * 64, cols], CT[g * 64:(g + 1) * 64, cols])
                    pt = psum(1)
                    MM(pt[0:Q, 0:16], Lm, dta[0:Q, c, :])
                    MM(pt[0:Q, 16:32], Um, dta[0:Q, c, :])
                    pcd = psum(1)
                    if prompt:
                        for g in range(2):
                            MM(pcd[g * 64:(g + 1) * 64, 0:8], onesf[0:Q, 0:64], dta[0:Q, c, g * 8:(g + 1) * 8])
                    else:
                        for g in range(2):
                            TT("dve", Xs[0:Q, g], bm.us(2).bc([Q, NS, 8]),
                               dta[0:Q, c, g * 8:(g + 1) * 8].us(1).bc([Q, NS, 8]), ALU.mult)
                            MM(pcd[g * 64:(g + 1) * 64, 0:NS * 8], onesf[0:Q, 0:64], Xs[0:Q, g].r("p b h -> p (b h)"))
                    yield
                    CP("dve", CBs[0:Q], psc[0:Q, :].r("p (g x) -> p g x", g=2)[:, :, 0:Q])
                    ACT(B["eac"][0:Q], pt[0:Q, 0:32], AF.Exp)
                    ACT(B["cdb"].r("p b h -> p (b h)"), pcd[:, 0:nseq * 8], AF.Exp)
                    TT("dve", B["GT"][0:Q].r("p (g k) l -> p g k l", g=2), expT[0:Q].r("p (g k) l -> p g k l", g=2),
                       CBs[0:Q].us(2).bc([Q, 2, 8, Q]), ALU.mult)
                    TT("dve", B["xdtw"][0:Q].r("p (h e) -> p h e", h=16), B["xdt"][0:Q].r("p (h e) -> p h e", h=16),
                       B["eac"][0:Q, 16:32].us(2).bc([Q, 16, 64]), ALU.mult)
                    yield

                def s2_back(c, B):
                    cols = slice(c * Q, (c + 1) * Q)
                    eac, cdb, xdtw, xdt, GT, xs_tok, B_tok = (B["eac"], B["cdb"], B["xdtw"], B["xdt"], B["GT"],
                                                             B["xs_tok"], B["B_tok"])
                    def load_nat(b2):
                        nv = nat_l[b2 % 6].r("p (j g) n -> p j g n", g=2)
                        for g in range(2):
                            DMA("sp", nv[:, :, g, :],
                                dr["s_ssd"][l, b2][g * 512:(g + 1) * 512, :].rearrange("(j q) n -> q j n", q=128))

                    if not prompt:
                        for b2 in range(3):
                            load_nat(b2)
                    for b in range(nseq):
                        if prompt:
                            hT_f, hT_b = S["hT_f"], S["hT_b"]
                        else:
                            if b + 3 < nseq:
                                load_nat(b + 3)
                            sb_ = sq[b % 2]
                            nat, hT_f, hT_b = nat_l[b % 6], sb_["hT_f"], sb_["hT_b"]
                            natv = nat.r("p (j g) n -> p j g n", g=2)
                            pn = psum(1)
                            for jj in range(4):
                                TR(pn[:, jj * 128:(jj + 1) * 128], natv[:, jj].r("p g n -> p (g n)"), identf)
                            CP("dve", hT_f, pn)
                            CP("act", hT_b, pn)
                        YO = psum(2)
                        for g in range(2):
                            MM(YO[0:Q, g * 512:(g + 1) * 512], CT[g * 64:(g + 1) * 64, cols], hT_b[g * 64:(g + 1) * 64, :])
                        if prompt:
                            TT("dve", yo[0:Q].r("p (h e) -> p h e", h=16), YO[0:Q, :].r("p (h e) -> p h e", h=16),
                               eac[0:Q, 0:16].us(2).bc([Q, 16, 64]), ALU.mult)
                            xw = xdtw
                        else:
                            if b == 0:
                                TS("dve", yo[0:Q], YO[0:Q, :], bm[:, b:b + 1])
                            else:
                                STT(yo[0:Q], YO[0:Q, :], bm[:, b:b + 1], yo[0:Q], ALU.mult, ALU.add)
                            xw = sb_["xw"]
                            TS("dve", xw[0:Q], xdtw[0:Q], bm[:, b:b + 1])
                        HL = psum(1)
                        for g in range(2):
                            MM(HL[g * 64:(g + 1) * 64, :], B_tok[0:Q, g * 64:(g + 1) * 64], xw[0:Q, g * 512:(g + 1) * 512])
                        hn = hT_f if prompt else sb_["hn"]
                        TT("dve", hn.r("p (h e) -> p h e", h=8), hT_f.r("p (h e) -> p h e", h=8),
                           cdb[:, b, :].us(2).bc([128, 8, 64]), ALU.mult)
                        TT("dve", hn, hn, HL, ALU.add)
                        if prompt:
                            CP("act", hT_b, hn)
                        else:
                            po = psum(1)
                            for jj in range(4):
                                TR(po[:, jj * 128:(jj + 1) * 128], hn[:, jj * 128:(jj + 1) * 128], identf)
                            natn = natn_l[b % 6]
                            CP("act", natn.r("p k n -> p (k n)"), po)
                            natnv = natn.r("p (j g) n -> p j g n", g=2)
                            for g in range(2):
                                DMA("sp", dr["ssd_s"][l, b][g * 512:(g + 1) * 512, :].rearrange("(j q) n -> q j n", q=128),
                                    natnv[:, :, g, :])
                        yield
                    if not prompt:
                        TT("dve", yo[0:Q].r("p (h e) -> p h e", h=16), yo[0:Q].r("p (h e) -> p h e", h=16),
                           eac[0:Q, 0:16].us(2).bc([Q, 16, 64]), ALU.mult)
                    TT("pool", xsD[0:Q].r("p (h e) -> p h e", h=16), xs_tok[0:Q].r("p (h e) -> p h e", h=16),
                       Dp[0:Q].us(2).bc([Q, 16, 64]), ALU.mult)
                    Y = psum(2)
                    for h in range(16):
                        MM(Y[0:Q, h * 64:(h + 1) * 64], GT[0:Q, h, :], xdt[0:Q, h * 64:(h + 1) * 64])
                    TT("dve", yo[0:Q], yo[0:Q], Y[0:Q, :], ALU.add)
                    TT("dve", yo[0:Q], yo[0:Q], xsD[0:Q], ALU.add)
                    TT("dve", yo[0:Q], yo[0:Q], z_tok[c][0:Q], ALU.mult)
                    ACT(xsD[0:Q], yo[0:Q], AF.Square, accum=ss[0:Q, 0:1])
                    rsqrt_(ss[0:Q, 1:2], ss[0:Q, 0:1], 1.0 / 1024, Q, 1)
                    STT(y_n[0:Q], yo[0:Q], ss[0:Q, 1:2], snw, ALU.mult, ALU.mult)
                    yield
                    pyt = psum(1).bitcast(BF16)
                    for kc in range(8):
                        TR(pyt[:, kc * Q:(kc + 1) * Q], y_n[0:Q, kc * 128:(kc + 1) * 128], idq_b)
                    CP("act", yT[:, :, cols], pyt[:, 0:8 * Q].r("p (k q) -> p k q", k=8))
                    yield

                def s2_stream():
                    yield from s2_front(0, s2b[0])
                    for c in range(NCH):
                        if c + 1 < NCH:
                            yield from s2_front(c + 1, s2b[(c + 1) % 2])
                        yield from s2_back(c, s2b[c % 2])
                    if last_tile:
                        po = psum(2)
                        for blk in range(8):
                            g, jj = blk // 4, blk % 4
                            TR(po[:, g * 512 + jj * 64:g * 512 + (jj + 1) * 64], S["hT_f"][g * 64:(g + 1) * 64, jj * 128:(jj + 1) * 128],
                               identf[g * 64:(g + 1) * 64, g * 64:(g + 1) * 64])
                        natn = yo.r("p (k n) -> p k n", k=16)[:, 0:8, :]
                        CP("act", natn.r("p (g j) n -> p g (j n)", g=2), po.r("p (g x) -> p g x", g=2)[:, :, 0:256])
                        DMA("sp", dr["ssd_p"][l].rearrange("(k q) n -> q k n", q=128), natn)

                def m1_stream():
                    wb = nextw()
                    for cc in range(8):
                        proj_feat(xT, wb, cc * 128, lambda ps, cc=cc: CP("act", qT[:, cc, :], ps))
                        yield
                    wb = nextw()
                    for cc in range(8):
                        proj_feat(xT, wb, cc * 128, lambda ps, cc=cc: ACT(kT[:, cc, :], ps, AF.Copy, scale=0.0625))
                        yield
                    for c in range(NCH):
                        for hf in range(2):
                            proj_tok(xT, c, wb, hf * 512, 512,
                                     lambda ps, cb0, n, c=c, hf=hf: ACT(k_tok[c][0:Q, hf * 512:hf * 512 + n], ps, AF.Copy, scale=0.0625))
                            yield
                    wbv[0] = nextw()
                    wb = wbv[0]
                    psg = psum(1)
                    for c in range(NCH):
                        for kc in range(8):
                            MM(psg[0:Q, c * 8:(c + 1) * 8], xT[:, kc, c * Q:(c + 1) * Q], wb[:, kc, 1024:1032], kc == 0, kc == 7)
                    TT("dve", gx[0:Q], psg[0:Q, 0:NCH * 8].r("p (c h) -> p c h", c=NCH), gbp[0:Q].us(1).bc([Q, NCH, 8]), ALU.add)
                    yield
                    CP("dve", gi[0:Q], gx[0:Q, :, 0:4])
                    STT(gta[0:Q], gx[0:Q, :, 4:8], -1.0, gx[0:Q, :, 4:8], ALU.mult, ALU.max)
                    ACT(gta[0:Q], gta[0:Q], AF.Exp, scale=-1.0)
                    ACT(gta[0:Q], gta[0:Q], AF.Ln, bias=1.0)
                    STT(logf[0:Q], gx[0:Q, :, 4:8], 0.0, gta[0:Q], ALU.min, ALU.subtract)
                    yield

                wbv = [None]
                run_streams([(s2_stream(), list(range(6))), (m1_stream(), [6, 7])])
                release(mS)
                cut('M1')
                hmT = alloc((8, TTK), BF16)
                v_tok = [alloc(D, BF16) for _ in range(NCH)]
                o_tok = [alloc(D, BF16) for _ in range(NCH)]
                mnw = prow("mnw", Q)
                TS("dve", mnw, mnw, 0.5)

                mT = alloc((8, TTK), BF16)
                sgA = [alloc(TTK, BF16) for _ in range(2)]
                mM2 = mark()
                if prompt:
                    nT_f, nT_b, m_st = S["nT_f"], S["nT_b"], S["m_st"][0:1, :]
                else:
                    nT_f = alloc((2, 4, NS), F32); nT_b = alloc((2, 4, NS), BF16); m_st = alloc(4, F32)[0:NS, :]
                    mm = mark()
                    nat = alloc(256, F32)
                    DMA("sp", nat[0:64, :], dr["s_n"][l])
                    DMA("sp", m_st, dr["s_m"][l])
                    for dc in range(2):
                        ps = psum(1)
                        TR(ps[:, 0:64], nat[0:64, dc * 128:(dc + 1) * 128], identf[0:64, 0:64])
                        CP("dve", nT_f[:, dc].r("p h b -> p b h"), ps[:, 0:64].r("p (b h) -> p b h", h=4))
                    CP("act", nT_b.r("p a h b -> p (a h b)"), nT_f.r("p a h b -> p (a h b)"))
                    release(mm)
                m2b = []
                for _p in range(2 if NCH > 1 else 1):
                    m2b.append({"sm": alloc(48, F32), "gs": alloc(8, F32), "SwT": alloc((4, Q), BF16), "kw": alloc(D, BF16),
                                "wcb": alloc((nseq, 4), F32), "mprev": alloc(4, F32), "pes": alloc(12, F32)})
                R2 = alloc((4, Q), F32); Wt = alloc((4, Q), BF16); Sw = alloc((4, Q), BF16)
                ws = alloc(4, F32); wcs = alloc(4, F32); Z = alloc((nseq, 4), F32)
                ni = alloc(D, F32); denI = alloc(4, F32); emt = alloc(4, F32); junk = alloc(256, BF16)
                ow = alloc(D, F32); hmn = alloc(D, BF16)
                tmpd = None if prompt else alloc((4, NS), F32)
                if not prompt:
                    cq = [{"C_b": alloc((2, 4, 256), BF16), "kwm": alloc(D, BF16)} for _ in range(2)]
                    Cf_l = [alloc((2, 4, 256), F32) for _ in range(5)]

                def m2_front(c, B):
                    cols = slice(c * Q, (c + 1) * Q)
                    sm, gs, SwT, kw, wcb, mprev, pes = B["sm"], B["gs"], B["SwT"], B["kw"], B["wcb"], B["mprev"], B["pes"]
                    pg = psum(1)
                    MM(pg[0:Q, 0:4], Lm, logf[0:Q, c, :])
                    MM(pg[0:Q, 4:8], bmT, m_st)
                    CP("dve", gs[0:Q], pg[0:Q, 0:8])
                    bcum = gs[0:Q, 0:4]; m_tok = gs[0:Q, 4:8]
                    a_ = sm[0:Q, 0:4]; mloc = sm[0:Q, 4:8]; mxx = sm[0:Q, 8:12]; nmxx = sm[0:Q, 12:16]
                    wint = sm[0:Q, 16:20]; den = sm[0:Q, 20:24]; mt = sm[0:Q, 24:28]
                    TT("dve", a_, gi[0:Q, c, :], bcum, ALU.subtract)
                    TT("pool", R2[0:Q], idq_f.us(1).bc([Q, 4, Q]), a_.us(2).bc([Q, 4, Q]), ALU.mult)
                    yield
                    A = psum(1)
                    MM(A[0:Q, 0:4 * Q], onesf[0:Q, 0:Q], R2[0:Q].r("p h s -> p (h s)"), True, False)
                    MM(A[0:Q, 0:4 * Q], idq_b, NEGTb, False, True)
                    RED(mloc, A[0:Q, 0:4 * Q].r("p (h s) -> p h s", h=4), ALU.max)
                    TT("dve", mxx, mloc, m_tok, ALU.max)
                    TS("dve", nmxx, mxx, -1.0)
                    TT("dve", mt, bcum, mxx, ALU.add)
                    for h in range(4):
                        ACT(Wt[0:Q, h, :], A[0:Q, h * Q:(h + 1) * Q], AF.Exp, bias=nmxx[:, h:h + 1])
                    pe_ = psum(1)
                    MM(pe_[0:Q, 0:4], SELLAST, mxx)
                    MM(pe_[0:nseq, 4:8], SELROW, mt)
                    MM(pe_[0:nseq, 8:12], SELROW, mxx)
                    CP("dve", pes[0:Q, 0:4], pe_[0:Q, 0:4])
                    CP("dve", pes[0:nseq, 4:12], pe_[0:nseq, 4:12])
                    CP("dve", mprev[0:nseq], m_st)
                    CP("dve", m_st, pes[0:nseq, 4:8])
                    yield
                    QK = psum(1)
                    for h in range(4):
                        for dc in range(2):
                            MM(QK[0:Q, h * Q:(h + 1) * Q], qT[:, h * 2 + dc, cols], kT[:, h * 2 + dc, cols], dc == 0, dc == 1)
                    TT("dve", Sw[0:Q].r("p h s -> p (h s)"), Wt[0:Q].r("p h s -> p (h s)"), QK[0:Q, 0:4 * Q], ALU.mult)
                    TT("dve", wint, m_tok, mxx, ALU.subtract)
                    ACT(wint, wint, AF.Exp)
                    TT("dve", ws[0:Q], a_, pes[0:Q, 0:4], ALU.subtract)
                    ACT(ws[0:Q], ws[0:Q], AF.Exp)
                    TT("dve", kw[0:Q].r("p (h e) -> p h e", h=4), k_tok[c][0:Q].r("p (h e) -> p h e", h=4),
                       ws[0:Q].us(2).bc([Q, 4, 256]), ALU.mult)
                    TT("dve", wcs[0:nseq], mprev[0:nseq], pes[0:nseq, 8:12], ALU.subtract)
                    ACT(wcs[0:nseq], wcs[0:nseq], AF.Exp)
                    TT("dve", Z[0:nseq], identf[0:nseq, 0:nseq].us(2).bc([nseq, nseq, 4]),
                       wcs[0:nseq].us(1).bc([nseq, nseq, 4]), ALU.mult)
                    yield
                    pw = psum(1).bitcast(BF16)
                    for h in range(4):
                        TR(pw[0:Q, h * Q:(h + 1) * Q], Sw[0:Q, h, :], idq_b)
                    CP("act", SwT[0:Q].r("p h s -> p (h s)"), pw[0:Q, 0:4 * Q])
                    pwc = psum(1)
                    MM(pwc[:, 0:nseq * 4], onesf[0:nseq, 0:128], Z[0:nseq].r("p b h -> p (b h)"))
                    CP("act", wcb.r("p b h -> p (b h)"), pwc[:, 0:nseq * 4])
                    yield

                def m2_back(c, B):
                    cols = slice(c * Q, (c + 1) * Q)
                    sm, gs, SwT, kw, wcb = B["sm"], B["gs"], B["SwT"], B["kw"], B["wcb"]
                    bcum = gs[0:Q, 0:4]
                    wint = sm[0:Q, 16:20]; den = sm[0:Q, 20:24]; mt = sm[0:Q, 24:28]; r_ = sm[0:Q, 28:32]
                    ssq = sm[0:Q, 32:36]; rstd = sm[0:Q, 36:40]
                    pden = psum(1)
                    for h in range(4):
                        MM(pden[0:Q, h:h + 1], SwT[0:Q, h, :], onesb[0:Q, 0:1])
                    for h in range(4):
                        for dc in range(2):
                            MM(pden[0:Q, 8 + h * nseq:8 + (h + 1) * nseq], qT[:, h * 2 + dc, cols], nT_b[:, dc, h, :], dc == 0, dc == 1)
                    if prompt:
                        CP("dve", denI[0:Q], pden[0:Q, 8:12])
                    else:
                        TT("dve", tmpd[0:Q], pden[0:Q, 8:8 + 4 * NS].r("p (h b) -> p h b", h=4), bm.us(1).bc([Q, 4, NS]), ALU.mult)
                        RED(denI[0:Q], tmpd[0:Q], ALU.add)
                    TT("dve", denI[0:Q], denI[0:Q], wint, ALU.mult)
                    TT("dve", den, denI[0:Q], pden[0:Q, 0:4], ALU.add)
                    def load_C(b2):
                        for a2 in range(2):
                            DMA("sp", Cf_l[b2 % 5][:, a2], dr["s_c"][l, b2][:, a2 * 128:(a2 + 1) * 128, :].rearrange("h q e -> q h e"))

                    if not prompt:
                        for b2 in range(3):
                            load_C(b2)
                    for b in range(nseq):
                        if prompt:
                            C_f, C_b = S["C_f"], S["C_b"]
                            kwm = kw
                        else:
                            if b + 3 < nseq:
                                load_C(b + 3)
                            cb2 = cq[b % 2]
                            C_f, C_b, kwm = Cf_l[b % 5], cb2["C_b"], cb2["kwm"]
                            CP("act", C_b.r("p a h e -> p (a h e)"), C_f.r("p a h e -> p (a h e)"))
                            TS("dve", kwm[0:Q], kw[0:Q], bm[:, b:b + 1])
                        NI = psum(2)
                        for h in range(4):
                            for dc in range(2):
                                MM(NI[0:Q, h * 256:(h + 1) * 256], qT[:, h * 2 + dc, cols], C_b[:, dc, h, :], dc == 0, dc == 1)
                        if prompt:
                            TT("dve", ni[0:Q].r("p (h e) -> p h e", h=4), NI[0:Q, :].r("p (h e) -> p h e", h=4),
                               wint.us(2).bc([Q, 4, 256]), ALU.mult)
                        elif b == 0:
                            TS("dve", ni[0:Q], NI[0:Q, :], bm[:, b:b + 1])
                        else:
                            STT(ni[0:Q], NI[0:Q, :], bm[:, b:b + 1], ni[0:Q], ALU.mult, ALU.add)
                        for dc in range(2):
                            Cn = psum(2)
                            for h in range(4):
                                MM(Cn[:, h * 256:(h + 1) * 256], kwm[0:Q, h * 256 + dc * 128:h * 256 + (dc + 1) * 128],
                                   v_tok[c][0:Q, h * 256:(h + 1) * 256])
                            for h in range(4):
                                STT(C_f[:, dc, h, :], C_f[:, dc, h, :], wcb[:, b, h:h + 1], Cn[:, h * 256:(h + 1) * 256],
                                    ALU.mult, ALU.add)
                        if prompt:
                            CP("act", C_b.r("p a h e -> p (a h e)"), C_f.r("p a h e -> p (a h e)"))
                        else:
                            for a2 in range(2):
                                DMA("sp", dr["c_s"][l, b][:, a2 * 128:(a2 + 1) * 128, :].rearrange("h q e -> q h e"), C_f[:, a2])
                        yield
                    if not prompt:
                        TT("dve", ni[0:Q].r("p (h e) -> p h e", h=4), ni[0:Q].r("p (h e) -> p h e", h=4),
                           wint.us(2).bc([Q, 4, 256]), ALU.mult)
                    pn2 = psum(1)
                    for dc in range(2):
                        for h in range(4):
                            MM(pn2[:, (dc * 4 + h) * nseq:(dc * 4 + h + 1) * nseq],
                               kw[0:Q, h * 256 + dc * 128:h * 256 + (dc + 1) * 128], bmb)
                    for dc in range(2):
                        TT("dve", nT_f[:, dc], nT_f[:, dc], wcb.r("p b h -> p h b"), ALU.mult)
                    TT("dve", nT_f.r("p a h b -> p (a h b)"), nT_f.r("p a h b -> p (a h b)"), pn2[:, 0:8 * nseq], ALU.add)
                    CP("act", nT_b.r("p a h b -> p (a h b)"), nT_f.r("p a h b -> p (a h b)"))
                    NUM = psum(2)
                    for h in range(4):
                        MM(NUM[0:Q, h * 256:(h + 1) * 256], SwT[0:Q, h, :], v_tok[c][0:Q, h * 256:(h + 1) * 256])
                    TT("dve", ni[0:Q], ni[0:Q], NUM[0:Q, :], ALU.add)
                    ACT(emt[0:Q], mt, AF.Exp, scale=-1.0)
                    STT(den, den, -1.0, den, ALU.mult, ALU.max)
                    TT("dve", den, den, emt[0:Q], ALU.max)
                    P.rec("dve", lambda e, o=r_.ap, i=den.ap: e.reciprocal(out=o, in_=i), reads=[den], writes=[r_])
                    TT("dve", ni[0:Q].r("p (h e) -> p h e", h=4), ni[0:Q].r("p (h e) -> p h e", h=4),
                       r_.us(2).bc([Q, 4, 256]), ALU.mult)
                    for h in range(4):
                        ACT(junk[0:Q], ni[0:Q, h * 256:(h + 1) * 256], AF.Square, accum=ssq[:, h:h + 1])
                    rsqrt_(rstd, ssq, 1.0 / 256, Q, 4)
                    STT(ow[0:Q], o_tok[c][0:Q], 1.0, mnw, ALU.add, ALU.mult)
                    for h in range(4):
                        STT(hmn[0:Q, h * 256:(h + 1) * 256], ni[0:Q, h * 256:(h + 1) * 256], rstd[:, h:h + 1],
                            ow[0:Q, h * 256:(h + 1) * 256], ALU.mult, ALU.mult)
                    yield
                    pht = psum(1).bitcast(BF16)
                    for kc in range(8):
                        TR(pht[:, kc * Q:(kc + 1) * Q], hmn[0:Q, kc * 128:(kc + 1) * 128], idq_b)
                    CP("act", hmT[:, :, cols], pht[:, 0:8 * Q].r("p (k q) -> p k q", k=8))
                    yield

                def m2_stream():
                    yield from m2_front(0, m2b[0])
                    for c in range(NCH):
                        if c + 1 < NCH:
                            yield from m2_front(c + 1, m2b[(c + 1) % 2])
                        yield from m2_back(c, m2b[c % 2])
                    if last_tile or not prompt:
                        nrow = 4 * nseq
                        po = psum(1)
                        tmpn = ni.r("p (a x) -> p a x", a=2)[:, :, 0:nseq * 4].r("p a (b h) -> p a b h", h=4)
                        for dc in range(2):
                            CP("dve", tmpn[:, dc], nT_f[:, dc].r("p h b -> p b h"))
                            TR(po[0:nrow, dc * 128:(dc + 1) * 128], tmpn[:, dc].r("p b h -> p (b h)"), identf)
                        natn = ow[:, 0:256]
                        CP("act", natn[0:nrow], po[0:nrow, 0:256])
                        DMA("sp", dr["n_p"][l] if prompt else dr["n_s"][l], natn[0:nrow])
                        DMA("sp", dr["m_p"][l] if prompt else dr["m_s"][l], m_st)
                        if prompt:
                            for a2 in range(2):
                                DMA("sp", dr["c_p"][l][:, a2 * 128:(a2 + 1) * 128, :].rearrange("h q e -> q h e"), S["C_f"][:, a2])

                def ga_stream():
                    wbo = None
                    for c in range(NCH):
                        for hf in range(2):
                            proj_tok(xT, c, wbv[0], hf * 512, 512,
                                     lambda ps, cb0, n, c=c, hf=hf: CP("act", v_tok[c][0:Q, hf * 512:hf * 512 + n], ps))
                            yield
                        if wbo is None:
                            wbo = nextw(prefetch=False)
                        for hf in range(2):
                            proj_tok(xT, c, wbo, hf * 512, 512,
                                     lambda ps, cb0, n, c=c, hf=hf: ACT(o_tok[c][0:Q, hf * 512:hf * 512 + n], ps, AF.Tanh, scale=0.5))
                            yield
                    wa = nextw()
                    wg = nextw(prefetch=False)
                    for j in range(8):
                        pb = psum(1); pgt = psum(1)
                        for kc in range(8):
                            MM(pgt[:, 0:TTK], wg[:, kc, j * 128:(j + 1) * 128], xT[:, kc, :], kc == 0, kc == 7)
                        for kc in range(8):
                            MM(pb[:, 0:TTK], wa[:, kc, j * 128:(j + 1) * 128], yT[:, kc, :], kc == 0, kc == 7)
                        sg = sgA[j % 2]
                        ACT(sg, pgt[:, 0:TTK], AF.Tanh, scale=0.5)
                        STT(mT[:, j, :], sg, 1.0, pb[:, 0:TTK], ALU.add, ALU.mult)
                        yield

                run_streams([(m2_stream(), list(range(6))), (ga_stream(), [6, 7])])
                release(mM2)
                cut('M2')
                cur_ring[0] = make_ring(12 * 1024)
                wa = nextw()
                wg = nextw(prefetch=False)
                for j in range(8):
                    pb = psum(1); pgt = psum(1)
                    for kc in range(8):
                        MM(pgt[:, 0:TTK], wg[:, kc, j * 128:(j + 1) * 128], xT[:, kc, :], kc == 0, kc == 7)
                    for kc in range(8):
                        MM(pb[:, 0:TTK], wa[:, kc, j * 128:(j + 1) * 128], hmT[:, kc, :], kc == 0, kc == 7)
                    sg = sgA[j % 2]
                    ACT(sg, pgt[:, 0:TTK], AF.Tanh, scale=0.5)
                    t2 = cur_ring[0](TTK, F32)
                    STT(t2, sg, 1.0, pb[:, 0:TTK], ALU.add, ALU.mult)
                    TT("dve", mT[:, j, :], mT[:, j, :], t2, ALU.add)

                def layer_norm(c, pss, g_, b_, mixscale=1.0):
                    st6 = cur_ring[0](12, F32); mv = cur_ring[0](4, F32)
                    for hf in range(2):
                        xv = x_tok[0:Q, c, hf * 512:(hf + 1) * 512]
                        if mixscale == 1.0:
                            STT(xv, xv, ALPHA, pss[hf], ALU.mult, ALU.add)
                        else:
                            TS("dve", xv, xv, ALPHA)
                            STT(xv, pss[hf], mixscale, xv, ALU.mult, ALU.add)
                        P.rec("dve", lambda e, o=st6[0:Q, hf * 6:(hf + 1) * 6].ap, i=xv.ap: e.bn_stats(out=o, in_=i),
                              reads=[xv], writes=[st6])
                    P.rec("dve", lambda e, o=mv[0:Q, 0:2].ap, i=st6[0:Q].ap: e.bn_aggr(out=o, in_=i), reads=[st6], writes=[mv])
                    rsqrt_(mv[0:Q, 2:3], mv[0:Q, 1:2], 1.0, Q, 1)
                    xv = x_tok[0:Q, c, :]
                    TS("dve", xv, xv, mv[0:Q, 0:1], mv[0:Q, 2:3], ALU.subtract, ALU.mult)
                    TT("dve", xv, xv, g_, ALU.mult)
                    TT("dve", xv, xv, b_, ALU.add)

                wb = nextw()
                l1g = prow("l1g", Q); l1b = prow("l1b", Q)
                for c in range(NCH):
                    pss = []
                    for hf in range(2):
                        ps = psum(1)
                        for kc in range(8):
                            MM(ps[0:Q, :], mT[:, kc, c * Q:(c + 1) * Q], wb[:, kc, hf * 512:(hf + 1) * 512], kc == 0, kc == 7)
                        pss.append(ps[0:Q, :])
                    layer_norm(c, pss, l1g, l1b, mixscale=0.5)

                cut('G')
                release(m0)
                m0b = mark()
                x1T = alloc((8, TTK), BF16)
                to_featmajor(x_tok, x1T)
                hT = alloc((22, TTK), BF16)
                histF = None
                if not prompt:
                    histF = alloc((44, NS * 2), F32)
                    for piece in range(4):
                        mm = mark()
                        nat = alloc(1408, F32)
                        DMA("sp", nat[0:NS * 2, :], dr["s_fconv"][l, :, piece * 1408:(piece + 1) * 1408])
                        for k in range(11):
                            cc = piece * 11 + k
                            ps = psum(1)
                            TR(ps[:, 0:NS * 2], nat[0:NS * 2, k * 128:(k + 1) * 128], identf[0:NS * 2, 0:NS * 2])
                            CPX(histF[:, cc, :], ps[:, 0:NS * 2])
                        release(mm)

                cur_ring[0] = make_ring(40 * 1024)

                def ffn_conv(ps, cc, silu):
                    xp = cur_ring[0]((nseq, 2 + Lq), F32)
                    acc = cur_ring[0]((nseq, Lq), F32)
                    if prompt:
                        CP("act", xp[:, 0, 0:2], S["tl_f"][:, cc, :])
                    else:
                        CP("act", xp[:, :, 0:2], histF[:, cc, :].r("p (b j) -> p b j", j=2))
                    CP("act", xp[:, :, 2:2 + Lq], ps.r("p (b t) -> p b t", b=nseq))
                    ACT(acc, ps.r("p (b t) -> p b t", b=nseq), AF.Identity, scale=cwf[:, cc, 2:3], bias=cbf[:, cc:cc + 1])
                    for j in (1, 0):
                        STT(acc, xp[:, :, j:j + Lq], cwf[:, cc, j:j + 1], acc, ALU.mult, ALU.add)
                    if prompt:
                        CP("act", S["tl_f"][:, cc, :], xp[:, 0, Lq:Lq + 2])
                    if silu:
                        sg = cur_ring[0]((nseq, Lq), BF16)
                        ACT(sg, acc, AF.Silu)
                        return sg
                    return acc

                for bb in range(6):
                    nj = 4 if bb < 5 else 2
                    n = nj * 128
                    wb = nextw()
                    for jj in range(nj):
                        j = bb * 4 + jj
                        pg_ = psum(1); pv_ = psum(1)
                        for kc in range(8):
                            MM(pg_[:, 0:TTK], wb[:, kc, jj * 128:(jj + 1) * 128], x1T[:, kc, :], kc == 0, kc == 7)
                        for kc in range(8):
                            MM(pv_[:, 0:TTK], wb[:, kc, n + jj * 128:n + (jj + 1) * 128], x1T[:, kc, :], kc == 0, kc == 7)
                        sg = ffn_conv(pg_[:, 0:TTK], j, True)
                        av = ffn_conv(pv_[:, 0:TTK], 22 + j, False)
                        TT("dve", hT[:, j, :].r("p (b t) -> p b t", b=nseq), sg, av, ALU.mult)
                    if want_rows:
                        emit_rows(x1T, wb, 0, n, dr["fconv_p"], dr["fconv_s"], 2, bb * 512)
                        emit_rows(x1T, wb, n, n, dr["fconv_p"], dr["fconv_s"], 2, DFF + bb * 512)
                l2g = prow("l2g", Q); l2b = prow("l2b", Q)
                for cb0 in range(2):
                    pss = [psum(1) for _ in range(NCH)]
                    for kh in range(2):
                        wb = nextw()
                        for c in range(NCH):
                            for k in range(11):
                                kc = kh * 11 + k
                                MM(pss[c][0:Q, :], hT[:, kc, c * Q:(c + 1) * Q], wb[:, k, :], kc == 0, kc == 21)
                    if cb0 == 0:
                        keep = []
                        for c in range(NCH):
                            t = alloc(512, F32)
                            CPX(t[0:Q], pss[c][0:Q, :])
                            keep.append(t[0:Q])
                    else:
                        for c in range(NCH):
                            layer_norm(c, [keep[c], pss[c][0:Q, :]], l2g, l2b)
                release(m0b)
                release(m0)

            for l in range(DEPTH):
                if l >= KL:
                    continue
                base_m = mark()
                try:
                    layer_body(l)
                except _Cut:
                    pass
                release(base_m)
                gctr[0] = (tiles.index((kind, ti)) * DEPTH + l + 1) * NWB
                wstate["issued"] = max(wstate["issued"], gctr[0])
            if prompt:
                DMA("sp", dr["y_p"][ti * 512:(ti + 1) * 512, :].rearrange("(c p) d -> p c d", p=128), x_tok)
            else:
                DMA("sp", dr["y_s"], x_tok[0:Q, 0, :])

        print("final peak", astate.get("peak"))
        P.emit()
        build_program.stats = P.stats
    return nc


_NC_CACHE = {}


def _get_nc():
    if "nc" not in _NC_CACHE:
        _NC_CACHE["nc"] = build_program()
    return _NC_CACHE["nc"]


def kernel(x_prompt, x_sample, state_ssd, state_ssd_conv, state_mlstm_c, state_mlstm_n, state_mlstm_m,
           state_ffn_conv, w_in, ssd_conv_w, ssd_conv_b, ssd_dt_bias, ssd_a_log, ssd_d, ssd_norm_w,
           mlstm_gate_b, mlstm_norm_w, w_branch_a, w_branch_b, w_out, ln1_g, ln1_b, ffn_w_up, ffn_conv_w,
           ffn_conv_b, ffn_w_down, ln2_g, ln2_b):
    f = lambda a: np.ascontiguousarray(np.asarray(a, dtype=np.float32))
    nc = _get_nc()
    prow = np.zeros((DEPTH, NPR), np.float32)
    for name, arr in (("dtb", ssd_dt_bias), ("alog", ssd_a_log), ("D", ssd_d), ("gb", mlstm_gate_b),
                      ("snw", ssd_norm_w), ("mnw", mlstm_norm_w), ("l1g", ln1_g), ("l1b", ln1_b),
                      ("l2g", ln2_g), ("l2b", ln2_b)):
        o, n = PR[name]
        prow[:, o:o + n] = f(arr)
    cw_s = f(ssd_conv_w).reshape(DEPTH, 4, 10, 128).transpose(0, 3, 2, 1).reshape(DEPTH, 128, 40)
    cb_s = f(ssd_conv_b).reshape(DEPTH, 10, 128).transpose(0, 2, 1)
    cw_f = f(ffn_conv_w).reshape(DEPTH, 3, 44, 128).transpose(0, 3, 2, 1).reshape(DEPTH, 128, 132)
    cb_f = f(ffn_conv_b).reshape(DEPTH, 44, 128).transpose(0, 2, 1)
    consts = _make_consts()
    shared = {"w_in": f(w_in), "w_a": f(w_branch_a), "w_b": f(w_branch_b), "w_out": f(w_out), "w_up": f(ffn_w_up),
              "w_down": f(ffn_w_down), "prow": prow, "cw_s": f(cw_s), "cb_s": f(cb_s), "cw_f": f(cw_f),
              "cb_f": f(cb_f), "consts": consts}
    xp = f(x_prompt); xs = f(x_sample)
    s_ssd = f(state_ssd); s_sc = f(state_ssd_conv); s_c = f(state_mlstm_c); s_n = f(state_mlstm_n)
    s_m = f(state_mlstm_m); s_fc = f(state_ffn_conv)
    in_maps = []
    for i in range(NCORES):
        sl = slice(i * NS, (i + 1) * NS)
        m = dict(shared)
        m["xp"] = xp[i]
        m["xs"] = np.ascontiguousarray(xs[sl].reshape(NS * SL, D))
        m["s_ssd"] = np.ascontiguousarray(s_ssd[:, sl].reshape(DEPTH, NS, 1024, 64))
        m["s_sconv"] = np.ascontiguousarray(s_sc[:, sl].reshape(DEPTH, NS * 3, 1280))
        m["s_c"] = np.ascontiguousarray(s_c[:, sl])
        m["s_n"] = np.ascontiguousarray(s_n[:, sl].reshape(DEPTH, NS * 4, 256))
        m["s_m"] = np.ascontiguousarray(s_m[:, sl])
        m["s_fconv"] = np.ascontiguousarray(s_fc[:, sl].reshape(DEPTH, NS * 2, 2 * DFF))
        in_maps.append(m)
    res = run_bass_kernel_spmd(nc, in_maps, core_ids=list(range(NCORES)))
    R = res.results
    cat0 = lambda k: np.stack([R[i][k] for i in range(NCORES)], 0)
    y_p = cat0("y_p")
    y_s = cat0("y_s").reshape(NCORES * NS, SL, D)
    ssd_p = np.stack([R[i]["ssd_p"] for i in range(NCORES)], 1).reshape(DEPTH, NCORES, 16, 64, 64)
    ssd_s = np.concatenate([R[i]["ssd_s"] for i in range(NCORES)], 1).reshape(DEPTH, NCORES * NS, 16, 64, 64)
    sconv_p = np.stack([R[i]["sconv_p"] for i in range(NCORES)], 1)
    sconv_s = np.concatenate([R[i]["sconv_s"] for i in range(NCORES)], 1)
    c_p = np.stack([R[i]["c_p"] for i in range(NCORES)], 1)
    c_s = np.concatenate([R[i]["c_s"] for i in range(NCORES)], 1)
    n_p = np.stack([R[i]["n_p"] for i in range(NCORES)], 1)
    n_s = np.concatenate([R[i]["n_s"].reshape(DEPTH, NS, 4, 256) for i in range(NCORES)], 1)
    m_p = np.stack([R[i]["m_p"].reshape(DEPTH, 4) for i in range(NCORES)], 1)
    m_s = np.concatenate([R[i]["m_s"] for i in range(NCORES)], 1)
    fconv_p = np.stack([R[i]["fconv_p"] for i in range(NCORES)], 1)
    fconv_s = np.concatenate([R[i]["fconv_s"] for i in range(NCORES)], 1)
    outs = (y_p, y_s, ssd_p, ssd_s, sconv_p, sconv_s, c_p, c_s, n_p, n_s, m_p, m_s, fconv_p, fconv_s)
    return tuple(np.ascontiguousarray(o, dtype=np.float32) for o in outs)
```

```python
import contextlib
import os
import numpy as np
import concourse.bass as bass
import concourse.mybir as mybir
from concourse.bass_utils import run_bass_kernel_spmd

F32 = mybir.dt.float32
BF16 = mybir.dt.bfloat16
AF = mybir.ActivationFunctionType
ALU = mybir.AluOpType
AX = mybir.AxisListType

NCORES = 8
D = 1024
SEQ = 2048
DEPTH = 2
NS = 16
SL = 4
NIN = 8472
DFF = 2816
ALPHA = (2 * DEPTH) ** 0.25
EPS = 1e-5
NEGV = -30000.0
ENGS = ("pe", "act", "dve", "pool", "sp")


class _Cut(Exception):
    pass


KT = os.environ.get('KT', '')
KL = int(os.environ.get('KL', '2'))
KCUT = os.environ.get('KCUT', '')


def cut(name):
    if KCUT == name:
        raise _Cut()


class Op:
    __slots__ = ("eng", "fn", "deps", "dma", "sem", "val", "waits", "snap", "needed")

    def __init__(self, eng, fn, dma):
        self.eng = eng
        self.fn = fn
        self.deps = set()
        self.dma = dma
        self.sem = None
        self.val = 0
        self.waits = []
        self.snap = None
        self.needed = False


class Buf:
    __slots__ = ("ap", "toks", "meta")

    def __init__(self, ap, toks, meta=None):
        self.ap = ap
        self.toks = toks
        self.meta = meta

    def __getitem__(self, k):
        return Buf(self.ap[k], self.toks, self.meta)

    def r(self, pat, **kw):
        return Buf(self.ap.rearrange(pat, **kw), self.toks, self.meta)

    def us(self, axis):
        return Buf(self.ap.unsqueeze(axis), self.toks, self.meta)

    def bc(self, shape):
        return Buf(self.ap.to_broadcast(list(shape)), self.toks, self.meta)

    def bitcast(self, dt):
        return Buf(self.ap.bitcast(dt), self.toks, self.meta)


def _toks(lst):
    out = []
    for b in lst:
        if b is None or isinstance(b, (int, float)):
            continue
        if isinstance(b, Buf):
            out.extend(b.toks)
        else:
            out.append(b)
    return out


def _ap(x):
    return x.ap if isinstance(x, Buf) else x


class Prog:
    def __init__(self, nc, n_dma_sems=12):
        self.nc = nc
        self.ops = {e: [] for e in ENGS}
        self.order = []
        self.last_write = {}
        self.readers = {}
        self.n_dma_sems = n_dma_sems
        self.checker = None

    def rec(self, eng, fn, reads=(), writes=(), dma=False):
        op = Op(eng, fn, dma)
        if self.checker is not None:
            for b in list(reads) + list(writes):
                if isinstance(b, Buf) and b.meta is not None:
                    self.checker(b)
        rt = _toks(reads)
        wt = _toks(writes)
        if any(isinstance(t, tuple) and t[0] == "ps" for t in rt):
            wt = wt + [t for t in rt if isinstance(t, tuple) and t[0] == "ps"]
            rt = [t for t in rt if not (isinstance(t, tuple) and t[0] == "ps")]
        for t in rt:
            w = self.last_write.get(t)
            if w is not None:
                op.deps.add(w)
        for t in wt:
            w = self.last_write.get(t)
            if w is not None:
                op.deps.add(w)
            for r in self.readers.get(t, ()):
                op.deps.add(r)
        for t in rt:
            lst = self.readers.setdefault(t, [])
            if not dma:
                for i, r in enumerate(lst):
                    if (not r.dma) and r.eng == eng:
                        lst[i] = op
                        break
                else:
                    lst.append(op)
            else:
                lst.append(op)
        for t in wt:
            self.last_write[t] = op
            self.readers[t] = []
        op.deps.discard(op)
        self.ops[eng].append(op)
        self.order.append(op)
        return op

    def emit(self):
        nc = self.nc
        for op in self.order:
            if op.eng == "pe" and not op.dma:
                op.deps = {d for d in op.deps if not (d.eng == "pe" and not d.dma)}
        dma_cnt = {}
        dma_last = {}
        for op in self.order:
            if op.dma:
                k = dma_cnt.get(op.eng, 0)
                dma_cnt[op.eng] = k + 1
                slot = (op.eng, k % self.n_dma_sems)
                prev = dma_last.get(slot)
                if prev is not None:
                    op.deps.add(prev)
                dma_last[slot] = op
                op.sem = slot
                op.val = 16 * (k // self.n_dma_sems + 1)
        for op in self.order:
            for d in op.deps:
                d.needed = True
        final_ops = list(dma_last.values())
        cnt = {e: 0 for e in ENGS}
        for op in self.order:
            if not op.dma and op.needed:
                cnt[op.eng] += 1
                op.sem = ("c", op.eng)
                op.val = cnt[op.eng]
        know = {e: {} for e in ENGS}
        for op in self.order:
            kn = know[op.eng]
            wd = {}
            for d in sorted(op.deps, key=lambda d: -d.val):
                if d.sem is None:
                    continue
                if kn.get(d.sem, 0) >= d.val:
                    continue
                if wd.get(d.sem, 0) < d.val:
                    wd[d.sem] = d.val
                for s, v in d.snap.items():
                    if kn.get(s, 0) < v:
                        kn[s] = v
            op.waits = list(wd.items())
            if op.sem is not None:
                sn = dict(kn)
                sn[op.sem] = op.val
                op.snap = sn
        sem_keys = []
        seen = set()
        for op in self.order:
            if op.sem is not None and op.sem not in seen:
                seen.add(op.sem)
                sem_keys.append(op.sem)
        self.stats = {e: len(self.ops[e]) for e in ENGS}
        self.stats["sems"] = len(sem_keys)
        self.stats["waits"] = sum(len(op.waits) for op in self.order)
        self.stats["cnt"] = dict(cnt)
        self.stats["dma"] = dict(dma_cnt)
        sems = {}
        with contextlib.ExitStack() as st:
            for i, k in enumerate(sem_keys):
                sems[k] = st.enter_context(nc.semaphore("s%d" % i))
            block = st.enter_context(nc.Block())
            fin = [(sems[d.sem], d.val) for d in final_ops]
            ops = self.ops

            def run(eng_name, eng):
                for op in ops[eng_name]:
                    for s, v in op.waits:
                        eng.wait_ge(sems[s], v)
                    ins = op.fn(eng)
                    if op.sem is not None:
                        ins.then_inc(sems[op.sem], 16 if op.dma else 1)
                if eng_name == "sp":
                    for s, v in fin:
                        eng.wait_ge(s, v)

            @block.tensor
            def _(e):
                run("pe", e)

            @block.scalar
            def _(e):
                run("act", e)

            @block.vector
            def _(e):
                run("dve", e)

            @block.gpsimd
            def _(e):
                run("pool", e)

            @block.sync
            def _(e):
                run("sp", e)


def _const_layout():
    lay = {}
    off = 0

    def add(name, n):
        nonlocal off
        lay[name] = (off, n)
        off += n

    add("ident", 128)
    add("ones", 128)
    for sfx, q in (("p", 128), ("s", 64)):
        add("L" + sfx, q)
        add("U" + sfx, q)
        add("SELLAST" + sfx, q)
        add("NEG" + sfx, 512)
        add("NEGT" + sfx, 4 * q)
    add("SELROWp", 1)
    add("SELROWs", NS)
    add("bms", NS)
    add("bmTs", 64)
    return lay, off


CL, NCONST = _const_layout()


def _make_consts():
    c = np.zeros((128, NCONST), np.float32)

    def put(name, arr):
        o, n = CL[name]
        a = np.asarray(arr, np.float32)
        c[: a.shape[0], o:o + a.shape[1]] = a

    put("ident", np.eye(128))
    put("ones", np.ones((128, 128)))
    for sfx, q, sl in (("p", 128, 128), ("s", 64, SL)):
        i = np.arange(q)
        seq = i // sl
        same = seq[:, None] == seq[None, :]
        put("L" + sfx, same & (i[:, None] <= i[None, :]))
        put("U" + sfx, same & (i[:, None] > i[None, :]))
        last = seq * sl + sl - 1
        put("SELLAST" + sfx, i[:, None] == last[None, :])
        neg = np.where(same & (i[None, :] >= i[:, None]), 0.0, NEGV)
        put("NEG" + sfx, np.tile(neg, (1, 512 // q)))
        negt = np.where(same & (i[None, :] <= i[:, None]), 0.0, NEGV)
        put("NEGT" + sfx, np.tile(negt, (1, 4)))
    sr = np.zeros((128, 1))
    sr[127, 0] = 1
    put("SELROWp", sr)
    i = np.arange(64)
    put("SELROWs", (i[:, None] == (np.arange(NS) * SL + SL - 1)[None, :]))
    bm = (i[:, None] // SL) == np.arange(NS)[None, :]
    put("bms", bm)
    put("bmTs", bm.T)
    return c


PR = {"dtb": (0, 16), "alog": (16, 16), "D": (32, 16), "gb": (48, 8), "snw": (64, 1024), "mnw": (1088, 1024),
      "l1g": (2112, 1024), "l1b": (3136, 1024), "l2g": (4160, 1024), "l2b": (5184, 1024)}
NPR = 6208

def _wblocks():
    blks = [("w_in", 0, 8, [(0, 1024)]), ("w_in", 0, 8, [(1024, 1024)]), ("w_in", 0, 8, [(2048, 272)]),
            ("w_in", 0, 8, [(2320, 1024)]), ("w_in", 0, 8, [(3344, 1024)]), ("w_in", 0, 8, [(4368, 1032)]),
            ("w_in", 0, 8, [(5400, 1024)]), ("w_a", 0, 8, [(0, 1024)]), ("w_in", 0, 8, [(6424, 1024)]),
            ("w_b", 0, 8, [(0, 1024)]), ("w_in", 0, 8, [(7448, 1024)]), ("w_out", 0, 8, [(0, 1024)])]
    for bb in range(6):
        n = 512 if bb < 5 else 256
        blks.append(("w_up", 0, 8, [(bb * 512, n), (DFF + bb * 512, n)]))
    for cb in range(2):
        for kh in range(2):
            blks.append(("w_down", kh * 11 * 128, 11, [(cb * 512, 512)]))
    return blks


WBLK = _wblocks()
NWB = len(WBLK)
WCAP = 8 * 1032
NWBUF = 2
USE_WSC = True


def build_program():
    nc = bass.Bass("TRN2", target_bir_lowering=False)
    dr = {}

    def din(name, shape):
        dr[name] = nc.dram_tensor(name, list(shape), F32, kind="ExternalInput").ap()
        return dr[name]

    def dout(name, shape):
        dr[name] = nc.dram_tensor(name, list(shape), F32, kind="ExternalOutput").ap()
        return dr[name]

    din("xp", (SEQ, D)); din("xs", (NS * SL, D))
    din("s_ssd", (DEPTH, NS, 1024, 64)); din("s_sconv", (DEPTH, NS * 3, 1280))
    din("s_c", (DEPTH, NS, 4, 256, 256)); din("s_n", (DEPTH, NS * 4, 256)); din("s_m", (DEPTH, NS, 4))
    din("s_fconv", (DEPTH, NS * 2, 2 * DFF))
    din("w_in", (DEPTH, D, NIN)); din("w_a", (DEPTH, D, D)); din("w_b", (DEPTH, D, D)); din("w_out", (DEPTH, D, D))
    din("w_up", (DEPTH, D, 2 * DFF)); din("w_down", (DEPTH, DFF, D))
    din("prow", (DEPTH, NPR)); din("cw_s", (DEPTH, 128, 40)); din("cb_s", (DEPTH, 128, 10))
    din("cw_f", (DEPTH, 128, 132)); din("cb_f", (DEPTH, 128, 44)); din("consts", (128, NCONST))
    dout("y_p", (SEQ, D)); dout("y_s", (NS * SL, D))
    dout("ssd_p", (DEPTH, 1024, 64)); dout("ssd_s", (DEPTH, NS, 1024, 64))
    dout("sconv_p", (DEPTH, 3, 1280)); dout("sconv_s", (DEPTH, NS, 3, 1280))
    dout("c_p", (DEPTH, 4, 256, 256)); dout("c_s", (DEPTH, NS, 4, 256, 256))
    dout("n_p", (DEPTH, 4, 256)); dout("n_s", (DEPTH, NS * 4, 256))
    dout("m_p", (DEPTH, 1, 4)); dout("m_s", (DEPTH, NS, 4))
    dout("fconv_p", (DEPTH, 2, 2 * DFF)); dout("fconv_s", (DEPTH, NS, 2, 2 * DFF))

    wsc = nc.dram_tensor("wsc", [DEPTH * NWB, 128, WCAP], BF16, kind="Internal").ap()
    st = contextlib.ExitStack()
    with st:
        P = Prog(nc)
        ARENA_BYTES = 211456
        arena_t = st.enter_context(nc.sbuf_tensor("arena", [128, ARENA_BYTES // 4], F32))
        PS_t = st.enter_context(nc.psum_tensor("PS", [128, 4096], F32))
        astate = {"off": 0}

        def alloc(free, dt=F32):
            if isinstance(free, int):
                free = (free,)
            n = int(np.prod(free))
            nb = n * (4 if dt == F32 else 2)
            nb = (nb + 127) // 128 * 128
            o = astate["off"]
            astate["off"] = o + nb
            astate["peak"] = max(astate.get("peak", 0), astate["off"])
            assert astate["off"] <= ARENA_BYTES, ("arena overflow", astate["off"])
            base = arena_t[:, o // 4:(o + nb) // 4]
            ap = base if dt == F32 else base.bitcast(BF16)
            ap = ap[:, 0:n]
            if len(free) == 2:
                ap = ap.rearrange("p (a b) -> p a b", a=free[0])
            elif len(free) == 3:
                ap = ap.rearrange("p (a b c) -> p a b c", a=free[0], b=free[1])
            elif len(free) == 4:
                ap = ap.rearrange("p (a b c d) -> p a b c d", a=free[0], b=free[1], c=free[2])
            toks = [("sb", k) for k in range(o // 128, (o + nb) // 128)]
            return Buf(ap, toks)

        def mark():
            return astate["off"]

        def release(m):
            astate["off"] = m

        ps_use = [0] * 8
        ps_clock = [0]
        ps_gen = [0] * 8

        ring_gen = {}

        def ps_check(b):
            for kind, idx, g in b.meta:
                if kind == "ps":
                    assert ps_gen[idx] == g, ("stale PSUM buffer used", idx, g, ps_gen[idx])
                else:
                    assert ring_gen.get(idx, 0) == g, ("stale ring buffer used", idx, g, ring_gen.get(idx, 0))

        def make_ring(nbytes):
            base = (astate["off"] + 511) // 512 * 512
            nbytes = nbytes // 512 * 512
            astate["off"] = base + nbytes
            astate["peak"] = max(astate.get("peak", 0), astate["off"])
            assert astate["off"] <= ARENA_BYTES, ("arena overflow (ring)", astate["off"])
            assert base % 512 == 0 or True
            st_ = {"o": 0}

            def ralloc(free, dt=F32):
                if isinstance(free, int):
                    free = (free,)
                n = int(np.prod(free))
                nb = n * (4 if dt == F32 else 2)
                nb = (nb + 511) // 512 * 512
                assert nb <= nbytes
                if st_["o"] + nb > nbytes:
                    st_["o"] = 0
                o = base + st_["o"]
                st_["o"] += nb
                o4 = (o + 3) // 4 * 4
                bs_ = arena_t[:, o4 // 4:(o4 + nb) // 4 if o4 + nb <= ARENA_BYTES else ARENA_BYTES // 4]
                ap = bs_ if dt == F32 else bs_.bitcast(BF16)
                ap = ap[:, 0:n]
                if len(free) == 2:
                    ap = ap.rearrange("p (a b) -> p a b", a=free[0])
                elif len(free) == 3:
                    ap = ap.rearrange("p (a b c) -> p a b c", a=free[0], b=free[1])
                slots = list(range(o // 512, (o + nb) // 512))
                meta = []
                for k in slots:
                    ring_gen[k] = ring_gen.get(k, 0) + 1
                    meta.append(("ring", k, ring_gen[k]))
                return Buf(ap, [("sb", k) for k in range(o // 128, (o + nb) // 128)], tuple(meta))

            return ralloc

        P.checker = ps_check

        ps_allowed = [list(range(8))]

        def run_streams(streams):
            alive = list(streams)
            while alive:
                for item in list(alive):
                    g, banks = item
                    ps_allowed[0] = banks
                    try:
                        next(g)
                    except StopIteration:
                        alive.remove(item)
            ps_allowed[0] = list(range(8))

        def psum(nb=1):
            best, bs = None, None
            for s in range(0, 8, nb):
                if any(b not in ps_allowed[0] for b in range(s, s + nb)):
                    continue
                sc = max(ps_use[s:s + nb])
                if best is None or sc < best:
                    best, bs = sc, s
            ps_clock[0] += 1
            for b in range(bs, bs + nb):
                ps_use[b] = ps_clock[0]
                ps_gen[b] += 1
            return Buf(PS_t[:, bs * 512:(bs + nb) * 512], [("ps", b) for b in range(bs, bs + nb)],
                       tuple(("ps", b, ps_gen[b]) for b in range(bs, bs + nb)))

        def MM(out, lhsT, rhs, start=True, stop=True):
            P.rec("pe", lambda e: e.matmul(out.ap, lhsT.ap, rhs.ap, start=start, stop=stop),
                  reads=[lhsT, rhs], writes=[out])

        def TR(out, in_, ident):
            P.rec("pe", lambda e: e.transpose(out.ap, in_.ap, ident.ap), reads=[in_, ident], writes=[out])

        def ACT(out, in_, func, scale=1.0, bias=None, accum=None):
            kw = {}
            if bias is not None:
                kw["bias"] = _ap(bias)
            if accum is not None:
                kw["accum_out"] = accum.ap
            sc = _ap(scale)
            P.rec("act", lambda e: e.activation(out=out.ap, in_=in_.ap, func=func, scale=sc, **kw),
                  reads=[in_, scale, bias], writes=[out, accum])

        def TT(eng, out, in0, in1, op):
            P.rec(eng, lambda e: e.tensor_tensor(out=out.ap, in0=in0.ap, in1=in1.ap, op=op),
                  reads=[in0, in1], writes=[out])

        def TS(eng, out, in0, s1, s2=None, op0=ALU.mult, op1=None):
            a1, a2 = _ap(s1), _ap(s2)
            if op1 is None:
                P.rec(eng, lambda e: e.tensor_scalar(out=out.ap, in0=in0.ap, scalar1=a1, scalar2=None, op0=op0),
                      reads=[in0, s1], writes=[out])
            else:
                P.rec(eng, lambda e: e.tensor_scalar(out=out.ap, in0=in0.ap, scalar1=a1, scalar2=a2, op0=op0, op1=op1),
                      reads=[in0, s1, s2], writes=[out])

        def STT(out, in0, scalar, in1, op0, op1):
            sc = _ap(scalar)
            P.rec("dve", lambda e: e.scalar_tensor_tensor(out=out.ap, in0=in0.ap, scalar=sc, in1=in1.ap, op0=op0, op1=op1),
                  reads=[in0, scalar, in1], writes=[out])

        def CP(eng, out, in_):
            if eng == "act":
                ACT(out, in_, AF.Copy)
            else:
                P.rec(eng, lambda e: e.tensor_copy(out=out.ap, in_=in_.ap), reads=[in_], writes=[out])

        def MEMSET(eng, out, v):
            P.rec(eng, lambda e: e.memset(out.ap, v), writes=[out])

        def DMA(q, out, in_, reads=(), writes=()):
            oa, ia = _ap(out), _ap(in_)
            P.rec(q, lambda e: e.dma_start(out=oa, in_=ia), reads=list(reads) + ([in_] if isinstance(in_, Buf) else []),
                  writes=list(writes) + ([out] if isinstance(out, Buf) else []), dma=True)

        def RED(out, in_, op):
            P.rec("dve", lambda e: e.tensor_reduce(out=out.ap, in_=in_.ap, axis=AX.X, op=op), reads=[in_], writes=[out])

        cp_rr = [0]

        def CPX(out, in_):
            cp_rr[0] ^= 1
            CP("act" if cp_rr[0] else "dve", out, in_)

        cf = alloc(NCONST, F32)
        cb_ = alloc(NCONST, BF16)
        DMA("sp", cf, dr["consts"])
        DMA("pool", cb_, dr["consts"])

        def CF(name, rows=128, sub=None):
            o, n = CL[name]
            if sub is not None:
                o, n = o + sub[0], sub[1]
            return cf[0:rows, o:o + n]

        def CB(name, rows=128, sub=None):
            o, n = CL[name]
            if sub is not None:
                o, n = o + sub[0], sub[1]
            return cb_[0:rows, o:o + n]

        mhalf = alloc(1, F32)
        MEMSET("pool", mhalf, -0.5)
        wbufs = [alloc(WCAP, BF16) for _ in range(NWBUF)]
        x_tok = alloc((4, D), F32)
        prm_small = alloc(64, F32)
        cws = alloc((10, 4), F32); cbs = alloc(10, F32); cwf = alloc((44, 3), F32); cbf = alloc(44, F32)
        a_bc = alloc(16, F32)
        pst = []
        for l in range(DEPTH):
            s = {"hT_f": alloc(512, F32), "hT_b": alloc(512, BF16), "C_f": alloc((2, 4, 256), F32),
                 "C_b": alloc((2, 4, 256), BF16), "nT_f": alloc((2, 4, 1), F32), "nT_b": alloc((2, 4, 1), BF16),
                 "m_st": alloc(4, F32), "tl_s": alloc((10, 3), F32), "tl_f": alloc((44, 2), F32)}
            for k in ("hT_f", "hT_b", "C_f", "C_b", "nT_f", "nT_b", "m_st", "tl_s", "tl_f"):
                MEMSET("pool", s[k], 0.0)
            pst.append(s)

        wstate = {"issued": 0, "gs": None, "lst": None}

        def wbuf_of(g):
            if wstate["gs"] is not None and g >= wstate["gs"]:
                lst = wstate["lst"]
                return lst[(g - wstate["gs"]) % len(lst)]
            return wbufs[g % NWBUF]

        def wdepth(g):
            if wstate["gs"] is not None and g >= wstate["gs"]:
                return len(wstate["lst"]) - 1
            return 1

        def wissue(g):
            l = (g // NWB) % DEPTH
            name, row0, nk, segs = WBLK[g % NWB]
            wb = wbuf_of(g)
            tot = sum(n for _, n in segs)
            view = wb[:, 0:nk * tot].r("p (k n) -> p k n", k=nk)
            lk = g % (DEPTH * NWB)
            if g >= DEPTH * NWB and USE_WSC:
                DMA("pool", wb[:, 0:nk * tot], wsc[lk, :, 0:nk * tot], reads=[("dram", lk)])
                return
            src = dr[name][l]
            o = 0
            for c0, n in segs:
                sap = src[row0:row0 + nk * 128, c0:c0 + n].rearrange("(k p) n -> p k n", p=128)
                DMA("pool", view[:, :, o:o + n], sap)
                o += n
            if USE_WSC:
                DMA("sp", wsc[lk, :, 0:nk * tot], wb[:, 0:nk * tot], writes=[("dram", lk)])

        def wget(g, total, prefetch=True):
            while wstate["issued"] <= min(g + (wdepth(g) if prefetch else 0), total - 1):
                nx = wstate["issued"]
                if nx >= SAMPLE_G0 and wstate["gs"] is None and nx > g:
                    break
                wissue(nx)
                wstate["issued"] += 1
            name, row0, nk, segs = WBLK[g % NWB]
            tot = sum(n for _, n in segs)
            return wbuf_of(g)[:, 0:nk * tot].r("p (k n) -> p k n", k=nk)

        tiles = [("p", i) for i in range(4)] + [("s", 0)]
        if KT:
            tiles = [(t[0], int(t[1:] or 0)) for t in KT.split(",")]
        total_blocks = len(tiles) * DEPTH * NWB
        SAMPLE_G0 = 10 ** 9
        for _i, (_k, _t) in enumerate(tiles):
            if _k == "s":
                SAMPLE_G0 = _i * DEPTH * NWB
        gctr = [0]

        def nextw(prefetch=True):
            g = gctr[0]
            gctr[0] += 1
            return wget(g, total_blocks, prefetch)

        identf = CF("ident")
        identb = CB("ident")
        onesf = CF("ones")
        onesb = CB("ones")

        def rsqrt_(out, in_, scale, rows, n):
            TS("dve", out, in_, scale, EPS, ALU.mult, ALU.add)
            TT("pool", out, out, mhalf[0:rows, 0:1].bc([rows, n]), ALU.pow)

        for kind, ti in tiles:
            prompt = kind == "p"
            Q = 128 if prompt else 64
            NCH = 4 if prompt else 1
            TTK = Q * NCH
            nseq = 1 if prompt else NS
            Lq = TTK if prompt else SL
            sfx = "p" if prompt else "s"
            Lm = CF("L" + sfx, Q); Um = CF("U" + sfx, Q); SELLAST = CF("SELLAST" + sfx, Q)
            NEGb = CB("NEG" + sfx, Q); NEGTb = CB("NEGT" + sfx, Q)
            SELROW = CF("SELROW" + sfx, Q)
            if prompt:
                bm = onesf[0:Q, 0:1]; bmb = onesb[0:Q, 0:1]; bmT = onesf[0:1, 0:Q]
            else:
                bm = CF("bms", Q); bmb = CB("bms", Q); bmT = CF("bmTs", NS)
            idq_f = identf[0:Q, 0:Q]; idq_b = identb[0:Q, 0:Q]
            last_tile = prompt and ti == 3
            want_rows = last_tile or not prompt

            if not prompt and tiles.index((kind, ti)) * DEPTH * NWB == SAMPLE_G0:
                extra = [alloc(WCAP, BF16)]
                wstate["gs"] = max(SAMPLE_G0, wstate["issued"])
                wstate["lst"] = extra + [wbufs[(wstate["gs"] + k) % NWBUF] for k in range(NWBUF)]
            if prompt:
                DMA("sp", x_tok, dr["xp"][ti * 512:(ti + 1) * 512, :].rearrange("(c p) d -> p c d", p=128))
            else:
                DMA("sp", x_tok[0:Q, 0, :], dr["xs"])

            def layer_body(l):
                S = pst[l]
                m0 = mark()
                DMA("sp", prm_small, dr["prow"][l, 0:64].partition_broadcast(128))
                DMA("sp", cws, dr["cw_s"][l].rearrange("p (a b) -> p a b", a=10))
                DMA("sp", cbs, dr["cb_s"][l])
                DMA("sp", cwf, dr["cw_f"][l].rearrange("p (a b) -> p a b", a=44))
                DMA("sp", cbf, dr["cb_f"][l])
                ACT(a_bc, prm_small[:, 16:32], AF.Exp)
                TS("dve", a_bc, a_bc, -1.0)
                dtb = prm_small[:, 0:16]; Dp = prm_small[:, 32:48]; gbp = prm_small[:, 48:56]

                def prow(name, rows):
                    o, n = PR[name]
                    b = alloc(n, F32)
                    DMA("sp", b, dr["prow"][l, o:o + n].partition_broadcast(128))
                    return b[0:rows, :]

                xT = alloc((8, TTK), BF16)
                yT = alloc((8, TTK), BF16)
                qT = alloc((8, TTK), BF16)
                kT = alloc((8, TTK), BF16)
                k_tok = [alloc(D, BF16) for _ in range(NCH)]
                gi = alloc((NCH, 4), F32)
                logf = alloc((NCH, 4), F32)
                mS = mark()

                def to_featmajor(src_tok, dstT):
                    for c in range(NCH):
                        for half in range(2):
                            ps = psum(1)
                            for j in range(4):
                                kc = half * 4 + j
                                TR(ps[:, j * Q:(j + 1) * Q], src_tok[0:Q, c, kc * 128:(kc + 1) * 128], idq_f)
                            CPX(dstT[:, half * 4:half * 4 + 4, c * Q:(c + 1) * Q],
                                ps[:, 0:4 * Q].r("p (j q) -> p j q", j=4))

                to_featmajor(x_tok, xT)

                def proj_tok(src, c, wb, col0, ncols, evac, rows=None):
                    r0, r1 = (c * Q, (c + 1) * Q) if rows is None else rows
                    M = r1 - r0
                    for cb0 in range(0, ncols, 512):
                        n = min(512, ncols - cb0)
                        ps = psum(1)
                        for kc in range(8):
                            MM(ps[0:M, 0:n], src[:, kc, r0:r1], wb[:, kc, col0 + cb0:col0 + cb0 + n], kc == 0, kc == 7)
                        evac(ps[0:M, 0:n], cb0, n)

                def proj_feat(src, wb, col0, evac, m=128):
                    ps = psum(1)
                    for kc in range(8):
                        MM(ps[0:m, 0:TTK], wb[:, kc, col0:col0 + m], src[:, kc, 0:TTK], kc == 0, kc == 7)
                    evac(ps[0:m, 0:TTK])

                if prompt:
                    row_rng = (TTK - 32, TTK)
                else:
                    row_rng = (0, 64)
                RM = row_rng[1] - row_rng[0]

                def emit_rows(src, wb, col0, ncols, dst_p, dst_s, nrow, dcol0):
                    def ev(ps, cb0, n):
                        t = cur_ring[0](512, F32)
                        CPX(t[0:RM, 0:n], ps)
                        if prompt:
                            DMA("sp", dst_p[l, :, dcol0 + cb0:dcol0 + cb0 + n], t[RM - nrow:RM, 0:n])
                        else:
                            for tt in range(nrow):
                                tok = SL - nrow + tt
                                DMA("sp", dst_s[l, :, tt, dcol0 + cb0:dcol0 + cb0 + n], t[tok:64:SL, 0:n])
                    proj_tok(src, 0, wb, col0, ncols, ev, rows=row_rng)

                cut('P0')
                z_tok = [alloc(D, BF16) for _ in range(NCH)]
                xsT = alloc((8, TTK), BF16)
                BT = alloc(TTK, BF16)
                CT = alloc(TTK, BF16)
                dt_ = alloc((NCH, 16), F32)
                dta = alloc((NCH, 16), F32)
                wb = nextw()
                for c in range(NCH):
                    proj_tok(xT, c, wb, 0, 1024,
                             lambda ps, cb0, n, c=c: ACT(z_tok[c][0:Q, cb0:cb0 + n], ps, AF.Silu))
                histS = None
                if not prompt:
                    histS = alloc((10, NS * 3), F32)
                    mm = mark()
                    nat = alloc(1280, F32)
                    DMA("sp", nat[0:NS * 3, :], dr["s_sconv"][l])
                    for cc in range(10):
                        ps = psum(1)
                        TR(ps[:, 0:NS * 3], nat[0:NS * 3, cc * 128:(cc + 1) * 128], identf[0:NS * 3, 0:NS * 3])
                        CPX(histS[:, cc, :], ps[:, 0:NS * 3])
                    release(mm)

                mR = mark()
                cur_ring = [make_ring(20 * 1024)]

                def conv_chunk(ps, cc):
                    xp = cur_ring[0]((nseq, 3 + Lq), F32)
                    acc = cur_ring[0]((nseq, Lq), F32)
                    if prompt:
                        CP("act", xp[:, 0, 0:3], S["tl_s"][:, cc, :])
                    else:
                        CP("act", xp[:, :, 0:3], histS[:, cc, :].r("p (b j) -> p b j", j=3))
                    CP("act", xp[:, :, 3:3 + Lq], ps.r("p (b t) -> p b t", b=nseq))
                    ACT(acc, ps.r("p (b t) -> p b t", b=nseq), AF.Identity, scale=cws[:, cc, 3:4], bias=cbs[:, cc:cc + 1])
                    for j in (2, 1, 0):
                        STT(acc, xp[:, :, j:j + Lq], cws[:, cc, j:j + 1], acc, ALU.mult, ALU.add)
                    if cc < 8:
                        dst = xsT[:, cc, :]
                    elif cc == 8:
                        dst = BT
                    else:
                        dst = CT
                    ACT(dst.r("p (b t) -> p b t", b=nseq), acc, AF.Silu)
                    if prompt:
                        CP("act", S["tl_s"][:, cc, :], xp[:, 0, Lq:Lq + 3])

                wb = nextw()
                for cc in range(8):
                    proj_feat(xT, wb, cc * 128, lambda ps, cc=cc: conv_chunk(ps, cc))
                if want_rows:
                    emit_rows(xT, wb, 0, 1024, dr["sconv_p"], dr["sconv_s"], 3, 0)
                wb = nextw()
                for cc in (8, 9):
                    proj_feat(xT, wb, (cc - 8) * 128, lambda ps, cc=cc: conv_chunk(ps, cc))
                if want_rows:
                    emit_rows(xT, wb, 0, 256, dr["sconv_p"], dr["sconv_s"], 3, 1024)
                psd = psum(1)
                for c in range(NCH):
                    for kc in range(8):
                        MM(psd[0:Q, c * 16:(c + 1) * 16], xT[:, kc, c * Q:(c + 1) * Q], wb[:, kc, 256:272], kc == 0, kc == 7)
                mm = mark()
                tx = alloc((NCH, 16), F32); ta = alloc((NCH, 16), F32)
                TT("dve", tx[0:Q], psd[0:Q, 0:NCH * 16].r("p (c h) -> p c h", c=NCH), dtb[0:Q].us(1).bc([Q, NCH, 16]), ALU.add)
                STT(ta[0:Q], tx[0:Q], -1.0, tx[0:Q], ALU.mult, ALU.max)
                ACT(ta[0:Q], ta[0:Q], AF.Exp, scale=-1.0)
                ACT(ta[0:Q], ta[0:Q], AF.Ln, bias=1.0)
                STT(dt_[0:Q], tx[0:Q], 0.0, ta[0:Q], ALU.max, ALU.add)
                TT("dve", dta[0:Q], dt_[0:Q], a_bc[0:Q].us(1).bc([Q, NCH, 16]), ALU.mult)
                release(mm)

                release(mR)
                cut('S1')
                snw = prow("snw", Q)
                gx = alloc((NCH, 8), F32); gta = alloc((NCH, 4), F32)
                s2b = []
                for _p in range(2 if NCH > 1 else 1):
                    s2b.append({"xs_tok": alloc(D, BF16), "xdt": alloc(D, BF16), "B_tok": alloc(128, BF16),
                                "GT": alloc((16, Q), BF16), "eac": alloc(32, F32), "cdb": alloc((nseq, 8), F32),
                                "xdtw": alloc(D, BF16)})
                R = alloc((16, Q), F32); expT = alloc((16, Q), BF16); CBs = alloc((2, Q), BF16)
                Xs = None if prompt else alloc((2, NS, 8), F32)
                yo = alloc(D, F32); xsD = alloc(D, F32); ss = alloc(2, F32); y_n = alloc(D, BF16)
                if not prompt:
                    sq = [{"hT_f": alloc(512, F32), "hT_b": alloc(512, BF16),
                           "xw": alloc(D, BF16), "hn": alloc(512, F32)} for _ in range(2)]
                    nat_l = [alloc((8, 64), F32) for _ in range(6)]
                    natn_l = [alloc((8, 64), F32) for _ in range(6)]

                def s2_front(c, B):
                    cols = slice(c * Q, (c + 1) * Q)
                    psx = psum(1).bitcast(BF16)
                    for kc in range(8):
                        TR(psx[0:Q, kc * 128:(kc + 1) * 128], xsT[:, kc, cols], identb)
                    CP("act", B["xs_tok"][0:Q], psx[0:Q, :])
                    TT("dve", B["xdt"][0:Q].r("p (h e) -> p h e", h=16), psx[0:Q, :].r("p (h e) -> p h e", h=16),
                       dt_[0:Q, c, :].us(2).bc([Q, 16, 64]), ALU.mult)
                    psb = psum(1).bitcast(BF16)
                    TR(psb[0:Q, 0:128], BT[:, cols], identb)
                    CP("act", B["B_tok"][0:Q], psb[0:Q, 0:128])
                    TT("pool", R[0:Q], Lm.us(1).bc([Q, 16, Q]), dta[0:Q, c, :].us(2).bc([Q, 16, Q]), ALU.mult)
                    yield
                    nbk = 16 * Q // 512
                    SEG = psum(nbk)
                    Rf = R[0:Q].r("p h l -> p (h l)")
                    for bk in range(nbk):
                        MM(SEG[0:Q, bk * 512:(bk + 1) * 512], Um, Rf[:, bk * 512:(bk + 1) * 512], True, False)
                        MM(SEG[0:Q, bk * 512:(bk + 1) * 512], idq_b, NEGb, False, True)
                    ACT(expT[0:Q].r("p h l -> p (h l)"), SEG[0:Q, 0:16 * Q], AF.Exp)
                    psc = psum(2)
                    for g in range(2):
                        MM(psc[0:Q, g * 512:g * 512 + Q], BT[g * 64:(g + 1) * 64, cols], CT[g * 64:(g + 1) * 64, cols])
                    pt = psum(1)
                    MM(pt[0:Q, 0:16], Lm, dta[0:Q, c, :])
                    MM(pt[0:Q, 16:32], Um, dta[0:Q, c, :])
                    pcd = psum(1)
                    if prompt:
                        for g in range(2):
                            MM(pcd[g * 64:(g + 1) * 64, 0:8], onesf[0:Q, 0:64], dta[0:Q, c, g * 8:(g + 1) * 8])
                    else:
                        for g in range(2):
                            TT("dve", Xs[0:Q, g], bm.us(2).bc([Q, NS, 8]),
                               dta[0:Q, c, g * 8:(g + 1) * 8].us(1).bc([Q, NS, 8]), ALU.mult)
                            MM(pcd[g * 64:(g + 1) * 64, 0:NS * 8], onesf[0:Q, 0:64], Xs[0:Q, g].r("p b h -> p (b h)"))
                    yield
                    CP("dve", CBs[0:Q], psc[0:Q, :].r("p (g x) -> p g x", g=2)[:, :, 0:Q])
                    ACT(B["eac"][0:Q], pt[0:Q, 0:32], AF.Exp)
                    ACT(B["cdb"].r("p b h -> p (b h)"), pcd[:, 0:nseq * 8], AF.Exp)
                    TT("dve", B["GT"][0:Q].r("p (g k) l -> p g k l", g=2), expT[0:Q].r("p (g k) l -> p g k l", g=2),
                       CBs[0:Q].us(2).bc([Q, 2, 8, Q]), ALU.mult)
                    TT("dve", B["xdtw"][0:Q].r("p (h e) -> p h e", h=16), B["xdt"][0:Q].r("p (h e) -> p h e", h=16),
                       B["eac"][0:Q, 16:32].us(2).bc([Q, 16, 64]), ALU.mult)
                    yield

                def s2_back(c, B):
                    cols = slice(c * Q, (c + 1) * Q)
                    eac, cdb, xdtw, xdt, GT, xs_tok, B_tok = (B["eac"], B["cdb"], B["xdtw"], B["xdt"], B["GT"],
                                                             B["xs_tok"], B["B_tok"])
                    def load_nat(b2):
                        nv = nat_l[b2 % 6].r("p (j g) n -> p j g n", g=2)
                        for g in range(2):
                            DMA("sp", nv[:, :, g, :],
                                dr["s_ssd"][l, b2][g * 512:(g + 1) * 512, :].rearrange("(j q) n -> q j n", q=128))

                    if not prompt:
                        for b2 in range(3):
                            load_nat(b2)
                    for b in range(nseq):
                        if prompt:
                            hT_f, hT_b = S["hT_f"], S["hT_b"]
                        else:
                            if b + 3 < nseq:
                                load_nat(b + 3)
                            sb_ = sq[b % 2]
                            nat, hT_f, hT_b = nat_l[b % 6], sb_["hT_f"], sb_["hT_b"]
                            natv = nat.r("p (j g) n -> p j g n", g=2)
                            pn = psum(1)
                            for jj in range(4):
                                TR(pn[:, jj * 128:(jj + 1) * 128], natv[:, jj].r("p g n -> p (g n)"), identf)
                            CP("dve", hT_f, pn)
                            CP("act", hT_b, pn)
                        YO = psum(2)
                        for g in range(2):
                            MM(YO[0:Q, g * 512:(g + 1) * 512], CT[g * 64:(g + 1) * 64, cols], hT_b[g * 64:(g + 1) * 64, :])
                        if prompt:
                            TT("dve", yo[0:Q].r("p (h e) -> p h e", h=16), YO[0:Q, :].r("p (h e) -> p h e", h=16),
                               eac[0:Q, 0:16].us(2).bc([Q, 16, 64]), ALU.mult)
                            xw = xdtw
                        else:
                            if b == 0:
                                TS("dve", yo[0:Q], YO[0:Q, :], bm[:, b:b + 1])
                            else:
                                STT(yo[0:Q], YO[0:Q, :], bm[:, b:b + 1], yo[0:Q], ALU.mult, ALU.add)
                            xw = sb_["xw"]
                            TS("dve", xw[0:Q], xdtw[0:Q], bm[:, b:b + 1])
                        HL = psum(1)
                        for g in range(2):
                            MM(HL[g * 64:(g + 1) * 64, :], B_tok[0:Q, g * 64:(g + 1) * 64], xw[0:Q, g * 512:(g + 1) * 512])
                        hn = hT_f if prompt else sb_["hn"]
                        TT("dve", hn.r("p (h e) -> p h e", h=8), hT_f.r("p (h e) -> p h e", h=8),
                           cdb[:, b, :].us(2).bc([128, 8, 64]), ALU.mult)
                        TT("dve", hn, hn, HL, ALU.add)
                        if prompt:
                            CP("act", hT_b, hn)
                        else:
                            po = psum(1)
                            for jj in range(4):
                                TR(po[:, jj * 128:(jj + 1) * 128], hn[:, jj * 128:(jj + 1) * 128], identf)
                            natn = natn_l[b % 6]
                            CP("act", natn.r("p k n -> p (k n)"), po)
                            natnv = natn.r("p (j g) n -> p j g n", g=2)
                            for g in range(2):
                                DMA("sp", dr["ssd_s"][l, b][g * 512:(g + 1) * 512, :].rearrange("(j q) n -> q j n", q=128),
                                    natnv[:, :, g, :])
                        yield
                    if not prompt:
                        TT("dve", yo[0:Q].r("p (h e) -> p h e", h=16), yo[0:Q].r("p (h e) -> p h e", h=16),
                           eac[0:Q, 0:16].us(2).bc([Q, 16, 64]), ALU.mult)
                    TT("pool", xsD[0:Q].r("p (h e) -> p h e", h=16), xs_tok[0:Q].r("p (h e) -> p h e", h=16),
                       Dp[0:Q].us(2).bc([Q, 16, 64]), ALU.mult)
                    Y = psum(2)
                    for h in range(16):
                        MM(Y[0:Q, h * 64:(h + 1) * 64], GT[0:Q, h, :], xdt[0:Q, h * 64:(h + 1) * 64])
                    TT("dve", yo[0:Q], yo[0:Q], Y[0:Q, :], ALU.add)
                    TT("dve", yo[0:Q], yo[0:Q], xsD[0:Q], ALU.add)
                    TT("dve", yo[0:Q], yo[0:Q], z_tok[c][0:Q], ALU.mult)
                    ACT(xsD[0:Q], yo[0:Q], AF.Square, accum=ss[0:Q, 0:1])
                    rsqrt_(ss[0:Q, 1:2], ss[0:Q, 0:1], 1.0 / 1024, Q, 1)
                    STT(y_n[0:Q], yo[0:Q], ss[0:Q, 1:2], snw, ALU.mult, ALU.mult)
                    yield
                    pyt = psum(1).bitcast(BF16)
                    for kc in range(8):
                        TR(pyt[:, kc * Q:(kc + 1) * Q], y_n[0:Q, kc * 128:(kc + 1) * 128], idq_b)
                    CP("act", yT[:, :, cols], pyt[:, 0:8 * Q].r("p (k q) -> p k q", k=8))
                    yield

                def s2_stream():
                    yield from s2_front(0, s2b[0])
                    for c in range(NCH):
                        if c + 1 < NCH:
                            yield from s2_front(c + 1, s2b[(c + 1) % 2])
                        yield from s2_back(c, s2b[c % 2])
                    if last_tile:
                        po = psum(2)
                        for blk in range(8):
                            g, jj = blk // 4, blk % 4
                            TR(po[:, g * 512 + jj * 64:g * 512 + (jj + 1) * 64], S["hT_f"][g * 64:(g + 1) * 64, jj * 128:(jj + 1) * 128],
                               identf[g * 64:(g + 1) * 64, g * 64:(g + 1) * 64])
                        natn = yo.r("p (k n) -> p k n", k=16)[:, 0:8, :]
                        CP("act", natn.r("p (g j) n -> p g (j n)", g=2), po.r("p (g x) -> p g x", g=2)[:, :, 0:256])
                        DMA("sp", dr["ssd_p"][l].rearrange("(k q) n -> q k n", q=128), natn)

                def m1_stream():
                    wb = nextw()
                    for cc in range(8):
                        proj_feat(xT, wb, cc * 128, lambda ps, cc=cc: CP("act", qT[:, cc, :], ps))
                        yield
                    wb = nextw()
                    for cc in range(8):
                        proj_feat(xT, wb, cc * 128, lambda ps, cc=cc: ACT(kT[:, cc, :], ps, AF.Copy, scale=0.0625))
                        yield
                    for c in range(NCH):
                        for hf in range(2):
                            proj_tok(xT, c, wb, hf * 512, 512,
                                     lambda ps, cb0, n, c=c, hf=hf: ACT(k_tok[c][0:Q, hf * 512:hf * 512 + n], ps, AF.Copy, scale=0.0625))
                            yield
                    wbv[0] = nextw()
                    wb = wbv[0]
                    psg = psum(1)
                    for c in range(NCH):
                        for kc in range(8):
                            MM(psg[0:Q, c * 8:(c + 1) * 8], xT[:, kc, c * Q:(c + 1) * Q], wb[:, kc, 1024:1032], kc == 0, kc == 7)
                    TT("dve", gx[0:Q], psg[0:Q, 0:NCH * 8].r("p (c h) -> p c h", c=NCH), gbp[0:Q].us(1).bc([Q, NCH, 8]), ALU.add)
                    yield
                    CP("dve", gi[0:Q], gx[0:Q, :, 0:4])
                    STT(gta[0:Q], gx[0:Q, :, 4:8], -1.0, gx[0:Q, :, 4:8], ALU.mult, ALU.max)
                    ACT(gta[0:Q], gta[0:Q], AF.Exp, scale=-1.0)
                    ACT(gta[0:Q], gta[0:Q], AF.Ln, bias=1.0)
                    STT(logf[0:Q], gx[0:Q, :, 4:8], 0.0, gta[0:Q], ALU.min, ALU.subtract)
                    yield

                wbv = [None]
                run_streams([(s2_stream(), list(range(6))), (m1_stream(), [6, 7])])
                release(mS)
                cut('M1')
                hmT = alloc((8, TTK), BF16)
                v_tok = [alloc(D, BF16) for _ in range(NCH)]
                o_tok = [alloc(D, BF16) for _ in range(NCH)]
                mnw = prow("mnw", Q)
                TS("dve", mnw, mnw, 0.5)

                mT = alloc((8, TTK), BF16)
                sgA = [alloc(TTK, BF16) for _ in range(2)]
                mM2 = mark()
                if prompt:
                    nT_f, nT_b, m_st = S["nT_f"], S["nT_b"], S["m_st"][0:1, :]
                else:
                    nT_f = alloc((2, 4, NS), F32); nT_b = alloc((2, 4, NS), BF16); m_st = alloc(4, F32)[0:NS, :]
                    mm = mark()
                    nat = alloc(256, F32)
                    DMA("sp", nat[0:64, :], dr["s_n"][l])
                    DMA("sp", m_st, dr["s_m"][l])
                    for dc in range(2):
                        ps = psum(1)
                        TR(ps[:, 0:64], nat[0:64, dc * 128:(dc + 1) * 128], identf[0:64, 0:64])
                        CP("dve", nT_f[:, dc].r("p h b -> p b h"), ps[:, 0:64].r("p (b h) -> p b h", h=4))
                    CP("act", nT_b.r("p a h b -> p (a h b)"), nT_f.r("p a h b -> p (a h b)"))
                    release(mm)
                m2b = []
                for _p in range(2 if NCH > 1 else 1):
                    m2b.append({"sm": alloc(48, F32), "gs": alloc(8, F32), "SwT": alloc((4, Q), BF16), "kw": alloc(D, BF16),
                                "wcb": alloc((nseq, 4), F32), "mprev": alloc(4, F32), "pes": alloc(12, F32)})
                R2 = alloc((4, Q), F32); Wt = alloc((4, Q), BF16); Sw = alloc((4, Q), BF16)
                ws = alloc(4, F32); wcs = alloc(4, F32); Z = alloc((nseq, 4), F32)
                ni = alloc(D, F32); denI = alloc(4, F32); emt = alloc(4, F32); junk = alloc(256, BF16)
                ow = alloc(D, F32); hmn = alloc(D, BF16)
                tmpd = None if prompt else alloc((4, NS), F32)
                if not prompt:
                    cq = [{"C_b": alloc((2, 4, 256), BF16), "kwm": alloc(D, BF16)} for _ in range(2)]
                    Cf_l = [alloc((2, 4, 256), F32) for _ in range(5)]

                def m2_front(c, B):
                    cols = slice(c * Q, (c + 1) * Q)
                    sm, gs, SwT, kw, wcb, mprev, pes = B["sm"], B["gs"], B["SwT"], B["kw"], B["wcb"], B["mprev"], B["pes"]
                    pg = psum(1)
                    MM(pg[0:Q, 0:4], Lm, logf[0:Q, c, :])
                    MM(pg[0:Q, 4:8], bmT, m_st)
                    CP("dve", gs[0:Q], pg[0:Q, 0:8])
                    bcum = gs[0:Q, 0:4]; m_tok = gs[0:Q, 4:8]
                    a_ = sm[0:Q, 0:4]; mloc = sm[0:Q, 4:8]; mxx = sm[0:Q, 8:12]; nmxx = sm[0:Q, 12:16]
                    wint = sm[0:Q, 16:20]; den = sm[0:Q, 20:24]; mt = sm[0:Q, 24:28]
                    TT("dve", a_, gi[0:Q, c, :], bcum, ALU.subtract)
                    TT("pool", R2[0:Q], idq_f.us(1).bc([Q, 4, Q]), a_.us(2).bc([Q, 4, Q]), ALU.mult)
                    yield
                    A = psum(1)
                    MM(A[0:Q, 0:4 * Q], onesf[0:Q, 0:Q], R2[0:Q].r("p h s -> p (h s)"), True, False)
                    MM(A[0:Q, 0:4 * Q], idq_b, NEGTb, False, True)
                    RED(mloc, A[0:Q, 0:4 * Q].r("p (h s) -> p h s", h=4), ALU.max)
                    TT("dve", mxx, mloc, m_tok, ALU.max)
                    TS("dve", nmxx, mxx, -1.0)
                    TT("dve", mt, bcum, mxx, ALU.add)
                    for h in range(4):
                        ACT(Wt[0:Q, h, :], A[0:Q, h * Q:(h + 1) * Q], AF.Exp, bias=nmxx[:, h:h + 1])
                    pe_ = psum(1)
                    MM(pe_[0:Q, 0:4], SELLAST, mxx)
                    MM(pe_[0:nseq, 4:8], SELROW, mt)
                    MM(pe_[0:nseq, 8:12], SELROW, mxx)
                    CP("dve", pes[0:Q, 0:4], pe_[0:Q, 0:4])
                    CP("dve", pes[0:nseq, 4:12], pe_[0:nseq, 4:12])
                    CP("dve", mprev[0:nseq], m_st)
                    CP("dve", m_st, pes[0:nseq, 4:8])
                    yield
                    QK = psum(1)
                    for h in range(4):
                        for dc in range(2):
                            MM(QK[0:Q, h * Q:(h + 1) * Q], qT[:, h * 2 + dc, cols], kT[:, h * 2 + dc, cols], dc == 0, dc == 1)
                    TT("dve", Sw[0:Q].r("p h s -> p (h s)"), Wt[0:Q].r("p h s -> p (h s)"), QK[0:Q, 0:4 * Q], ALU.mult)
                    TT("dve", wint, m_tok, mxx, ALU.subtract)
                    ACT(wint, wint, AF.Exp)
                    TT("dve", ws[0:Q], a_, pes[0:Q, 0:4], ALU.subtract)
                    ACT(ws[0:Q], ws[0:Q], AF.Exp)
                    TT("dve", kw[0:Q].r("p (h e) -> p h e", h=4), k_tok[c][0:Q].r("p (h e) -> p h e", h=4),
                       ws[0:Q].us(2).bc([Q, 4, 256]), ALU.mult)
                    TT("dve", wcs[0:nseq], mprev[0:nseq], pes[0:nseq, 8:12], ALU.subtract)
                    ACT(wcs[0:nseq], wcs[0:nseq], AF.Exp)
                    TT("dve", Z[0:nseq], identf[0:nseq, 0:nseq].us(2).bc([nseq, nseq, 4]),
                       wcs[0:nseq].us(1).bc([nseq, nseq, 4]), ALU.mult)
                    yield
                    pw = psum(1).bitcast(BF16)
                    for h in range(4):
                        TR(pw[0:Q, h * Q:(h + 1) * Q], Sw[0:Q, h, :], idq_b)
                    CP("act", SwT[0:Q].r("p h s -> p (h s)"), pw[0:Q, 0:4 * Q])
                    pwc = psum(1)
                    MM(pwc[:, 0:nseq * 4], onesf[0:nseq, 0:128], Z[0:nseq].r("p b h -> p (b h)"))
                    CP("act", wcb.r("p b h -> p (b h)"), pwc[:, 0:nseq * 4])
                    yield

                def m2_back(c, B):
                    cols = slice(c * Q, (c + 1) * Q)
                    sm, gs, SwT, kw, wcb = B["sm"], B["gs"], B["SwT"], B["kw"], B["wcb"]
                    bcum = gs[0:Q, 0:4]
                    wint = sm[0:Q, 16:20]; den = sm[0:Q, 20:24]; mt = sm[0:Q, 24:28]; r_ = sm[0:Q, 28:32]
                    ssq = sm[0:Q, 32:36]; rstd = sm[0:Q, 36:40]
                    pden = psum(1)
                    for h in range(4):
                        MM(pden[0:Q, h:h + 1], SwT[0:Q, h, :], onesb[0:Q, 0:1])
                    for h in range(4):
                        for dc in range(2):
                            MM(pden[0:Q, 8 + h * nseq:8 + (h + 1) * nseq], qT[:, h * 2 + dc, cols], nT_b[:, dc, h, :], dc == 0, dc == 1)
                    if prompt:
                        CP("dve", denI[0:Q], pden[0:Q, 8:12])
                    else:
                        TT("dve", tmpd[0:Q], pden[0:Q, 8:8 + 4 * NS].r("p (h b) -> p h b", h=4), bm.us(1).bc([Q, 4, NS]), ALU.mult)
                        RED(denI[0:Q], tmpd[0:Q], ALU.add)
                    TT("dve", denI[0:Q], denI[0:Q], wint, ALU.mult)
                    TT("dve", den, denI[0:Q], pden[0:Q, 0:4], ALU.add)
                    def load_C(b2):
                        for a2 in range(2):
                            DMA("sp", Cf_l[b2 % 5][:, a2], dr["s_c"][l, b2][:, a2 * 128:(a2 + 1) * 128, :].rearrange("h q e -> q h e"))

                    if not prompt:
                        for b2 in range(3):
                            load_C(b2)
                    for b in range(nseq):
                        if prompt:
                            C_f, C_b = S["C_f"], S["C_b"]
                            kwm = kw
                        else:
                            if b + 3 < nseq:
                                load_C(b + 3)
                            cb2 = cq[b % 2]
                            C_f, C_b, kwm = Cf_l[b % 5], cb2["C_b"], cb2["kwm"]
                            CP("act", C_b.r("p a h e -> p (a h e)"), C_f.r("p a h e -> p (a h e)"))
                            TS("dve", kwm[0:Q], kw[0:Q], bm[:, b:b + 1])
                        NI = psum(2)
                        for h in range(4):
                            for dc in range(2):
                                MM(NI[0:Q, h * 256:(h + 1) * 256], qT[:, h * 2 + dc, cols], C_b[:, dc, h, :], dc == 0, dc == 1)
                        if prompt:
                            TT("dve", ni[0:Q].r("p (h e) -> p h e", h=4), NI[0:Q, :].r("p (h e) -> p h e", h=4),
                               wint.us(2).bc([Q, 4, 256]), ALU.mult)
                        elif b == 0:
                            TS("dve", ni[0:Q], NI[0:Q, :], bm[:, b:b + 1])
                        else:
                            STT(ni[0:Q], NI[0:Q, :], bm[:, b:b + 1], ni[0:Q], ALU.mult, ALU.add)
                        for dc in range(2):
                            Cn = psum(2)
                            for h in range(4):
                                MM(Cn[:, h * 256:(h + 1) * 256], kwm[0:Q, h * 256 + dc * 128:h * 256 + (dc + 1) * 128],
                                   v_tok[c][0:Q, h * 256:(h + 1) * 256])
                            for h in range(4):
                                STT(C_f[:, dc, h, :], C_f[:, dc, h, :], wcb[:, b, h:h + 1], Cn[:, h * 256:(h + 1) * 256],
                                    ALU.mult, ALU.add)
                        if prompt:
                            CP("act", C_b.r("p a h e -> p (a h e)"), C_f.r("p a h e -> p (a h e)"))
                        else:
                            for a2 in range(2):
                                DMA("sp", dr["c_s"][l, b][:, a2 * 128:(a2 + 1) * 128, :].rearrange("h q e -> q h e"), C_f[:, a2])
                        yield
                    if not prompt:
                        TT("dve", ni[0:Q].r("p (h e) -> p h e", h=4), ni[0:Q].r("p (h e) -> p h e", h=4),
                           wint.us(2).bc([Q, 4, 256]), ALU.mult)
                    pn2 = psum(1)
                    for dc in range(2):
                        for h in range(4):
                            MM(pn2[:, (dc * 4 + h) * nseq:(dc * 4 + h + 1) * nseq],
                               kw[0:Q, h * 256 + dc * 128:h * 256 + (dc + 1) * 128], bmb)
                    for dc in range(2):
                        TT("dve", nT_f[:, dc], nT_f[:, dc], wcb.r("p b h -> p h b"), ALU.mult)
                    TT("dve", nT_f.r("p a h b -> p (a h b)"), nT_f.r("p a h b -> p (a h b)"), pn2[:, 0:8 * nseq], ALU.add)
                    CP("act", nT_b.r("p a h b -> p (a h b)"), nT_f.r("p a h b -> p (a h b)"))
                    NUM = psum(2)
                    for h in range(4):
                        MM(NUM[0:Q, h * 256:(h + 1) * 256], SwT[0:Q, h, :], v_tok[c][0:Q, h * 256:(h + 1) * 256])
                    TT("dve", ni[0:Q], ni[0:Q], NUM[0:Q, :], ALU.add)
                    ACT(emt[0:Q], mt, AF.Exp, scale=-1.0)
                    STT(den, den, -1.0, den, ALU.mult, ALU.max)
                    TT("dve", den, den, emt[0:Q], ALU.max)
                    P.rec("dve", lambda e, o=r_.ap, i=den.ap: e.reciprocal(out=o, in_=i), reads=[den], writes=[r_])
                    TT("dve", ni[0:Q].r("p (h e) -> p h e", h=4), ni[0:Q].r("p (h e) -> p h e", h=4),
                       r_.us(2).bc([Q, 4, 256]), ALU.mult)
                    for h in range(4):
                        ACT(junk[0:Q], ni[0:Q, h * 256:(h + 1) * 256], AF.Square, accum=ssq[:, h:h + 1])
                    rsqrt_(rstd, ssq, 1.0 / 256, Q, 4)
                    STT(ow[0:Q], o_tok[c][0:Q], 1.0, mnw, ALU.add, ALU.mult)
                    for h in range(4):
                        STT(hmn[0:Q, h * 256:(h + 1) * 256], ni[0:Q, h * 256:(h + 1) * 256], rstd[:, h:h + 1],
                            ow[0:Q, h * 256:(h + 1) * 256], ALU.mult, ALU.mult)
                    yield
                    pht = psum(1).bitcast(BF16)
                    for kc in range(8):
                        TR(pht[:, kc * Q:(kc + 1) * Q], hmn[0:Q, kc * 128:(kc + 1) * 128], idq_b)
                    CP("act", hmT[:, :, cols], pht[:, 0:8 * Q].r("p (k q) -> p k q", k=8))
                    yield

                def m2_stream():
                    yield from m2_front(0, m2b[0])
                    for c in range(NCH):
                        if c + 1 < NCH:
                            yield from m2_front(c + 1, m2b[(c + 1) % 2])
                        yield from m2_back(c, m2b[c % 2])
                    if last_tile or not prompt:
                        nrow = 4 * nseq
                        po = psum(1)
                        tmpn = ni.r("p (a x) -> p a x", a=2)[:, :, 0:nseq * 4].r("p a (b h) -> p a b h", h=4)
                        for dc in range(2):
                            CP("dve", tmpn[:, dc], nT_f[:, dc].r("p h b -> p b h"))
                            TR(po[0:nrow, dc * 128:(dc + 1) * 128], tmpn[:, dc].r("p b h -> p (b h)"), identf)
                        natn = ow[:, 0:256]
                        CP("act", natn[0:nrow], po[0:nrow, 0:256])
                        DMA("sp", dr["n_p"][l] if prompt else dr["n_s"][l], natn[0:nrow])
                        DMA("sp", dr["m_p"][l] if prompt else dr["m_s"][l], m_st)
                        if prompt:
                            for a2 in range(2):
                                DMA("sp", dr["c_p"][l][:, a2 * 128:(a2 + 1) * 128, :].rearrange("h q e -> q h e"), S["C_f"][:, a2])

                def ga_stream():
                    wbo = None
                    for c in range(NCH):
                        for hf in range(2):
                            proj_tok(xT, c, wbv[0], hf * 512, 512,
                                     lambda ps, cb0, n, c=c, hf=hf: CP("act", v_tok[c][0:Q, hf * 512:hf * 512 + n], ps))
                            yield
                        if wbo is None:
                            wbo = nextw(prefetch=False)
                        for hf in range(2):
                            proj_tok(xT, c, wbo, hf * 512, 512,
                                     lambda ps, cb0, n, c=c, hf=hf: ACT(o_tok[c][0:Q, hf * 512:hf * 512 + n], ps, AF.Tanh, scale=0.5))
                            yield
                    wa = nextw()
                    wg = nextw(prefetch=False)
                    for j in range(8):
                        pb = psum(1); pgt = psum(1)
                        for kc in range(8):
                            MM(pgt[:, 0:TTK], wg[:, kc, j * 128:(j + 1) * 128], xT[:, kc, :], kc == 0, kc == 7)
                        for kc in range(8):
                            MM(pb[:, 0:TTK], wa[:, kc, j * 128:(j + 1) * 128], yT[:, kc, :], kc == 0, kc == 7)
                        sg = sgA[j % 2]
                        ACT(sg, pgt[:, 0:TTK], AF.Tanh, scale=0.5)
                        STT(mT[:, j, :], sg, 1.0, pb[:, 0:TTK], ALU.add, ALU.mult)
                        yield

                run_streams([(m2_stream(), list(range(6))), (ga_stream(), [6, 7])])
                release(mM2)
                cut('M2')
                cur_ring[0] = make_ring(12 * 1024)
                wa = nextw()
                wg = nextw(prefetch=False)
                for j in range(8):
                    pb = psum(1); pgt = psum(1)
                    for kc in range(8):
                        MM(pgt[:, 0:TTK], wg[:, kc, j * 128:(j + 1) * 128], xT[:, kc, :], kc == 0, kc == 7)
                    for kc in range(8):
                        MM(pb[:, 0:TTK], wa[:, kc, j * 128:(j + 1) * 128], hmT[:, kc, :], kc == 0, kc == 7)
                    sg = sgA[j % 2]
                    ACT(sg, pgt[:, 0:TTK], AF.Tanh, scale=0.5)
                    t2 = cur_ring[0](TTK, F32)
                    STT(t2, sg, 1.0, pb[:, 0:TTK], ALU.add, ALU.mult)
                    TT("dve", mT[:, j, :], mT[:, j, :], t2, ALU.add)

                def layer_norm(c, pss, g_, b_, mixscale=1.0):
                    st6 = cur_ring[0](12, F32); mv = cur_ring[0](4, F32)
                    for hf in range(2):
                        xv = x_tok[0:Q, c, hf * 512:(hf + 1) * 512]
                        if mixscale == 1.0:
                            STT(xv, xv, ALPHA, pss[hf], ALU.mult, ALU.add)
                        else:
                            TS("dve", xv, xv, ALPHA)
                            STT(xv, pss[hf], mixscale, xv, ALU.mult, ALU.add)
                        P.rec("dve", lambda e, o=st6[0:Q, hf * 6:(hf + 1) * 6].ap, i=xv.ap: e.bn_stats(out=o, in_=i),
                              reads=[xv], writes=[st6])
                    P.rec("dve", lambda e, o=mv[0:Q, 0:2].ap, i=st6[0:Q].ap: e.bn_aggr(out=o, in_=i), reads=[st6], writes=[mv])
                    rsqrt_(mv[0:Q, 2:3], mv[0:Q, 1:2], 1.0, Q, 1)
                    xv = x_tok[0:Q, c, :]
                    TS("dve", xv, xv, mv[0:Q, 0:1], mv[0:Q, 2:3], ALU.subtract, ALU.mult)
                    TT("dve", xv, xv, g_, ALU.mult)
                    TT("dve", xv, xv, b_, ALU.add)

                wb = nextw()
                l1g = prow("l1g", Q); l1b = prow("l1b", Q)
                for c in range(NCH):
                    pss = []
                    for hf in range(2):
                        ps = psum(1)
                        for kc in range(8):
                            MM(ps[0:Q, :], mT[:, kc, c * Q:(c + 1) * Q], wb[:, kc, hf * 512:(hf + 1) * 512], kc == 0, kc == 7)
                        pss.append(ps[0:Q, :])
                    layer_norm(c, pss, l1g, l1b, mixscale=0.5)

                cut('G')
                release(m0)
                m0b = mark()
                x1T = alloc((8, TTK), BF16)
                to_featmajor(x_tok, x1T)
                hT = alloc((22, TTK), BF16)
                histF = None
                if not prompt:
                    histF = alloc((44, NS * 2), F32)
                    for piece in range(4):
                        mm = mark()
                        nat = alloc(1408, F32)
                        DMA("sp", nat[0:NS * 2, :], dr["s_fconv"][l, :, piece * 1408:(piece + 1) * 1408])
                        for k in range(11):
                            cc = piece * 11 + k
                            ps = psum(1)
                            TR(ps[:, 0:NS * 2], nat[0:NS * 2, k * 128:(k + 1) * 128], identf[0:NS * 2, 0:NS * 2])
                            CPX(histF[:, cc, :], ps[:, 0:NS * 2])
                        release(mm)

                cur_ring[0] = make_ring(40 * 1024)

                def ffn_conv(ps, cc, silu):
                    xp = cur_ring[0]((nseq, 2 + Lq), F32)
                    acc = cur_ring[0]((nseq, Lq), F32)
                    if prompt:
                        CP("act", xp[:, 0, 0:2], S["tl_f"][:, cc, :])
                    else:
                        CP("act", xp[:, :, 0:2], histF[:, cc, :].r("p (b j) -> p b j", j=2))
                    CP("act", xp[:, :, 2:2 + Lq], ps.r("p (b t) -> p b t", b=nseq))
                    ACT(acc, ps.r("p (b t) -> p b t", b=nseq), AF.Identity, scale=cwf[:, cc, 2:3], bias=cbf[:, cc:cc + 1])
                    for j in (1, 0):
                        STT(acc, xp[:, :, j:j + Lq], cwf[:, cc, j:j + 1], acc, ALU.mult, ALU.add)
                    if prompt:
                        CP("act", S["tl_f"][:, cc, :], xp[:, 0, Lq:Lq + 2])
                    if silu:
                        sg = cur_ring[0]((nseq, Lq), BF16)
                        ACT(sg, acc, AF.Silu)
                        return sg
                    return acc

                for bb in range(6):
                    nj = 4 if bb < 5 else 2
                    n = nj * 128
                    wb = nextw()
                    for jj in range(nj):
                        j = bb * 4 + jj
                        pg_ = psum(1); pv_ = psum(1)
                        for kc in range(8):
                            MM(pg_[:, 0:TTK], wb[:, kc, jj * 128:(jj + 1) * 128], x1T[:, kc, :], kc == 0, kc == 7)
                        for kc in range(8):
                            MM(pv_[:, 0:TTK], wb[:, kc, n + jj * 128:n + (jj + 1) * 128], x1T[:, kc, :], kc == 0, kc == 7)
                        sg = ffn_conv(pg_[:, 0:TTK], j, True)
                        av = ffn_conv(pv_[:, 0:TTK], 22 + j, False)
                        TT("dve", hT[:, j, :].r("p (b t) -> p b t", b=nseq), sg, av, ALU.mult)
                    if want_rows:
                        emit_rows(x1T, wb, 0, n, dr["fconv_p"], dr["fconv_s"], 2, bb * 512)
                        emit_rows(x1T, wb, n, n, dr["fconv_p"], dr["fconv_s"], 2, DFF + bb * 512)
                l2g = prow("l2g", Q); l2b = prow("l2b", Q)
                for cb0 in range(2):
                    pss = [psum(1) for _ in range(NCH)]
                    for kh in range(2):
                        wb = nextw()
                        for c in range(NCH):
                            for k in range(11):
                                kc = kh * 11 + k
                                MM(pss[c][0:Q, :], hT[:, kc, c * Q:(c + 1) * Q], wb[:, k, :], kc == 0, kc == 21)
                    if cb0 == 0:
                        keep = []
                        for c in range(NCH):
                            t = alloc(512, F32)
                            CPX(t[0:Q], pss[c][0:Q, :])
                            keep.append(t[0:Q])
                    else:
                        for c in range(NCH):
                            layer_norm(c, [keep[c], pss[c][0:Q, :]], l2g, l2b)
                release(m0b)
                release(m0)

            for l in range(DEPTH):
                if l >= KL:
                    continue
                base_m = mark()
                try:
                    layer_body(l)
                except _Cut:
                    pass
                release(base_m)
                gctr[0] = (tiles.index((kind, ti)) * DEPTH + l + 1) * NWB
                wstate["issued"] = max(wstate["issued"], gctr[0])
            if prompt:
                DMA("sp", dr["y_p"][ti * 512:(ti + 1) * 512, :].rearrange("(c p) d -> p c d", p=128), x_tok)
            else:
                DMA("sp", dr["y_s"], x_tok[0:Q, 0, :])

        print("final peak", astate.get("peak"))
        P.emit()
        build_program.stats = P.stats
    return nc


_NC_CACHE = {}


def _get_nc():
    if "nc" not in _NC_CACHE:
        _NC_CACHE["nc"] = build_program()
    return _NC_CACHE["nc"]


def kernel(x_prompt, x_sample, state_ssd, state_ssd_conv, state_mlstm_c, state_mlstm_n, state_mlstm_m,
           state_ffn_conv, w_in, ssd_conv_w, ssd_conv_b, ssd_dt_bias, ssd_a_log, ssd_d, ssd_norm_w,
           mlstm_gate_b, mlstm_norm_w, w_branch_a, w_branch_b, w_out, ln1_g, ln1_b, ffn_w_up, ffn_conv_w,
           ffn_conv_b, ffn_w_down, ln2_g, ln2_b):
    f = lambda a: np.ascontiguousarray(np.asarray(a, dtype=np.float32))
    nc = _get_nc()
    prow = np.zeros((DEPTH, NPR), np.float32)
    for name, arr in (("dtb", ssd_dt_bias), ("alog", ssd_a_log), ("D", ssd_d), ("gb", mlstm_gate_b),
                      ("snw", ssd_norm_w), ("mnw", mlstm_norm_w), ("l1g", ln1_g), ("l1b", ln1_b),
                      ("l2g", ln2_g), ("l2b", ln2_b)):
        o, n = PR[name]
        prow[:, o:o + n] = f(arr)
    cw_s = f(ssd_conv_w).reshape(DEPTH, 4, 10, 128).transpose(0, 3, 2, 1).reshape(DEPTH, 128, 40)
    cb_s = f(ssd_conv_b).reshape(DEPTH, 10, 128).transpose(0, 2, 1)
    cw_f = f(ffn_conv_w).reshape(DEPTH, 3, 44, 128).transpose(0, 3, 2, 1).reshape(DEPTH, 128, 132)
    cb_f = f(ffn_conv_b).reshape(DEPTH, 44, 128).transpose(0, 2, 1)
    consts = _make_consts()
    shared = {"w_in": f(w_in), "w_a": f(w_branch_a), "w_b": f(w_branch_b), "w_out": f(w_out), "w_up": f(ffn_w_up),
              "w_down": f(ffn_w_down), "prow": prow, "cw_s": f(cw_s), "cb_s": f(cb_s), "cw_f": f(cw_f),
              "cb_f": f(cb_f), "consts": consts}
    xp = f(x_prompt); xs = f(x_sample)
    s_ssd = f(state_ssd); s_sc = f(state_ssd_conv); s_c = f(state_mlstm_c); s_n = f(state_mlstm_n)
    s_m = f(state_mlstm_m); s_fc = f(state_ffn_conv)
    in_maps = []
    for i in range(NCORES):
        sl = slice(i * NS, (i + 1) * NS)
        m = dict(shared)
        m["xp"] = xp[i]
        m["xs"] = np.ascontiguousarray(xs[sl].reshape(NS * SL, D))
        m["s_ssd"] = np.ascontiguousarray(s_ssd[:, sl].reshape(DEPTH, NS, 1024, 64))
        m["s_sconv"] = np.ascontiguousarray(s_sc[:, sl].reshape(DEPTH, NS * 3, 1280))
        m["s_c"] = np.ascontiguousarray(s_c[:, sl])
        m["s_n"] = np.ascontiguousarray(s_n[:, sl].reshape(DEPTH, NS * 4, 256))
        m["s_m"] = np.ascontiguousarray(s_m[:, sl])
        m["s_fconv"] = np.ascontiguousarray(s_fc[:, sl].reshape(DEPTH, NS * 2, 2 * DFF))
        in_maps.append(m)
    res = run_bass_kernel_spmd(nc, in_maps, core_ids=list(range(NCORES)))
    R = res.results
    cat0 = lambda k: np.stack([R[i][k] for i in range(NCORES)], 0)
    y_p = cat0("y_p")
    y_s = cat0("y_s").reshape(NCORES * NS, SL, D)
    ssd_p = np.stack([R[i]["ssd_p"] for i in range(NCORES)], 1).reshape(DEPTH, NCORES, 16, 64, 64)
    ssd_s = np.concatenate([R[i]["ssd_s"] for i in range(NCORES)], 1).reshape(DEPTH, NCORES * NS, 16, 64, 64)
    sconv_p = np.stack([R[i]["sconv_p"] for i in range(NCORES)], 1)
    sconv_s = np.concatenate([R[i]["sconv_s"] for i in range(NCORES)], 1)
    c_p = np.stack([R[i]["c_p"] for i in range(NCORES)], 1)
    c_s = np.concatenate([R[i]["c_s"] for i in range(NCORES)], 1)
    n_p = np.stack([R[i]["n_p"] for i in range(NCORES)], 1)
    n_s = np.concatenate([R[i]["n_s"].reshape(DEPTH, NS, 4, 256) for i in range(NCORES)], 1)
    m_p = np.stack([R[i]["m_p"].reshape(DEPTH, 4) for i in range(NCORES)], 1)
    m_s = np.concatenate([R[i]["m_s"] for i in range(NCORES)], 1)
    fconv_p = np.stack([R[i]["fconv_p"] for i in range(NCORES)], 1)
    fconv_s = np.concatenate([R[i]["fconv_s"] for i in range(NCORES)], 1)
    outs = (y_p, y_s, ssd_p, ssd_s, sconv_p, sconv_s, c_p, c_s, n_p, n_s, m_p, m_s, fconv_p, fconv_s)
    return tuple(np.ascontiguousarray(o, dtype=np.float32) for o in outs)
```

```python
import contextlib
import os
import numpy as np
import concourse.bass as bass
import concourse.mybir as mybir
from concourse.bass_utils import run_bass_kernel_spmd

F32 = mybir.dt.float32
BF16 = mybir.dt.bfloat16
AF = mybir.ActivationFunctionType
ALU = mybir.AluOpType
AX = mybir.AxisListType

NCORES = 8
D = 1024
SEQ = 2048
DEPTH = 2
NS = 16
SL = 4
NIN = 8472
DFF = 2816
ALPHA = (2 * DEPTH) ** 0.25
EPS = 1e-5
NEGV = -30000.0
ENGS = ("pe", "act", "dve", "pool", "sp")


class _Cut(Exception):
    pass


KT = os.environ.get('KT', '')
KL = int(os.environ.get('KL', '2'))
KCUT = os.environ.get('KCUT', '')


def cut(name):
    if KCUT == name:
        raise _Cut()


class Op:
    __slots__ = ("eng", "fn", "deps", "dma", "sem", "val", "waits", "snap", "needed")

    def __init__(self, eng, fn, dma):
        self.eng = eng
        self.fn = fn
        self.deps = set()
        self.dma = dma
        self.sem = None
        self.val = 0
        self.waits = []
        self.snap = None
        self.needed = False


class Buf:
    __slots__ = ("ap", "toks", "meta")

    def __init__(self, ap, toks, meta=None):
        self.ap = ap
        self.toks = toks
        self.meta = meta

    def __getitem__(self, k):
        return Buf(self.ap[k], self.toks, self.meta)

    def r(self, pat, **kw):
        return Buf(self.ap.rearrange(pat, **kw), self.toks, self.meta)

    def us(self, axis):
        return Buf(self.ap.unsqueeze(axis), self.toks, self.meta)

    def bc(self, shape):
        return Buf(self.ap.to_broadcast(list(shape)), self.toks, self.meta)

    def bitcast(self, dt):
        return Buf(self.ap.bitcast(dt), self.toks, self.meta)


def _toks(lst):
    out = []
    for b in lst:
        if b is None or isinstance(b, (int, float)):
            continue
        if isinstance(b, Buf):
            out.extend(b.toks)
        else:
            out.append(b)
    return out


def _ap(x):
    return x.ap if isinstance(x, Buf) else x


class Prog:
    def __init__(self, nc, n_dma_sems=12):
        self.nc = nc
        self.ops = {e: [] for e in ENGS}
        self.order = []
        self.last_write = {}
        self.readers = {}
        self.n_dma_sems = n_dma_sems
        self.checker = None

    def rec(self, eng, fn, reads=(), writes=(), dma=False):
        op = Op(eng, fn, dma)
        if self.checker is not None:
            for b in list(reads) + list(writes):
                if isinstance(b, Buf) and b.meta is not None:
                    self.checker(b)
        rt = _toks(reads)
        wt = _toks(writes)
        if any(isinstance(t, tuple) and t[0] == "ps" for t in rt):
            wt = wt + [t for t in rt if isinstance(t, tuple) and t[0] == "ps"]
            rt = [t for t in rt if not (isinstance(t, tuple) and t[0] == "ps")]
        for t in rt:
            w = self.last_write.get(t)
            if w is not None:
                op.deps.add(w)
        for t in wt:
            w = self.last_write.get(t)
            if w is not None:
                op.deps.add(w)
            for r in self.readers.get(t, ()):
                op.deps.add(r)
        for t in rt:
            lst = self.readers.setdefault(t, [])
            if not dma:
                for i, r in enumerate(lst):
                    if (not r.dma) and r.eng == eng:
                        lst[i] = op
                        break
                else:
                    lst.append(op)
            else:
                lst.append(op)
        for t in wt:
            self.last_write[t] = op
            self.readers[t] = []
        op.deps.discard(op)
        self.ops[eng].append(op)
        self.order.append(op)
        return op

    def emit(self):
        nc = self.nc
        for op in self.order:
            if op.eng == "pe" and not op.dma:
                op.deps = {d for d in op.deps if not (d.eng == "pe" and not d.dma)}
        dma_cnt = {}
        dma_last = {}
        for op in self.order:
            if op.dma:
                k = dma_cnt.get(op.eng, 0)
                dma_cnt[op.eng] = k + 1
                slot = (op.eng, k % self.n_dma_sems)
                prev = dma_last.get(slot)
                if prev is not None:
                    op.deps.add(prev)
                dma_last[slot] = op
                op.sem = slot
                op.val = 16 * (k // self.n_dma_sems + 1)
        for op in self.order:
            for d in op.deps:
                d.needed = True
        final_ops = list(dma_last.values())
        cnt = {e: 0 for e in ENGS}
        for op in self.order:
            if not op.dma and op.needed:
                cnt[op.eng] += 1
                op.sem = ("c", op.eng)
                op.val = cnt[op.eng]
        know = {e: {} for e in ENGS}
        for op in self.order:
            kn = know[op.eng]
            wd = {}
            for d in sorted(op.deps, key=lambda d: -d.val):
                if d.sem is None:
                    continue
                if kn.get(d.sem, 0) >= d.val:
                    continue
                if wd.get(d.sem, 0) < d.val:
                    wd[d.sem] = d.val
                for s, v in d.snap.items():
                    if kn.get(s, 0) < v:
                        kn[s] = v
            op.waits = list(wd.items())
            if op.sem is not None:
                sn = dict(kn)
                sn[op.sem] = op.val
                op.snap = sn
        sem_keys = []
        seen = set()
        for op in self.order:
            if op.sem is not None and op.sem not in seen:
                seen.add(op.sem)
                sem_keys.append(op.sem)
        self.stats = {e: len(self.ops[e]) for e in ENGS}
        self.stats["sems"] = len(sem_keys)
        self.stats["waits"] = sum(len(op.waits) for op in self.order)
        self.stats["cnt"] = dict(cnt)
        self.stats["dma"] = dict(dma_cnt)
        sems = {}
        with contextlib.ExitStack() as st:
            for i, k in enumerate(sem_keys):
                sems[k] = st.enter_context(nc.semaphore("s%d" % i))
            block = st.enter_context(nc.Block())
            fin = [(sems[d.sem], d.val) for d in final_ops]
            ops = self.ops

            def run(eng_name, eng):
                for op in ops[eng_name]:
                    for s, v in op.waits:
                        eng.wait_ge(sems[s], v)
                    ins = op.fn(eng)
                    if op.sem is not None:
                        ins.then_inc(sems[op.sem], 16 if op.dma else 1)
                if eng_name == "sp":
                    for s, v in fin:
                        eng.wait_ge(s, v)

            @block.tensor
            def _(e):
                run("pe", e)

            @block.scalar
            def _(e):
                run("act", e)

            @block.vector
            def _(e):
                run("dve", e)

            @block.gpsimd
            def _(e):
                run("pool", e)

            @block.sync
            def _(e):
                run("sp", e)


def _const_layout():
    lay = {}
    off = 0

    def add(name, n):
        nonlocal off
        lay[name] = (off, n)
        off += n

    add("ident", 128)
    add("ones", 128)
    for sfx, q in (("p", 128), ("s", 64)):
        add("L" + sfx, q)
        add("U" + sfx, q)
        add("SELLAST" + sfx, q)
        add("NEG" + sfx, 512)
        add("NEGT" + sfx, 4 * q)
    add("SELROWp", 1)
    add("SELROWs", NS)
    add("bms", NS)
    add("bmTs", 64)
    return lay, off


CL, NCONST = _const_layout()


def _make_consts():
    c = np.zeros((128, NCONST), np.float32)

    def put(name, arr):
        o, n = CL[name]
        a = np.asarray(arr, np.float32)
        c[: a.shape[0], o:o + a.shape[1]] = a

    put("ident", np.eye(128))
    put("ones", np.ones((128, 128)))
    for sfx, q, sl in (("p", 128, 128), ("s", 64, SL)):
        i = np.arange(q)
        seq = i // sl
        same = seq[:, None] == seq[None, :]
        put("L" + sfx, same & (i[:, None] <= i[None, :]))
        put("U" + sfx, same & (i[:, None] > i[None, :]))
        last = seq * sl + sl - 1
        put("SELLAST" + sfx, i[:, None] == last[None, :])
        neg = np.where(same & (i[None, :] >= i[:, None]), 0.0, NEGV)
        put("NEG" + sfx, np.tile(neg, (1, 512 // q)))
        negt = np.where(same & (i[None, :] <= i[:, None]), 0.0, NEGV)
        put("NEGT" + sfx, np.tile(negt, (1, 4)))
    sr = np.zeros((128, 1))
    sr[127, 0] = 1
    put("SELROWp", sr)
    i = np.arange(64)
    put("SELROWs", (i[:, None] == (np.arange(NS) * SL + SL - 1)[None, :]))
    bm = (i[:, None] // SL) == np.arange(NS)[None, :]
    put("bms", bm)
    put("bmTs", bm.T)
    return c


PR = {"dtb": (0, 16), "alog": (16, 16), "D": (32, 16), "gb": (48, 8), "snw": (64, 1024), "mnw": (1088, 1024),
      "l1g": (2112, 1024), "l1b": (3136, 1024), "l2g": (4160, 1024), "l2b": (5184, 1024)}
NPR = 6208

def _wblocks():
    blks = [("w_in", 0, 8, [(0, 1024)]), ("w_in", 0, 8, [(1024, 1024)]), ("w_in", 0, 8, [(2048, 272)]),
            ("w_in", 0, 8, [(2320, 1024)]), ("w_in", 0, 8, [(3344, 1024)]), ("w_in", 0, 8, [(4368, 1032)]),
            ("w_in", 0, 8, [(5400, 1024)]), ("w_a", 0, 8, [(0, 1024)]), ("w_in", 0, 8, [(6424, 1024)]),
            ("w_b", 0, 8, [(0, 1024)]), ("w_in", 0, 8, [(7448, 1024)]), ("w_out", 0, 8, [(0, 1024)])]
    for bb in range(6):
        n = 512 if bb < 5 else 256
        blks.append(("w_up", 0, 8, [(bb * 512, n), (DFF + bb * 512, n)]))
    for cb in range(2):
        for kh in range(2):
            blks.append(("w_down", kh * 11 * 128, 11, [(cb * 512, 512)]))
    return blks


WBLK = _wblocks()
NWB = len(WBLK)
WCAP = 8 * 1032
NWBUF = 2
USE_WSC = True


def build_program():
    nc = bass.Bass("TRN2", target_bir_lowering=False)
    dr = {}

    def din(name, shape):
        dr[name] = nc.dram_tensor(name, list(shape), F32, kind="ExternalInput").ap()
        return dr[name]

    def dout(name, shape):
        dr[name] = nc.dram_tensor(name, list(shape), F32, kind="ExternalOutput").ap()
        return dr[name]

    din("xp", (SEQ, D)); din("xs", (NS * SL, D))
    din("s_ssd", (DEPTH, NS, 1024, 64)); din("s_sconv", (DEPTH, NS * 3, 1280))
    din("s_c", (DEPTH, NS, 4, 256, 256)); din("s_n", (DEPTH, NS * 4, 256)); din("s_m", (DEPTH, NS, 4))
    din("s_fconv", (DEPTH, NS * 2, 2 * DFF))
    din("w_in", (DEPTH, D, NIN)); din("w_a", (DEPTH, D, D)); din("w_b", (DEPTH, D, D)); din("w_out", (DEPTH, D, D))
    din("w_up", (DEPTH, D, 2 * DFF)); din("w_down", (DEPTH, DFF, D))
    din("prow", (DEPTH, NPR)); din("cw_s", (DEPTH, 128, 40)); din("cb_s", (DEPTH, 128, 10))
    din("cw_f", (DEPTH, 128, 132)); din("cb_f", (DEPTH, 128, 44)); din("consts", (128, NCONST))
    dout("y_p", (SEQ, D)); dout("y_s", (NS * SL, D))
    dout("ssd_p", (DEPTH, 1024, 64)); dout("ssd_s", (DEPTH, NS, 1024, 64))
    dout("sconv_p", (DEPTH, 3, 1280)); dout("sconv_s", (DEPTH, NS, 3, 1280))
    dout("c_p", (DEPTH, 4, 256, 256)); dout("c_s", (DEPTH, NS, 4, 256, 256))
    dout("n_p", (DEPTH, 4, 256)); dout("n_s", (DEPTH, NS * 4, 256))
    dout("m_p", (DEPTH, 1, 4)); dout("m_s", (DEPTH, NS, 4))
    dout("fconv_p", (DEPTH, 2, 2 * DFF)); dout("fconv_s", (DEPTH, NS, 2, 2 * DFF))

    wsc = nc.dram_tensor("wsc", [DEPTH * NWB, 128, WCAP], BF16, kind="Internal").ap()
    st = contextlib.ExitStack()
    with st:
        P = Prog(nc)
        ARENA_BYTES = 211456
        arena_t = st.enter_context(nc.sbuf_tensor("arena", [128, ARENA_BYTES // 4], F32))
        PS_t = st.enter_context(nc.psum_tensor("PS", [128, 4096], F32))
        astate = {"off": 0}

        def alloc(free, dt=F32):
            if isinstance(free, int):
                free = (free,)
            n = int(np.prod(free))
            nb = n * (4 if dt == F32 else 2)
            nb = (nb + 127) // 128 * 128
            o = astate["off"]
            astate["off"] = o + nb
            astate["peak"] = max(astate.get("peak", 0), astate["off"])
            assert astate["off"] <= ARENA_BYTES, ("arena overflow", astate["off"])
            base = arena_t[:, o // 4:(o + nb) // 4]
            ap = base if dt == F32 else base.bitcast(BF16)
            ap = ap[:, 0:n]
            if len(free) == 2:
                ap = ap.rearrange("p (a b) -> p a b", a=free[0])
            elif len(free) == 3:
                ap = ap.rearrange("p (a b c) -> p a b c", a=free[0], b=free[1])
            elif len(free) == 4:
                ap = ap.rearrange("p (a b c d) -> p a b c d", a=free[0], b=free[1], c=free[2])
            toks = [("sb", k) for k in range(o // 128, (o + nb) // 128)]
            return Buf(ap, toks)

        def mark():
            return astate["off"]

        def release(m):
            astate["off"] = m

        ps_use = [0] * 8
        ps_clock = [0]
        ps_gen = [0] * 8

        ring_gen = {}

        def ps_check(b):
            for kind, idx, g in b.meta:
                if kind == "ps":
                    assert ps_gen[idx] == g, ("stale PSUM buffer used", idx, g, ps_gen[idx])
                else:
                    assert ring_gen.get(idx, 0) == g, ("stale ring buffer used", idx, g, ring_gen.get(idx, 0))

        def make_ring(nbytes):
            base = (astate["off"] + 511) // 512 * 512
            nbytes = nbytes // 512 * 512
            astate["off"] = base + nbytes
            astate["peak"] = max(astate.get("peak", 0), astate["off"])
            assert astate["off"] <= ARENA_BYTES, ("arena overflow (ring)", astate["off"])
            assert base % 512 == 0 or True
            st_ = {"o": 0}

            def ralloc(free, dt=F32):
                if isinstance(free, int):
                    free = (free,)
                n = int(np.prod(free))
                nb = n * (4 if dt == F32 else 2)
                nb = (nb + 511) // 512 * 512
                assert nb <= nbytes
                if st_["o"] + nb > nbytes:
                    st_["o"] = 0
                o = base + st_["o"]
                st_["o"] += nb
                o4 = (o + 3) // 4 * 4
                bs_ = arena_t[:, o4 // 4:(o4 + nb) // 4 if o4 + nb <= ARENA_BYTES else ARENA_BYTES // 4]
                ap = bs_ if dt == F32 else bs_.bitcast(BF16)
                ap = ap[:, 0:n]
                if len(free) == 2:
                    ap = ap.rearrange("p (a b) -> p a b", a=free[0])
                elif len(free) == 3:
                    ap = ap.rearrange("p (a b c) -> p a b c", a=free[0], b=free[1])
                slots = list(range(o // 512, (o + nb) // 512))
                meta = []
                for k in slots:
                    ring_gen[k] = ring_gen.get(k, 0) + 1
                    meta.append(("ring", k, ring_gen[k]))
                return Buf(ap, [("sb", k) for k in range(o // 128, (o + nb) // 128)], tuple(meta))

            return ralloc

        P.checker = ps_check

        ps_allowed = [list(range(8))]

        def run_streams(streams):
            alive = list(streams)
            while alive:
                for item in list(alive):
                    g, banks = item
                    ps_allowed[0] = banks
                    try:
                        next(g)
                    except StopIteration:
                        alive.remove(item)
            ps_allowed[0] = list(range(8))

        def psum(nb=1):
            best, bs = None, None
            for s in range(0, 8, nb):
                if any(b not in ps_allowed[0] for b in range(s, s + nb)):
                    continue
                sc = max(ps_use[s:s + nb])
                if best is None or sc < best:
                    best, bs = sc, s
            ps_clock[0] += 1
            for b in range(bs, bs + nb):
                ps_use[b] = ps_clock[0]
                ps_gen[b] += 1
            return Buf(PS_t[:, bs * 512:(bs + nb) * 512], [("ps", b) for b in range(bs, bs + nb)],
                       tuple(("ps", b, ps_gen[b]) for b in range(bs, bs + nb)))

        def MM(out, lhsT, rhs, start=True, stop=True):
            P.rec("pe", lambda e: e.matmul(out.ap, lhsT.ap, rhs.ap, start=start, stop=stop),
                  reads=[lhsT, rhs], writes=[out])

        def TR(out, in_, ident):
            P.rec("pe", lambda e: e.transpose(out.ap, in_.ap, ident.ap), reads=[in_, ident], writes=[out])

        def ACT(out, in_, func, scale=1.0, bias=None, accum=None):
            kw = {}
            if bias is not None:
                kw["bias"] = _ap(bias)
            if accum is not None:
                kw["accum_out"] = accum.ap
            sc = _ap(scale)
            P.rec("act", lambda e: e.activation(out=out.ap, in_=in_.ap, func=func, scale=sc, **kw),
                  reads=[in_, scale, bias], writes=[out, accum])

        def TT(eng, out, in0, in1, op):
            P.rec(eng, lambda e: e.tensor_tensor(out=out.ap, in0=in0.ap, in1=in1.ap, op=op),
                  reads=[in0, in1], writes=[out])

        def TS(eng, out, in0, s1, s2=None, op0=ALU.mult, op1=None):
            a1, a2 = _ap(s1), _ap(s2)
            if op1 is None:
                P.rec(eng, lambda e: e.tensor_scalar(out=out.ap, in0=in0.ap, scalar1=a1, scalar2=None, op0=op0),
                      reads=[in0, s1], writes=[out])
            else:
                P.rec(eng, lambda e: e.tensor_scalar(out=out.ap, in0=in0.ap, scalar1=a1, scalar2=a2, op0=op0, op1=op1),
                      reads=[in0, s1, s2], writes=[out])

        def STT(out, in0, scalar, in1, op0, op1):
            sc = _ap(scalar)
            P.rec("dve", lambda e: e.scalar_tensor_tensor(out=out.ap, in0=in0.ap, scalar=sc, in1=in1.ap, op0=op0, op1=op1),
                  reads=[in0, scalar, in1], writes=[out])

        def CP(eng, out, in_):
            if eng == "act":
                ACT(out, in_, AF.Copy)
            else:
                P.rec(eng, lambda e: e.tensor_copy(out=out.ap, in_=in_.ap), reads=[in_], writes=[out])

        def MEMSET(eng, out, v):
            P.rec(eng, lambda e: e.memset(out.ap, v), writes=[out])

        def DMA(q, out, in_, reads=(), writes=()):
            oa, ia = _ap(out), _ap(in_)
            P.rec(q, lambda e: e.dma_start(out=oa, in_=ia), reads=list(reads) + ([in_] if isinstance(in_, Buf) else []),
                  writes=list(writes) + ([out] if isinstance(out, Buf) else []), dma=True)

        def RED(out, in_, op):
            P.rec("dve", lambda e: e.tensor_reduce(out=out.ap, in_=in_.ap, axis=AX.X, op=op), reads=[in_], writes=[out])

        cp_rr = [0]

        def CPX(out, in_):
            cp_rr[0] ^= 1
            CP("act" if cp_rr[0] else "dve", out, in_)

        cf = alloc(NCONST, F32)
        cb_ = alloc(NCONST, BF16)
        DMA("sp", cf, dr["consts"])
        DMA("pool", cb_, dr["consts"])

        def CF(name, rows=128, sub=None):
            o, n = CL[name]
            if sub is not None:
                o, n = o + sub[0], sub[1]
            return cf[0:rows, o:o + n]

        def CB(name, rows=128, sub=None):
            o, n = CL[name]
            if sub is not None:
                o, n = o + sub[0], sub[1]
            return cb_[0:rows, o:o + n]

        mhalf = alloc(1, F32)
        MEMSET("pool", mhalf, -0.5)
        wbufs = [alloc(WCAP, BF16) for _ in range(NWBUF)]
        x_tok = alloc((4, D), F32)
        prm_small = alloc(64, F32)
        cws = alloc((10, 4), F32); cbs = alloc(10, F32); cwf = alloc((44, 3), F32); cbf = alloc(44, F32)
        a_bc = alloc(16, F32)
        pst = []
        for l in range(DEPTH):
            s = {"hT_f": alloc(512, F32), "hT_b": alloc(512, BF16), "C_f": alloc((2, 4, 256), F32),
                 "C_b": alloc((2, 4, 256), BF16), "nT_f": alloc((2, 4, 1), F32), "nT_b": alloc((2, 4, 1), BF16),
                 "m_st": alloc(4, F32), "tl_s": alloc((10, 3), F32), "tl_f": alloc((44, 2), F32)}
            for k in ("hT_f", "hT_b", "C_f", "C_b", "nT_f", "nT_b", "m_st", "tl_s", "tl_f"):
                MEMSET("pool", s[k], 0.0)
            pst.append(s)

        wstate = {"issued": 0, "gs": None, "lst": None}

        def wbuf_of(g):
            if wstate["gs"] is not None and g >= wstate["gs"]:
                lst = wstate["lst"]
                return lst[(g - wstate["gs"]) % len(lst)]
            return wbufs[g % NWBUF]

        def wdepth(g):
            if wstate["gs"] is not None and g >= wstate["gs"]:
                return len(wstate["lst"]) - 1
            return 1

        def wissue(g):
            l = (g // NWB) % DEPTH
            name, row0, nk, segs = WBLK[g % NWB]
            wb = wbuf_of(g)
            tot = sum(n for _, n in segs)
            view = wb[:, 0:nk * tot].r("p (k n) -> p k n", k=nk)
            lk = g % (DEPTH * NWB)
            if g >= DEPTH * NWB and USE_WSC:
                DMA("pool", wb[:, 0:nk * tot], wsc[lk, :, 0:nk * tot], reads=[("dram", lk)])
                return
            src = dr[name][l]
            o = 0
            for c0, n in segs:
                sap = src[row0:row0 + nk * 128, c0:c0 + n].rearrange("(k p) n -> p k n", p=128)
                DMA("pool", view[:, :, o:o + n], sap)
                o += n
            if USE_WSC:
                DMA("sp", wsc[lk, :, 0:nk * tot], wb[:, 0:nk * tot], writes=[("dram", lk)])

        def wget(g, total, prefetch=True):
            while wstate["issued"] <= min(g + (wdepth(g) if prefetch else 0), total - 1):
                nx = wstate["issued"]
                if nx >= SAMPLE_G0 and wstate["gs"] is None and nx > g:
                    break
                wissue(nx)
                wstate["issued"] += 1
            name, row0, nk, segs = WBLK[g % NWB]
            tot = sum(n for _, n in segs)
            return wbuf_of(g)[:, 0:nk * tot].r("p (k n) -> p k n", k=nk)

        tiles = [("p", i) for i in range(4)] + [("s", 0)]
        if KT:
            tiles = [(t[0], int(t[1:] or 0)) for t in KT.split(",")]
        total_blocks = len(tiles) * DEPTH * NWB
        SAMPLE_G0 = 10 ** 9
        for _i, (_k, _t) in enumerate(tiles):
            if _k == "s":
                SAMPLE_G0 = _i * DEPTH * NWB
        gctr = [0]

        def nextw(prefetch=True):
            g = gctr[0]
            gctr[0] += 1
            return wget(g, total_blocks, prefetch)

        identf = CF("ident")
        identb = CB("ident")
        onesf = CF("ones")
        onesb = CB("ones")

        def rsqrt_(out, in_, scale, rows, n):
            TS("dve", out, in_, scale, EPS, ALU.mult, ALU.add)
            TT("pool", out, out, mhalf[0:rows, 0:1].bc([rows, n]), ALU.pow)

        for kind, ti in tiles:
            prompt = kind == "p"
            Q = 128 if prompt else 64
            NCH = 4 if prompt else 1
            TTK = Q * NCH
            nseq = 1 if prompt else NS
            Lq = TTK if prompt else SL
            sfx = "p" if prompt else "s"
            Lm = CF("L" + sfx, Q); Um = CF("U" + sfx, Q); SELLAST = CF("SELLAST" + sfx, Q)
            NEGb = CB("NEG" + sfx, Q); NEGTb = CB("NEGT" + sfx, Q)
            SELROW = CF("SELROW" + sfx, Q)
            if prompt:
                bm = onesf[0:Q, 0:1]; bmb = onesb[0:Q, 0:1]; bmT = onesf[0:1, 0:Q]
            else:
                bm = CF("bms", Q); bmb = CB("bms", Q); bmT = CF("bmTs", NS)
            idq_f = identf[0:Q, 0:Q]; idq_b = identb[0:Q, 0:Q]
            last_tile = prompt and ti == 3
            want_rows = last_tile or not prompt

            if not prompt and tiles.index((kind, ti)) * DEPTH * NWB == SAMPLE_G0:
                extra = [alloc(WCAP, BF16)]
                wstate["gs"] = max(SAMPLE_G0, wstate["issued"])
                wstate["lst"] = extra + [wbufs[(wstate["gs"] + k) % NWBUF] for k in range(NWBUF)]
            if prompt:
                DMA("sp", x_tok, dr["xp"][ti * 512:(ti + 1) * 512, :].rearrange("(c p) d -> p c d", p=128))
            else:
                DMA("sp", x_tok[0:Q, 0, :], dr["xs"])

            def layer_body(l):
                S = pst[l]
                m0 = mark()
                DMA("sp", prm_small, dr["prow"][l, 0:64].partition_broadcast(128))
                DMA("sp", cws, dr["cw_s"][l].rearrange("p (a b) -> p a b", a=10))
                DMA("sp", cbs, dr["cb_s"][l])
                DMA("sp", cwf, dr["cw_f"][l].rearrange("p (a b) -> p a b", a=44))
                DMA("sp", cbf, dr["cb_f"][l])
                ACT(a_bc, prm_small[:, 16:32], AF.Exp)
                TS("dve", a_bc, a_bc, -1.0)
                dtb = prm_small[:, 0:16]; Dp = prm_small[:, 32:48]; gbp = prm_small[:, 48:56]

                def prow(name, rows):
                    o, n = PR[name]
                    b = alloc(n, F32)
                    DMA("sp", b, dr["prow"][l, o:o + n].partition_broadcast(128))
                    return b[0:rows, :]

                xT = alloc((8, TTK), BF16)
                yT = alloc((8, TTK), BF16)
                qT = alloc((8, TTK), BF16)
                kT = alloc((8, TTK), BF16)
                k_tok = [alloc(D, BF16) for _ in range(NCH)]
                gi = alloc((NCH, 4), F32)
                logf = alloc((NCH, 4), F32)
                mS = mark()

                def to_featmajor(src_tok, dstT):
                    for c in range(NCH):
                        for half in range(2):
                            ps = psum(1)
                            for j in range(4):
                                kc = half * 4 + j
                                TR(ps[:, j * Q:(j + 1) * Q], src_tok[0:Q, c, kc * 128:(kc + 1) * 128], idq_f)
                            CPX(dstT[:, half * 4:half * 4 + 4, c * Q:(c + 1) * Q],
                                ps[:, 0:4 * Q].r("p (j q) -> p j q", j=4))

                to_featmajor(x_tok, xT)

                def proj_tok(src, c, wb, col0, ncols, evac, rows=None):
                    r0, r1 = (c * Q, (c + 1) * Q) if rows is None else rows
                    M = r1 - r0
                    for cb0 in range(0, ncols, 512):
                        n = min(512, ncols - cb0)
                        ps = psum(1)
                        for kc in range(8):
                            MM(ps[0:M, 0:n], src[:, kc, r0:r1], wb[:, kc, col0 + cb0:col0 + cb0 + n], kc == 0, kc == 7)
                        evac(ps[0:M, 0:n], cb0, n)

                def proj_feat(src, wb, col0, evac, m=128):
                    ps = psum(1)
                    for kc in range(8):
                        MM(ps[0:m, 0:TTK], wb[:, kc, col0:col0 + m], src[:, kc, 0:TTK], kc == 0, kc == 7)
                    evac(ps[0:m, 0:TTK])

                if prompt:
                    row_rng = (TTK - 32, TTK)
                else:
                    row_rng = (0, 64)
                RM = row_rng[1] - row_rng[0]

                def emit_rows(src, wb, col0, ncols, dst_p, dst_s, nrow, dcol0):
                    def ev(ps, cb0, n):
                        t = cur_ring[0](512, F32)
                        CPX(t[0:RM, 0:n], ps)
                        if prompt:
                            DMA("sp", dst_p[l, :, dcol0 + cb0:dcol0 + cb0 + n], t[RM - nrow:RM, 0:n])
                        else:
                            for tt in range(nrow):
                                tok = SL - nrow + tt
                                DMA("sp", dst_s[l, :, tt, dcol0 + cb0:dcol0 + cb0 + n], t[tok:64:SL, 0:n])
                    proj_tok(src, 0, wb, col0, ncols, ev, rows=row_rng)

                cut('P0')
                z_tok = [alloc(D, BF16) for _ in range(NCH)]
                xsT = alloc((8, TTK), BF16)
                BT = alloc(TTK, BF16)
                CT = alloc(TTK, BF16)
                dt_ = alloc((NCH, 16), F32)
                dta = alloc((NCH, 16), F32)
                wb = nextw()
                for c in range(NCH):
                    proj_tok(xT, c, wb, 0, 1024,
                             lambda ps, cb0, n, c=c: ACT(z_tok[c][0:Q, cb0:cb0 + n], ps, AF.Silu))
                histS = None
                if not prompt:
                    histS = alloc((10, NS * 3), F32)
                    mm = mark()
                    nat = alloc(1280, F32)
                    DMA("sp", nat[0:NS * 3, :], dr["s_sconv"][l])
                    for cc in range(10):
                        ps = psum(1)
                        TR(ps[:, 0:NS * 3], nat[0:NS * 3, cc * 128:(cc + 1) * 128], identf[0:NS * 3, 0:NS * 3])
                        CPX(histS[:, cc, :], ps[:, 0:NS * 3])
                    release(mm)

                mR = mark()
                cur_ring = [make_ring(20 * 1024)]

                def conv_chunk(ps, cc):
                    xp = cur_ring[0]((nseq, 3 + Lq), F32)
                    acc = cur_ring[0]((nseq, Lq), F32)
                    if prompt:
                        CP("act", xp[:, 0, 0:3], S["tl_s"][:, cc, :])
                    else:
                        CP("act", xp[:, :, 0:3], histS[:, cc, :].r("p (b j) -> p b j", j=3))
                    CP("act", xp[:, :, 3:3 + Lq], ps.r("p (b t) -> p b t", b=nseq))
                    ACT(acc, ps.r("p (b t) -> p b t", b=nseq), AF.Identity, scale=cws[:, cc, 3:4], bias=cbs[:, cc:cc + 1])
                    for j in (2, 1, 0):
                        STT(acc, xp[:, :, j:j + Lq], cws[:, cc, j:j + 1], acc, ALU.mult, ALU.add)
                    if cc < 8:
                        dst = xsT[:, cc, :]
                    elif cc == 8:
                        dst = BT
                    else:
                        dst = CT
                    ACT(dst.r("p (b t) -> p b t", b=nseq), acc, AF.Silu)
                    if prompt:
                        CP("act", S["tl_s"][:, cc, :], xp[:, 0, Lq:Lq + 3])

                wb = nextw()
                for cc in range(8):
                    proj_feat(xT, wb, cc * 128, lambda ps, cc=cc: conv_chunk(ps, cc))
                if want_rows:
                    emit_rows(xT, wb, 0, 1024, dr["sconv_p"], dr["sconv_s"], 3, 0)
                wb = nextw()
                for cc in (8, 9):
                    proj_feat(xT, wb, (cc - 8) * 128, lambda ps, cc=cc: conv_chunk(ps, cc))
                if want_rows:
                    emit_rows(xT, wb, 0, 256, dr["sconv_p"], dr["sconv_s"], 3, 1024)
                psd = psum(1)
                for c in range(NCH):
                    for kc in range(8):
                        MM(psd[0:Q, c * 16:(c + 1) * 16], xT[:, kc, c * Q:(c + 1) * Q], wb[:, kc, 256:272], kc == 0, kc == 7)
                mm = mark()
                tx = alloc((NCH, 16), F32); ta = alloc((NCH, 16), F32)
                TT("dve", tx[0:Q], psd[0:Q, 0:NCH * 16].r("p (c h) -> p c h", c=NCH), dtb[0:Q].us(1).bc([Q, NCH, 16]), ALU.add)
                STT(ta[0:Q], tx[0:Q], -1.0, tx[0:Q], ALU.mult, ALU.max)
                ACT(ta[0:Q], ta[0:Q], AF.Exp, scale=-1.0)
                ACT(ta[0:Q], ta[0:Q], AF.Ln, bias=1.0)
                STT(dt_[0:Q], tx[0:Q], 0.0, ta[0:Q], ALU.max, ALU.add)
                TT("dve", dta[0:Q], dt_[0:Q], a_bc[0:Q].us(1).bc([Q, NCH, 16]), ALU.mult)
                release(mm)

                release(mR)
                cut('S1')
                snw = prow("snw", Q)
                gx = alloc((NCH, 8), F32); gta = alloc((NCH, 4), F32)
                s2b = []
                for _p in range(2 if NCH > 1 else 1):
                    s2b.append({"xs_tok": alloc(D, BF16), "xdt": alloc(D, BF16), "B_tok": alloc(128, BF16),
                                "GT": alloc((16, Q), BF16), "eac": alloc(32, F32), "cdb": alloc((nseq, 8), F32),
                                "xdtw": alloc(D, BF16)})
                R = alloc((16, Q), F32); expT = alloc((16, Q), BF16); CBs = alloc((2, Q), BF16)
                Xs = None if prompt else alloc((2, NS, 8), F32)
                yo = alloc(D, F32); xsD = alloc(D, F32); ss = alloc(2, F32); y_n = alloc(D, BF16)
                if not prompt:
                    sq = [{"hT_f": alloc(512, F32), "hT_b": alloc(512, BF16),
                           "xw": alloc(D, BF16), "hn": alloc(512, F32)} for _ in range(2)]
                    nat_l = [alloc((8, 64), F32) for _ in range(6)]
                    natn_l = [alloc((8, 64), F32) for _ in range(6)]

                def s2_front(c, B):
                    cols = slice(c * Q, (c + 1) * Q)
                    psx = psum(1).bitcast(BF16)
                    for kc in range(8):
                        TR(psx[0:Q, kc * 128:(kc + 1) * 128], xsT[:, kc, cols], identb)
                    CP("act", B["xs_tok"][0:Q], psx[0:Q, :])
                    TT("dve", B["xdt"][0:Q].r("p (h e) -> p h e", h=16), psx[0:Q, :].r("p (h e) -> p h e", h=16),
                       dt_[0:Q, c, :].us(2).bc([Q, 16, 64]), ALU.mult)
                    psb = psum(1).bitcast(BF16)
                    TR(psb[0:Q, 0:128], BT[:, cols], identb)
                    CP("act", B["B_tok"][0:Q], psb[0:Q, 0:128])
                    TT("pool", R[0:Q], Lm.us(1).bc([Q, 16, Q]), dta[0:Q, c, :].us(2).bc([Q, 16, Q]), ALU.mult)
                    yield
                    nbk = 16 * Q // 512
                    SEG = psum(nbk)
                    Rf = R[0:Q].r("p h l -> p (h l)")
                    for bk in range(nbk):
                        MM(SEG[0:Q, bk * 512:(bk + 1) * 512], Um, Rf[:, bk * 512:(bk + 1) * 512], True, False)
                        MM(SEG[0:Q, bk * 512:(bk + 1) * 512], idq_b, NEGb, False, True)
                    ACT(expT[0:Q].r("p h l -> p (h l)"), SEG[0:Q, 0:16 * Q], AF.Exp)
                    psc = psum(2)
                    for g in range(2):
                        MM(psc[0:Q, g * 512:g * 512 + Q], BT[g * 64:(g + 1) * 64, cols], CT[g * 64:(g + 1) * 64, cols])
                    pt = psum(1)
                    MM(pt[0:Q, 0:16], Lm, dta[0:Q, c, :])
                    MM(pt[0:Q, 16:32], Um, dta[0:Q, c, :])
                    pcd = psum(1)
                    if prompt:
                        for g in range(2):
                            MM(pcd[g * 64:(g + 1) * 64, 0:8], onesf[0:Q, 0:64], dta[0:Q, c, g * 8:(g + 1) * 8])
                    else:
                        for g in range(2):
                            TT("dve", Xs[0:Q, g], bm.us(2).bc([Q, NS, 8]),
                               dta[0:Q, c, g * 8:(g + 1) * 8].us(1).bc([Q, NS, 8]), ALU.mult)
                            MM(pcd[g * 64:(g + 1) * 64, 0:NS * 8], onesf[0:Q, 0:64], Xs[0:Q, g].r("p b h -> p (b h)"))
                    yield
                    CP("dve", CBs[0:Q], psc[0:Q, :].r("p (g x) -> p g x", g=2)[:, :, 0:Q])
                    ACT(B["eac"][0:Q], pt[0:Q, 0:32], AF.Exp)
                    ACT(B["cdb"].r("p b h -> p (b h)"), pcd[:, 0:nseq * 8], AF.Exp)
                    TT("dve", B["GT"][0:Q].r("p (g k) l -> p g k l", g=2), expT[0:Q].r("p (g k) l -> p g k l", g=2),
                       CBs[0:Q].us(2).bc([Q, 2, 8, Q]), ALU.mult)
                    TT("dve", B["xdtw"][0:Q].r("p (h e) -> p h e", h=16), B["xdt"][0:Q].r("p (h e) -> p h e", h=16),
                       B["eac"][0:Q, 16:32].us(2).bc([Q, 16, 64]), ALU.mult)
                    yield

                def s2_back(c, B):
                    cols = slice(c * Q, (c + 1) * Q)
                    eac, cdb, xdtw, xdt, GT, xs_tok, B_tok = (B["eac"], B["cdb"], B["xdtw"], B["xdt"], B["GT"],
                                                             B["xs_tok"], B["B_tok"])
                    def load_nat(b2):
                        nv = nat_l[b2 % 6].r("p (j g) n -> p j g n", g=2)
                        for g in range(2):
                            DMA("sp", nv[:, :, g, :],
                                dr["s_ssd"][l, b2][g * 512:(g + 1) * 512, :].rearrange("(j q) n -> q j n", q=128))

                    if not prompt:
                        for b2 in range(3):
                            load_nat(b2)
                    for b in range(nseq):
                        if prompt:
                            hT_f, hT_b = S["hT_f"], S["hT_b"]
                        else:
                            if b + 3 < nseq:
                                load_nat(b + 3)
                            sb_ = sq[b % 2]
                            nat, hT_f, hT_b = nat_l[b % 6], sb_["hT_f"], sb_["hT_b"]
                            natv = nat.r("p (j g) n -> p j g n", g=2)
                            pn = psum(1)
                            for jj in range(4):
                                TR(pn[:, jj * 128:(jj + 1) * 128], natv[:, jj].r("p g n -> p (g n)"), identf)
                            CP("dve", hT_f, pn)
                            CP("act", hT_b, pn)
                        YO = psum(2)
                        for g in range(2):
                            MM(YO[0:Q, g * 512:(g + 1) * 512], CT[g * 64:(g + 1) * 64, cols], hT_b[g * 64:(g + 1) * 64, :])
                        if prompt:
                            TT("dve", yo[0:Q].r("p (h e) -> p h e", h=16), YO[0:Q, :].r("p (h e) -> p h e", h=16),
                               eac[0:Q, 0:16].us(2).bc([Q, 16, 64]), ALU.mult)
                            xw = xdtw
                        else:
                            if b == 0:
                                TS("dve", yo[0:Q], YO[0:Q, :], bm[:, b:b + 1])
                            else:
                                STT(yo[0:Q], YO[0:Q, :], bm[:, b:b + 1], yo[0:Q], ALU.mult, ALU.add)
                            xw = sb_["xw"]
                            TS("dve", xw[0:Q], xdtw[0:Q], bm[:, b:b + 1])
                        HL = psum(1)
                        for g in range(2):
                            MM(HL[g * 64:(g + 1) * 64, :], B_tok[0:Q, g * 64:(g + 1) * 64], xw[0:Q, g * 512:(g + 1) * 512])
                        hn = hT_f if prompt else sb_["hn"]
                        TT("dve", hn.r("p (h e) -> p h e", h=8), hT_f.r("p (h e) -> p h e", h=8),
                           cdb[:, b, :].us(2).bc([128, 8, 64]), ALU.mult)
                        TT("dve", hn, hn, HL, ALU.add)
                        if prompt:
                            CP("act", hT_b, hn)
                        else:
                            po = psum(1)
                            for jj in range(4):
                                TR(po[:, jj * 128:(jj + 1) * 128], hn[:, jj * 128:(jj + 1) * 128], identf)
                            natn = natn_l[b % 6]
                            CP("act", natn.r("p k n -> p (k n)"), po)
                            natnv = natn.r("p (j g) n -> p j g n", g=2)
                            for g in range(2):
                                DMA("sp", dr["ssd_s"][l, b][g * 512:(g + 1) * 512, :].rearrange("(j q) n -> q j n", q=128),
                                    natnv[:, :, g, :])
                        yield
                    if not prompt:
                        TT("dve", yo[0:Q].r("p (h e) -> p h e", h=16), yo[0:Q].r("p (h e) -> p h e", h=16),
                           eac[0:Q, 0:16].us(2).bc([Q, 16, 64]), ALU.mult)
                    TT("pool", xsD[0:Q].r("p (h e) -> p h e", h=16), xs_tok[0:Q].r("p (h e) -> p h e", h=16),
                       Dp[0:Q].us(2).bc([Q, 16, 64]), ALU.mult)
                    Y = psum(2)
                    for h in range(16):
                        MM(Y[0:Q, h * 64:(h + 1) * 64], GT[0:Q, h, :], xdt[0:Q, h * 64:(h + 1) * 64])
                    TT("dve", yo[0:Q], yo[0:Q], Y[0:Q, :], ALU.add)
                    TT("dve", yo[0:Q], yo[0:Q], xsD[0:Q], ALU.add)
                    TT("dve", yo[0:Q], yo[0:Q], z_tok[c][0:Q], ALU.mult)
                    ACT(xsD[0:Q], yo[0:Q], AF.Square, accum=ss[0:Q, 0:1])
                    rsqrt_(ss[0:Q, 1:2], ss[0:Q, 0:1], 1.0 / 1024, Q, 1)
                    STT(y_n[0:Q], yo[0:Q], ss[0:Q, 1:2], snw, ALU.mult, ALU.mult)
                    yield
                    pyt = psum(1).bitcast(BF16)
                    for kc in range(8):
                        TR(pyt[:, kc * Q:(kc + 1) * Q], y_n[0:Q, kc * 128:(kc + 1) * 128], idq_b)
                    CP("act", yT[:, :, cols], pyt[:, 0:8 * Q].r("p (k q) -> p k q", k=8))
                    yield

                def s2_stream():
                    yield from s2_front(0, s2b[0])
                    for c in range(NCH):
                        if c + 1 < NCH:
                            yield from s2_front(c + 1, s2b[(c + 1) % 2])
                        yield from s2_back(c, s2b[c % 2])
                    if last_tile:
                        po = psum(2)
                        for blk in range(8):
                            g, jj = blk // 4, blk % 4
                            TR(po[:, g * 512 + jj * 64:g * 512 + (jj + 1) * 64], S["hT_f"][g * 64:(g + 1) * 64, jj * 128:(jj + 1) * 128],
                               identf[g * 64:(g + 1) * 64, g * 64:(g + 1) * 64])
                        natn = yo.r("p (k n) -> p k n", k=16)[:, 0:8, :]
                        CP("act", natn.r("p (g j) n -> p g (j n)", g=2), po.r("p (g x) -> p g x", g=2)[:, :, 0:256])
                        DMA("sp", dr["ssd_p"][l].rearrange("(k q) n -> q k n", q=128), natn)

                def m1_stream():
                    wb = nextw()
                    for cc in range(8):
                        proj_feat(xT, wb, cc * 128, lambda ps, cc=cc: CP("act", qT[:, cc, :], ps))
                        yield
                    wb = nextw()
                    for cc in range(8):
                        proj_feat(xT, wb, cc * 128, lambda ps, cc=cc: ACT(kT[:, cc, :], ps, AF.Copy, scale=0.0625))
                        yield
                    for c in range(NCH):
                        for hf in range(2):
                            proj_tok(xT, c, wb, hf * 512, 512,
                                     lambda ps, cb0, n, c=c, hf=hf: ACT(k_tok[c][0:Q, hf * 512:hf * 512 + n], ps, AF.Copy, scale=0.0625))
                            yield
                    wbv[0] = nextw()
                    wb = wbv[0]
                    psg = psum(1)
                    for c in range(NCH):
                        for kc in range(8):
                            MM(psg[0:Q, c * 8:(c + 1) * 8], xT[:, kc, c * Q:(c + 1) * Q], wb[:, kc, 1024:1032], kc == 0, kc == 7)
                    TT("dve", gx[0:Q], psg[0:Q, 0:NCH * 8].r("p (c h) -> p c h", c=NCH), gbp[0:Q].us(1).bc([Q, NCH, 8]), ALU.add)
                    yield
                    CP("dve", gi[0:Q], gx[0:Q, :, 0:4])
                    STT(gta[0:Q], gx[0:Q, :, 4:8], -1.0, gx[0:Q, :, 4:8], ALU.mult, ALU.max)
                    ACT(gta[0:Q], gta[0:Q], AF.Exp, scale=-1.0)
                    ACT(gta[0:Q], gta[0:Q], AF.Ln, bias=1.0)
                    STT(logf[0:Q], gx[0:Q, :, 4:8], 0.0, gta[0:Q], ALU.min, ALU.subtract)
                    yield

                wbv = [None]
                run_streams([(s2_stream(), list(range(6))), (m1_stream(), [6, 7])])
                release(mS)
                cut('M1')
                hmT = alloc((8, TTK), BF16)
                v_tok = [alloc(D, BF16) for _ in range(NCH)]
                o_tok = [alloc(D, BF16) for _ in range(NCH)]
                mnw = prow("mnw", Q)
                TS("dve", mnw, mnw, 0.5)

                mT = alloc((8, TTK), BF16)
                sgA = [alloc(TTK, BF16) for _ in range(2)]
                mM2 = mark()
                if prompt:
                    nT_f, nT_b, m_st = S["nT_f"], S["nT_b"], S["m_st"][0:1, :]
                else:
                    nT_f = alloc((2, 4, NS), F32); nT_b = alloc((2, 4, NS), BF16); m_st = alloc(4, F32)[0:NS, :]
                    mm = mark()
                    nat = alloc(256, F32)
                    DMA("sp", nat[0:64, :], dr["s_n"][l])
                    DMA("sp", m_st, dr["s_m"][l])
                    for dc in range(2):
                        ps = psum(1)
                        TR(ps[:, 0:64], nat[0:64, dc * 128:(dc + 1) * 128], identf[0:64, 0:64])
                        CP("dve", nT_f[:, dc].r("p h b -> p b h"), ps[:, 0:64].r("p (b h) -> p b h", h=4))
                    CP("act", nT_b.r("p a h b -> p (a h b)"), nT_f.r("p a h b -> p (a h b)"))
                    release(mm)
                m2b = []
                for _p in range(2 if NCH > 1 else 1):
                    m2b.append({"sm": alloc(48, F32), "gs": alloc(8, F32), "SwT": alloc((4, Q), BF16), "kw": alloc(D, BF16),
                                "wcb": alloc((nseq, 4), F32), "mprev": alloc(4, F32), "pes": alloc(12, F32)})
                R2 = alloc((4, Q), F32); Wt = alloc((4, Q), BF16); Sw = alloc((4, Q), BF16)
                ws = alloc(4, F32); wcs = alloc(4, F32); Z = alloc((nseq, 4), F32)
                ni = alloc(D, F32); denI = alloc(4, F32); emt = alloc(4, F32); junk = alloc(256, BF16)
                ow = alloc(D, F32); hmn = alloc(D, BF16)
                tmpd = None if prompt else alloc((4, NS), F32)
                if not prompt:
                    cq = [{"C_b": alloc((2, 4, 256), BF16), "kwm": alloc(D, BF16)} for _ in range(2)]
                    Cf_l = [alloc((2, 4, 256), F32) for _ in range(5)]

                def m2_front(c, B):
                    cols = slice(c * Q, (c + 1) * Q)
                    sm, gs, SwT, kw, wcb, mprev, pes = B["sm"], B["gs"], B["SwT"], B["kw"], B["wcb"], B["mprev"], B["pes"]
                    pg = psum(1)
                    MM(pg[0:Q, 0:4], Lm, logf[0:Q, c, :])
                    MM(pg[0:Q, 4:8], bmT, m_st)
                    CP("dve", gs[0:Q], pg[0:Q, 0:8])
                    bcum = gs[0:Q, 0:4]; m_tok = gs[0:Q, 4:8]
                    a_ = sm[0:Q, 0:4]; mloc = sm[0:Q, 4:8]; mxx = sm[0:Q, 8:12]; nmxx = sm[0:Q, 12:16]
                    wint = sm[0:Q, 16:20]; den = sm[0:Q, 20:24]; mt = sm[0:Q, 24:28]
                    TT("dve", a_, gi[0:Q, c, :], bcum, ALU.subtract)
                    TT("pool", R2[0:Q], idq_f.us(1).bc([Q, 4, Q]), a_.us(2).bc([Q, 4, Q]), ALU.mult)
                    yield
                    A = psum(1)
                    MM(A[0:Q, 0:4 * Q], onesf[0:Q, 0:Q], R2[0:Q].r("p h s -> p (h s)"), True, False)
                    MM(A[0:Q, 0:4 * Q], idq_b, NEGTb, False, True)
                    RED(mloc, A[0:Q, 0:4 * Q].r("p (h s) -> p h s", h=4), ALU.max)
                    TT("dve", mxx, mloc, m_tok, ALU.max)
                    TS("dve", nmxx, mxx, -1.0)
                    TT("dve", mt, bcum, mxx, ALU.add)
                    for h in range(4):
                        ACT(Wt[0:Q, h, :], A[0:Q, h * Q:(h + 1) * Q], AF.Exp, bias=nmxx[:, h:h + 1])
                    pe_ = psum(1)
                    MM(pe_[0:Q, 0:4], SELLAST, mxx)
                    MM(pe_[0:nseq, 4:8], SELROW, mt)
                    MM(pe_[0:nseq, 8:12], SELROW, mxx)
                    CP("dve", pes[0:Q, 0:4], pe_[0:Q, 0:4])
                    CP("dve", pes[0:nseq, 4:12], pe_[0:nseq, 4:12])
                    CP("dve", mprev[0:nseq], m_st)
                    CP("dve", m_st, pes[0:nseq, 4:8])
                    yield
                    QK = psum(1)
                    for h in range(4):
                        for dc in range(2):
                            MM(QK[0:Q, h * Q:(h + 1) * Q], qT[:, h * 2 + dc, cols], kT[:, h * 2 + dc, cols], dc == 0, dc == 1)
                    TT("dve", Sw[0:Q].r("p h s -> p (h s)"), Wt[0:Q].r("p h s -> p (h s)"), QK[0:Q, 0:4 * Q], ALU.mult)
                    TT("dve", wint, m_tok, mxx, ALU.subtract)
                    ACT(wint, wint, AF.Exp)
                    TT("dve", ws[0:Q], a_, pes[0:Q, 0:4], ALU.subtract)
                    ACT(ws[0:Q], ws[0:Q], AF.Exp)
                    TT("dve", kw[0:Q].r("p (h e) -> p h e", h=4), k_tok[c][0:Q].r("p (h e) -> p h e", h=4),
                       ws[0:Q].us(2).bc([Q, 4, 256]), ALU.mult)
                    TT("dve", wcs[0:nseq], mprev[0:nseq], pes[0:nseq, 8:12], ALU.subtract)
                    ACT(wcs[0:nseq], wcs[0:nseq], AF.Exp)
                    TT("dve", Z[0:nseq], identf[0:nseq, 0:nseq].us(2).bc([nseq, nseq, 4]),
                       wcs[0:nseq].us(1).bc([nseq, nseq, 4]), ALU.mult)
                    yield
                    pw = psum(1).bitcast(BF16)
                    for h in range(4):
                        TR(pw[0:Q, h * Q:(h + 1) * Q], Sw[0:Q, h, :], idq_b)
                    CP("act", SwT[0:Q].r("p h s -> p (h s)"), pw[0:Q, 0:4 * Q])
                    pwc = psum(1)
                    MM(pwc[:, 0:nseq * 4], onesf[0:nseq, 0:128], Z[0:nseq].r("p b h -> p (b h)"))
                    CP("act", wcb.r("p b h -> p (b h)"), pwc[:, 0:nseq * 4])
                    yield

                def m2_back(c, B):
                    cols = slice(c * Q, (c + 1) * Q)
                    sm, gs, SwT, kw, wcb = B["sm"], B["gs"], B["SwT"], B["kw"], B["wcb"]
                    bcum = gs[0:Q, 0:4]
                    wint = sm[0:Q, 16:20]; den = sm[0:Q, 20:24]; mt = sm[0:Q, 24:28]; r_ = sm[0:Q, 28:32]
                    ssq = sm[0:Q, 32:36]; rstd = sm[0:Q, 36:40]
                    pden = psum(1)
                    for h in range(4):
                        MM(pden[0:Q, h:h + 1], SwT[0:Q, h, :], onesb[0:Q, 0:1])
                    for h in range(4):
                        for dc in range(2):
                            MM(pden[0:Q, 8 + h * nseq:8 + (h + 1) * nseq], qT[:, h * 2 + dc, cols], nT_b[:, dc, h, :], dc == 0, dc == 1)
                    if prompt:
                        CP("dve", denI[0:Q], pden[0:Q, 8:12])
                    else:
                        TT("dve", tmpd[0:Q], pden[0:Q, 8:8 + 4 * NS].r("p (h b) -> p h b", h=4), bm.us(1).bc([Q, 4, NS]), ALU.mult)
                        RED(denI[0:Q], tmpd[0:Q], ALU.add)
                    TT("dve", denI[0:Q], denI[0:Q], wint, ALU.mult)
                    TT("dve", den, denI[0:Q], pden[0:Q, 0:4], ALU.add)
                    def load_C(b2):
                        for a2 in range(2):
                            DMA("sp", Cf_l[b2 % 5][:, a2], dr["s_c"][l, b2][:, a2 * 128:(a2 + 1) * 128, :].rearrange("h q e -> q h e"))

                    if not prompt:
                        for b2 in range(3):
                            load_C(b2)
                    for b in range(nseq):
                        if prompt:
                            C_f, C_b = S["C_f"], S["C_b"]
                            kwm = kw
                        else:
                            if b + 3 < nseq:
                                load_C(b + 3)
                            cb2 = cq[b % 2]
                            C_f, C_b, kwm = Cf_l[b % 5], cb2["C_b"], cb2["kwm"]
                            CP("act", C_b.r("p a h e -> p (a h e)"), C_f.r("p a h e -> p (a h e)"))
                            TS("dve", kwm[0:Q], kw[0:Q], bm[:, b:b + 1])
                        NI = psum(2)
                        for h in range(4):
                            for dc in range(2):
                                MM(NI[0:Q, h * 256:(h + 1) * 256], qT[:, h * 2 + dc, cols], C_b[:, dc, h, :], dc == 0, dc == 1)
                        if prompt:
                            TT("dve", ni[0:Q].r("p (h e) -> p h e", h=4), NI[0:Q, :].r("p (h e) -> p h e", h=4),
                               wint.us(2).bc([Q, 4, 256]), ALU.mult)
                        elif b == 0:
                            TS("dve", ni[0:Q], NI[0:Q, :], bm[:, b:b + 1])
                        else:
                            STT(ni[0:Q], NI[0:Q, :], bm[:, b:b + 1], ni[0:Q], ALU.mult, ALU.add)
                        for dc in range(2):
                            Cn = psum(2)
                            for h in range(4):
                                MM(Cn[:, h * 256:(h + 1) * 256], kwm[0:Q, h * 256 + dc * 128:h * 256 + (dc + 1) * 128],
                                   v_tok[c][0:Q, h * 256:(h + 1) * 256])
                            for h in range(4):
                                STT(C_f[:, dc, h, :], C_f[:, dc, h, :], wcb[:, b, h:h + 1], Cn[:, h * 256:(h + 1) * 256],
                                    ALU.mult, ALU.add)
                        if prompt:
                            CP("act", C_b.r("p a h e -> p (a h e)"), C_f.r("p a h e -> p (a h e)"))
                        else:
                            for a2 in range(2):
                                DMA("sp", dr["c_s"][l, b][:, a2 * 128:(a2 + 1) * 128, :].rearrange("h q e -> q h e"), C_f[:, a2])
                        yield
                    if not prompt:
                        TT("dve", ni[0:Q].r("p (h e) -> p h e", h=4), ni[0:Q].r("p (h e) -> p h e", h=4),
                           wint.us(2).bc([Q, 4, 256]), ALU.mult)
                    pn2 = psum(1)
                    for dc in range(2):
                        for h in range(4):
                            MM(pn2[:, (dc * 4 + h) * nseq:(dc * 4 + h + 1) * nseq],
                               kw[0:Q, h * 256 + dc * 128:h * 256 + (dc + 1) * 128], bmb)
                    for dc in range(2):
                        TT("dve", nT_f[:, dc], nT_f[:, dc], wcb.r("p b h -> p h b"), ALU.mult)
                    TT("dve", nT_f.r("p a h b -> p (a h b)"), nT_f.r("p a h b -> p (a h b)"), pn2[:, 0:8 * nseq], ALU.add)
                    CP("act", nT_b.r("p a h b -> p (a h b)"), nT_f.r("p a h b -> p (a h b)"))
                    NUM = psum(2)
                    for h in range(4):
                        MM(NUM[0:Q, h * 256:(h + 1) * 256], SwT[0:Q, h, :], v_tok[c][0:Q, h * 256:(h + 1) * 256])
                    TT("dve", ni[0:Q], ni[0:Q], NUM[0:Q, :], ALU.add)
                    ACT(emt[0:Q], mt, AF.Exp, scale=-1.0)
                    STT(den, den, -1.0, den, ALU.mult, ALU.max)
                    TT("dve", den, den, emt[0:Q], ALU.max)
                    P.rec("dve", lambda e, o=r_.ap, i=den.ap: e.reciprocal(out=o, in_=i), reads=[den], writes=[r_])
                    TT("dve", ni[0:Q].r("p (h e) -> p h e", h=4), ni[0:Q].r("p (h e) -> p h e", h=4),
                       r_.us(2).bc([Q, 4, 256]), ALU.mult)
                    for h in range(4):
                        ACT(junk[0:Q], ni[0:Q, h * 256:(h + 1) * 256], AF.Square, accum=ssq[:, h:h + 1])
                    rsqrt_(rstd, ssq, 1.0 / 256, Q, 4)
                    STT(ow[0:Q], o_tok[c][0:Q], 1.0, mnw, ALU.add, ALU.mult)
                    for h in range(4):
                        STT(hmn[0:Q, h * 256:(h + 1) * 256], ni[0:Q, h * 256:(h + 1) * 256], rstd[:, h:h + 1],
                            ow[0:Q, h * 256:(h + 1) * 256], ALU.mult, ALU.mult)
                    yield
                    pht = psum(1).bitcast(BF16)
                    for kc in range(8):
                        TR(pht[:, kc * Q:(kc + 1) * Q], hmn[0:Q, kc * 128:(kc + 1) * 128], idq_b)
                    CP("act", hmT[:, :, cols], pht[:, 0:8 * Q].r("p (k q) -> p k q", k=8))
                    yield

                def m2_stream():
                    yield from m2_front(0, m2b[0])
                    for c in range(NCH):
                        if c + 1 < NCH:
                            yield from m2_front(c + 1, m2b[(c + 1) % 2])
                        yield from m2_back(c, m2b[c % 2])
                    if last_tile or not prompt:
                        nrow = 4 * nseq
                        po = psum(1)
                        tmpn = ni.r("p (a x) -> p a x", a=2)[:, :, 0:nseq * 4].r("p a (b h) -> p a b h", h=4)
                        for dc in range(2):
                            CP("dve", tmpn[:, dc], nT_f[:, dc].r("p h b -> p b h"))
                            TR(po[0:nrow, dc * 128:(dc + 1) * 128], tmpn[:, dc].r("p b h -> p (b h)"), identf)
                        natn = ow[:, 0:256]
                        CP("act", natn[0:nrow], po[0:nrow, 0:256])
                        DMA("sp", dr["n_p"][l] if prompt else dr["n_s"][l], natn[0:nrow])
                        DMA("sp", dr["m_p"][l] if prompt else dr["m_s"][l], m_st)
                        if prompt:
                            for a2 in range(2):
                                DMA("sp", dr["c_p"][l][:, a2 * 128:(a2 + 1) * 128, :].rearrange("h q e -> q h e"), S["C_f"][:, a2])

                def ga_stream():
                    wbo = None
                    for c in range(NCH):
                        for hf in range(2):
                            proj_tok(xT, c, wbv[0], hf * 512, 512,
                                     lambda ps, cb0, n, c=c, hf=hf: CP("act", v_tok[c][0:Q, hf * 512:hf * 512 + n], ps))
                            yield
                        if wbo is None:
                            wbo = nextw(prefetch=False)
                        for hf in range(2):
                            proj_tok(xT, c, wbo, hf * 512, 512,
                                     lambda ps, cb0, n, c=c, hf=hf: ACT(o_tok[c][0:Q, hf * 512:hf * 512 + n], ps, AF.Tanh, scale=0.5))
                            yield
                    wa = nextw()
                    wg = nextw(prefetch=False)
                    for j in range(8):
                        pb = psum(1); pgt = psum(1)
                        for kc in range(8):
                            MM(pgt[:, 0:TTK], wg[:, kc, j * 128:(j + 1) * 128], xT[:, kc, :], kc == 0, kc == 7)
                        for kc in range(8):
                            MM(pb[:, 0:TTK], wa[:, kc, j * 128:(j + 1) * 128], yT[:, kc, :], kc == 0, kc == 7)
                        sg = sgA[j % 2]
                        ACT(sg, pgt[:, 0:TTK], AF.Tanh, scale=0.5)
                        STT(mT[:, j, :], sg, 1.0, pb[:, 0:TTK], ALU.add, ALU.mult)
                        yield

                run_streams([(m2_stream(), list(range(6))), (ga_stream(), [6, 7])])
                release(mM2)
                cut('M2')
                cur_ring[0] = make_ring(12 * 1024)
                wa = nextw()
                wg = nextw(prefetch=False)
                for j in range(8):
                    pb = psum(1); pgt = psum(1)
                    for kc in range(8):
                        MM(pgt[:, 0:TTK], wg[:, kc, j * 128:(j + 1) * 128], xT[:, kc, :], kc == 0, kc == 7)
                    for kc in range(8):
                        MM(pb[:, 0:TTK], wa[:, kc, j * 128:(j + 1) * 128], hmT[:, kc, :], kc == 0, kc == 7)
                    sg = sgA[j % 2]
                    ACT(sg, pgt[:, 0:TTK], AF.Tanh, scale=0.5)
                    t2 = cur_ring[0](TTK, F32)
                    STT(t2, sg, 1.0, pb[:, 0:TTK], ALU.add, ALU.mult)
                    TT("dve", mT[:, j, :], mT[:, j, :], t2, ALU.add)

                def layer_norm(c, pss, g_, b_, mixscale=1.0):
                    st6 = cur_ring[0](12, F32); mv = cur_ring[0](4, F32)
                    for hf in range(2):
                        xv = x_tok[0:Q, c, hf * 512:(hf + 1) * 512]
                        if mixscale == 1.0:
                            STT(xv, xv, ALPHA, pss[hf], ALU.mult, ALU.add)
                        else:
                            TS("dve", xv, xv, ALPHA)
                            STT(xv, pss[hf], mixscale, xv, ALU.mult, ALU.add)
                        P.rec("dve", lambda e, o=st6[0:Q, hf * 6:(hf + 1) * 6].ap, i=xv.ap: e.bn_stats(out=o, in_=i),
                              reads=[xv], writes=[st6])
                    P.rec("dve", lambda e, o=mv[0:Q, 0:2].ap, i=st6[0:Q].ap: e.bn_aggr(out=o, in_=i), reads=[st6], writes=[mv])
                    rsqrt_(mv[0:Q, 2:3], mv[0:Q, 1:2], 1.0, Q, 1)
                    xv = x_tok[0:Q, c, :]
                    TS("dve", xv, xv, mv[0:Q, 0:1], mv[0:Q, 2:3], ALU.subtract, ALU.mult)
                    TT("dve", xv, xv, g_, ALU.mult)
                    TT("dve", xv, xv, b_, ALU.add)

                wb = nextw()
                l1g = prow("l1g", Q); l1b = prow("l1b", Q)
                for c in range(NCH):
                    pss = []
                    for hf in range(2):
                        ps = psum(1)
                        for kc in range(8):
                            MM(ps[0:Q, :], mT[:, kc, c * Q:(c + 1) * Q], wb[:, kc, hf * 512:(hf + 1) * 512], kc == 0, kc == 7)
                        pss.append(ps[0:Q, :])
                    layer_norm(c, pss, l1g, l1b, mixscale=0.5)

                cut('G')
                release(m0)
                m0b = mark()
                x1T = alloc((8, TTK), BF16)
                to_featmajor(x_tok, x1T)
                hT = alloc((22, TTK), BF16)
                histF = None
                if not prompt:
                    histF = alloc((44, NS * 2), F32)
                    for piece in range(4):
                        mm = mark()
                        nat = alloc(1408, F32)
                        DMA("sp", nat[0:NS * 2, :], dr["s_fconv"][l, :, piece * 1408:(piece + 1) * 1408])
                        for k in range(11):
                            cc = piece * 11 + k
                            ps = psum(1)
                            TR(ps[:, 0:NS * 2], nat[0:NS * 2, k * 128:(k + 1) * 128], identf[0:NS * 2, 0:NS * 2])
                            CPX(histF[:, cc, :], ps[:, 0:NS * 2])
                        release(mm)

                cur_ring[0] = make_ring(40 * 1024)

                def ffn_conv(ps, cc, silu):
                    xp = cur_ring[0]((nseq, 2 + Lq), F32)
                    acc = cur_ring[0]((nseq, Lq), F32)
                    if prompt:
                        CP("act", xp[:, 0, 0:2], S["tl_f"][:, cc, :])
                    else:
                        CP("act", xp[:, :, 0:2], histF[:, cc, :].r("p (b j) -> p b j", j=2))
                    CP("act", xp[:, :, 2:2 + Lq], ps.r("p (b t) -> p b t", b=nseq))
                    ACT(acc, ps.r("p (b t) -> p b t", b=nseq), AF.Identity, scale=cwf[:, cc, 2:3], bias=cbf[:, cc:cc + 1])
                    for j in (1, 0):
                        STT(acc, xp[:, :, j:j + Lq], cwf[:, cc, j:j + 1], acc, ALU.mult, ALU.add)
                    if prompt:
                        CP("act", S["tl_f"][:, cc, :], xp[:, 0, Lq:Lq + 2])
                    return acc

                for bb in range(6):
                    nj = 4 if bb < 5 else 2
                    n = nj * 128
                    wb = nextw()
                    for jj in range(nj):
                        j = bb * 4 + jj
                        pg_ = psum(1); pv_ = psum(1)
                        for kc in range(8):
                            MM(pg_[:, 0:TTK], wb[:, kc, jj * 128:(jj + 1) * 128], x1T[:, kc, :], kc == 0, kc == 7)
                        for kc in range(8):
                            MM(pv_[:, 0:TTK], wb[:, kc, n + jj * 128:n + (jj + 1) * 128], x1T[:, kc, :], kc == 0, kc == 7)
                        ag = ffn_conv(pg_[:, 0:TTK], j, True)
                        av = ffn_conv(pv_[:, 0:TTK], 22 + j, False)
                        sg = cur_ring[0]((nseq, Lq), BF16)
                        ACT(sg, ag, AF.Silu)
                        TT("dve", hT[:, j, :].r("p (b t) -> p b t", b=nseq), sg, av, ALU.mult)
                    if want_rows:
                        emit_rows(x1T, wb, 0, n, dr["fconv_p"], dr["fconv_s"], 2, bb * 512)
                        emit_rows(x1T, wb, n, n, dr["fconv_p"], dr["fconv_s"], 2, DFF + bb * 512)
                l2g = prow("l2g", Q); l2b = prow("l2b", Q)
                for cb0 in range(2):
                    pss = [psum(1) for _ in range(NCH)]
                    for kh in range(2):
                        wb = nextw()
                        for c in range(NCH):
                            for k in range(11):
                                kc = kh * 11 + k
                                MM(pss[c][0:Q, :], hT[:, kc, c * Q:(c + 1) * Q], wb[:, k, :], kc == 0, kc == 21)
                    if cb0 == 0:
                        keep = []
                        for c in range(NCH):
                            t = alloc(512, F32)
                            CPX(t[0:Q], pss[c][0:Q, :])
                            keep.append(t[0:Q])
                    else:
                        for c in range(NCH):
                            layer_norm(c, [keep[c], pss[c][0:Q, :]], l2g, l2b)
                release(m0b)
                release(m0)

            for l in range(DEPTH):
                if l >= KL:
                    continue
                base_m = mark()
                try:
                    layer_body(l)
                except _Cut:
                    pass
                release(base_m)
                gctr[0] = (tiles.index((kind, ti)) * DEPTH + l + 1) * NWB
                wstate["issued"] = max(wstate["issued"], gctr[0])
            if prompt:
                DMA("sp", dr["y_p"][ti * 512:(ti + 1) * 512, :].rearrange("(c p) d -> p c d", p=128), x_tok)
            else:
                DMA("sp", dr["y_s"], x_tok[0:Q, 0, :])

        print("final peak", astate.get("peak"))
        P.emit()
        build_program.stats = P.stats
    return nc


_NC_CACHE = {}


def _get_nc():
    if "nc" not in _NC_CACHE:
        _NC_CACHE["nc"] = build_program()
    return _NC_CACHE["nc"]


def kernel(x_prompt, x_sample, state_ssd, state_ssd_conv, state_mlstm_c, state_mlstm_n, state_mlstm_m,
           state_ffn_conv, w_in, ssd_conv_w, ssd_conv_b, ssd_dt_bias, ssd_a_log, ssd_d, ssd_norm_w,
           mlstm_gate_b, mlstm_norm_w, w_branch_a, w_branch_b, w_out, ln1_g, ln1_b, ffn_w_up, ffn_conv_w,
           ffn_conv_b, ffn_w_down, ln2_g, ln2_b):
    f = lambda a: np.ascontiguousarray(np.asarray(a, dtype=np.float32))
    nc = _get_nc()
    prow = np.zeros((DEPTH, NPR), np.float32)
    for name, arr in (("dtb", ssd_dt_bias), ("alog", ssd_a_log), ("D", ssd_d), ("gb", mlstm_gate_b),
                      ("snw", ssd_norm_w), ("mnw", mlstm_norm_w), ("l1g", ln1_g), ("l1b", ln1_b),
                      ("l2g", ln2_g), ("l2b", ln2_b)):
        o, n = PR[name]
        prow[:, o:o + n] = f(arr)
    cw_s = f(ssd_conv_w).reshape(DEPTH, 4, 10, 128).transpose(0, 3, 2, 1).reshape(DEPTH, 128, 40)
    cb_s = f(ssd_conv_b).reshape(DEPTH, 10, 128).transpose(0, 2, 1)
    cw_f = f(ffn_conv_w).reshape(DEPTH, 3, 44, 128).transpose(0, 3, 2, 1).reshape(DEPTH, 128, 132)
    cb_f = f(ffn_conv_b).reshape(DEPTH, 44, 128).transpose(0, 2, 1)
    consts = _make_consts()
    shared = {"w_in": f(w_in), "w_a": f(w_branch_a), "w_b": f(w_branch_b), "w_out": f(w_out), "w_up": f(ffn_w_up),
              "w_down": f(ffn_w_down), "prow": prow, "cw_s": f(cw_s), "cb_s": f(cb_s), "cw_f": f(cw_f),
              "cb_f": f(cb_f), "consts": consts}
    xp = f(x_prompt); xs = f(x_sample)
    s_ssd = f(state_ssd); s_sc = f(state_ssd_conv); s_c = f(state_mlstm_c); s_n = f(state_mlstm_n)
    s_m = f(state_mlstm_m); s_fc = f(state_ffn_conv)
    in_maps = []
    for i in range(NCORES):
        sl = slice(i * NS, (i + 1) * NS)
        m = dict(shared)
        m["xp"] = xp[i]
        m["xs"] = np.ascontiguousarray(xs[sl].reshape(NS * SL, D))
        m["s_ssd"] = np.ascontiguousarray(s_ssd[:, sl].reshape(DEPTH, NS, 1024, 64))
        m["s_sconv"] = np.ascontiguousarray(s_sc[:, sl].reshape(DEPTH, NS * 3, 1280))
        m["s_c"] = np.ascontiguousarray(s_c[:, sl])
        m["s_n"] = np.ascontiguousarray(s_n[:, sl].reshape(DEPTH, NS * 4, 256))
        m["s_m"] = np.ascontiguousarray(s_m[:, sl])
        m["s_fconv"] = np.ascontiguousarray(s_fc[:, sl].reshape(DEPTH, NS * 2, 2 * DFF))
        in_maps.append(m)
    res = run_bass_kernel_spmd(nc, in_maps, core_ids=list(range(NCORES)))
    R = res.results
    cat0 = lambda k: np.stack([R[i][k] for i in range(NCORES)], 0)
    y_p = cat0("y_p")
    y_s = cat0("y_s").reshape(NCORES * NS, SL, D)
    ssd_p = np.stack([R[i]["ssd_p"] for i in range(NCORES)], 1).reshape(DEPTH, NCORES, 16, 64, 64)
    ssd_s = np.concatenate([R[i]["ssd_s"] for i in range(NCORES)], 1).reshape(DEPTH, NCORES * NS, 16, 64, 64)
    sconv_p = np.stack([R[i]["sconv_p"] for i in range(NCORES)], 1)
    sconv_s = np.concatenate([R[i]["sconv_s"] for i in range(NCORES)], 1)
    c_p = np.stack([R[i]["c_p"] for i in range(NCORES)], 1)
    c_s = np.concatenate([R[i]["c_s"] for i in range(NCORES)], 1)
    n_p = np.stack([R[i]["n_p"] for i in range(NCORES)], 1)
    n_s = np.concatenate([R[i]["n_s"].reshape(DEPTH, NS, 4, 256) for i in range(NCORES)], 1)
    m_p = np.stack([R[i]["m_p"].reshape(DEPTH, 4) for i in range(NCORES)], 1)
    m_s = np.concatenate([R[i]["m_s"] for i in range(NCORES)], 1)
    fconv_p = np.stack([R[i]["fconv_p"] for i in range(NCORES)], 1)
    fconv_s = np.concatenate([R[i]["fconv_s"] for i in range(NCORES)], 1)
    outs = (y_p, y_s, ssd_p, ssd_s, sconv_p, sconv_s, c_p, c_s, n_p, n_s, m_p, m_s, fconv_p, fconv_s)
    return tuple(np.ascontiguousarray(o, dtype=np.float32) for o in outs)
```

```python
import contextlib
import os
import numpy as np
import concourse.bass as bass
import concourse.mybir as mybir
from concourse.bass_utils import run_bass_kernel_spmd

F32 = mybir.dt.float32
BF16 = mybir.dt.bfloat16
AF = mybir.ActivationFunctionType
ALU = mybir.AluOpType
AX = mybir.AxisListType

NCORES = 8
D = 1024
SEQ = 2048
DEPTH = 2
NS = 16
SL = 4
NIN = 8472
DFF = 2816
ALPHA = (2 * DEPTH) ** 0.25
EPS = 1e-5
NEGV = -30000.0
ENGS = ("pe", "act", "dve", "pool", "sp")


class _Cut(Exception):
    pass


KT = os.environ.get('KT', '')
KL = int(os.environ.get('KL', '2'))
KCUT = os.environ.get('KCUT', '')


def cut(name):
    if KCUT == name:
        raise _Cut()


class Op:
    __slots__ = ("eng", "fn", "deps", "dma", "sem", "val", "waits", "snap", "needed")

    def __init__(self, eng, fn, dma):
        self.eng = eng
        self.fn = fn
        self.deps = set()
        self.dma = dma
        self.sem = None
        self.val = 0
        self.waits = []
        self.snap = None
        self.needed = False


class Buf:
    __slots__ = ("ap", "toks", "meta")

    def __init__(self, ap, toks, meta=None):
        self.ap = ap
        self.toks = toks
        self.meta = meta

    def __getitem__(self, k):
        return Buf(self.ap[k], self.toks, self.meta)

    def r(self, pat, **kw):
        return Buf(self.ap.rearrange(pat, **kw), self.toks, self.meta)

    def us(self, axis):
        return Buf(self.ap.unsqueeze(axis), self.toks, self.meta)

    def bc(self, shape):
        return Buf(self.ap.to_broadcast(list(shape)), self.toks, self.meta)

    def bitcast(self, dt):
        return Buf(self.ap.bitcast(dt), self.toks, self.meta)


def _toks(lst):
    out = []
    for b in lst:
        if b is None or isinstance(b, (int, float)):
            continue
        if isinstance(b, Buf):
            out.extend(b.toks)
        else:
            out.append(b)
    return out


def _ap(x):
    return x.ap if isinstance(x, Buf) else x


class Prog:
    def __init__(self, nc, n_dma_sems=12):
        self.nc = nc
        self.ops = {e: [] for e in ENGS}
        self.order = []
        self.last_write = {}
        self.readers = {}
        self.n_dma_sems = n_dma_sems
        self.checker = None

    def rec(self, eng, fn, reads=(), writes=(), dma=False):
        op = Op(eng, fn, dma)
        if self.checker is not None:
            for b in list(reads) + list(writes):
                if isinstance(b, Buf) and b.meta is not None:
                    self.checker(b)
        rt = _toks(reads)
        wt = _toks(writes)
        if any(isinstance(t, tuple) and t[0] == "ps" for t in rt):
            wt = wt + [t for t in rt if isinstance(t, tuple) and t[0] == "ps"]
            rt = [t for t in rt if not (isinstance(t, tuple) and t[0] == "ps")]
        for t in rt:
            w = self.last_write.get(t)
            if w is not None:
                op.deps.add(w)
        for t in wt:
            w = self.last_write.get(t)
            if w is not None:
                op.deps.add(w)
            for r in self.readers.get(t, ()):
                op.deps.add(r)
        for t in rt:
            lst = self.readers.setdefault(t, [])
            if not dma:
                for i, r in enumerate(lst):
                    if (not r.dma) and r.eng == eng:
                        lst[i] = op
                        break
                else:
                    lst.append(op)
            else:
                lst.append(op)
        for t in wt:
            self.last_write[t] = op
            self.readers[t] = []
        op.deps.discard(op)
        self.ops[eng].append(op)
        self.order.append(op)
        return op

    def emit(self):
        nc = self.nc
        for op in self.order:
            if op.eng == "pe" and not op.dma:
                op.deps = {d for d in op.deps if not (d.eng == "pe" and not d.dma)}
        dma_cnt = {}
        dma_last = {}
        for op in self.order:
            if op.dma:
                k = dma_cnt.get(op.eng, 0)
                dma_cnt[op.eng] = k + 1
                slot = (op.eng, k % self.n_dma_sems)
                prev = dma_last.get(slot)
                if prev is not None:
                    op.deps.add(prev)
                dma_last[slot] = op
                op.sem = slot
                op.val = 16 * (k // self.n_dma_sems + 1)
        for op in self.order:
            for d in op.deps:
                d.needed = True
        final_ops = list(dma_last.values())
        cnt = {e: 0 for e in ENGS}
        for op in self.order:
            if not op.dma and op.needed:
                cnt[op.eng] += 1
                op.sem = ("c", op.eng)
                op.val = cnt[op.eng]
        know = {e: {} for e in ENGS}
        for op in self.order:
            kn = know[op.eng]
            wd = {}
            for d in sorted(op.deps, key=lambda d: -d.val):
                if d.sem is None:
                    continue
                if kn.get(d.sem, 0) >= d.val:
                    continue
                if wd.get(d.sem, 0) < d.val:
                    wd[d.sem] = d.val
                for s, v in d.snap.items():
                    if kn.get(s, 0) < v:
                        kn[s] = v
            op.waits = list(wd.items())
            if op.sem is not None:
                sn = dict(kn)
                sn[op.sem] = op.val
                op.snap = sn
        sem_keys = []
        seen = set()
        for op in self.order:
            if op.sem is not None and op.sem not in seen:
                seen.add(op.sem)
                sem_keys.append(op.sem)
        self.stats = {e: len(self.ops[e]) for e in ENGS}
        self.stats["sems"] = len(sem_keys)
        self.stats["waits"] = sum(len(op.waits) for op in self.order)
        self.stats["cnt"] = dict(cnt)
        self.stats["dma"] = dict(dma_cnt)
        sems = {}
        with contextlib.ExitStack() as st:
            for i, k in enumerate(sem_keys):
                sems[k] = st.enter_context(nc.semaphore("s%d" % i))
            block = st.enter_context(nc.Block())
            fin = [(sems[d.sem], d.val) for d in final_ops]
            ops = self.ops

            def run(eng_name, eng):
                for op in ops[eng_name]:
                    for s, v in op.waits:
                        eng.wait_ge(sems[s], v)
                    ins = op.fn(eng)
                    if op.sem is not None:
                        ins.then_inc(sems[op.sem], 16 if op.dma else 1)
                if eng_name == "sp":
                    for s, v in fin:
                        eng.wait_ge(s, v)

            @block.tensor
            def _(e):
                run("pe", e)

            @block.scalar
            def _(e):
                run("act", e)

            @block.vector
            def _(e):
                run("dve", e)

            @block.gpsimd
            def _(e):
                run("pool", e)

            @block.sync
            def _(e):
                run("sp", e)


def _const_layout():
    lay = {}
    off = 0

    def add(name, n):
        nonlocal off
        lay[name] = (off, n)
        off += n

    add("ident", 128)
    add("ones", 128)
    for sfx, q in (("p", 128), ("s", 64)):
        add("L" + sfx, q)
        add("U" + sfx, q)
        add("SELLAST" + sfx, q)
        add("NEG" + sfx, 512)
        add("NEGT" + sfx, 4 * q)
    add("SELROWp", 1)
    add("SELROWs", NS)
    add("bms", NS)
    add("bmTs", 64)
    return lay, off


CL, NCONST = _const_layout()


def _make_consts():
    c = np.zeros((128, NCONST), np.float32)

    def put(name, arr):
        o, n = CL[name]
        a = np.asarray(arr, np.float32)
        c[: a.shape[0], o:o + a.shape[1]] = a

    put("ident", np.eye(128))
    put("ones", np.ones((128, 128)))
    for sfx, q, sl in (("p", 128, 128), ("s", 64, SL)):
        i = np.arange(q)
        seq = i // sl
        same = seq[:, None] == seq[None, :]
        put("L" + sfx, same & (i[:, None] <= i[None, :]))
        put("U" + sfx, same & (i[:, None] > i[None, :]))
        last = seq * sl + sl - 1
        put("SELLAST" + sfx, i[:, None] == last[None, :])
        neg = np.where(same & (i[None, :] >= i[:, None]), 0.0, NEGV)
        put("NEG" + sfx, np.tile(neg, (1, 512 // q)))
        negt = np.where(same & (i[None, :] <= i[:, None]), 0.0, NEGV)
        put("NEGT" + sfx, np.tile(negt, (1, 4)))
    sr = np.zeros((128, 1))
    sr[127, 0] = 1
    put("SELROWp", sr)
    i = np.arange(64)
    put("SELROWs", (i[:, None] == (np.arange(NS) * SL + SL - 1)[None, :]))
    bm = (i[:, None] // SL) == np.arange(NS)[None, :]
    put("bms", bm)
    put("bmTs", bm.T)
    return c


PR = {"dtb": (0, 16), "alog": (16, 16), "D": (32, 16), "gb": (48, 8), "snw": (64, 1024), "mnw": (1088, 1024),
      "l1g": (2112, 1024), "l1b": (3136, 1024), "l2g": (4160, 1024), "l2b": (5184, 1024)}
NPR = 6208

def _wblocks():
    blks = [("w_in", 0, 8, [(0, 1024)]), ("w_in", 0, 8, [(1024, 1024)]), ("w_in", 0, 8, [(2048, 272)]),
            ("w_in", 0, 8, [(2320, 1024)]), ("w_in", 0, 8, [(3344, 1024)]), ("w_in", 0, 8, [(4368, 1032)]),
            ("w_in", 0, 8, [(5400, 1024)]), ("w_a", 0, 8, [(0, 1024)]), ("w_in", 0, 8, [(6424, 1024)]),
            ("w_b", 0, 8, [(0, 1024)]), ("w_in", 0, 8, [(7448, 1024)]), ("w_out", 0, 8, [(0, 1024)])]
    for bb in range(6):
        n = 512 if bb < 5 else 256
        blks.append(("w_up", 0, 8, [(bb * 512, n), (DFF + bb * 512, n)]))
    for cb in range(2):
        for kh in range(2):
            blks.append(("w_down", kh * 11 * 128, 11, [(cb * 512, 512)]))
    return blks


WBLK = _wblocks()
NWB = len(WBLK)
WCAP = 8 * 1032
NWBUF = 2
USE_WSC = True


def build_program():
    nc = bass.Bass("TRN2", target_bir_lowering=False)
    dr = {}

    def din(name, shape):
        dr[name] = nc.dram_tensor(name, list(shape), F32, kind="ExternalInput").ap()
        return dr[name]

    def dout(name, shape):
        dr[name] = nc.dram_tensor(name, list(shape), F32, kind="ExternalOutput").ap()
        return dr[name]

    din("xp", (SEQ, D)); din("xs", (NS * SL, D))
    din("s_ssd", (DEPTH, NS, 1024, 64)); din("s_sconv", (DEPTH, NS * 3, 1280))
    din("s_c", (DEPTH, NS, 4, 256, 256)); din("s_n", (DEPTH, NS * 4, 256)); din("s_m", (DEPTH, NS, 4))
    din("s_fconv", (DEPTH, NS * 2, 2 * DFF))
    din("w_in", (DEPTH, D, NIN)); din("w_a", (DEPTH, D, D)); din("w_b", (DEPTH, D, D)); din("w_out", (DEPTH, D, D))
    din("w_up", (DEPTH, D, 2 * DFF)); din("w_down", (DEPTH, DFF, D))
    din("prow", (DEPTH, NPR)); din("cw_s", (DEPTH, 128, 40)); din("cb_s", (DEPTH, 128, 10))
    din("cw_f", (DEPTH, 128, 132)); din("cb_f", (DEPTH, 128, 44)); din("consts", (128, NCONST))
    dout("y_p", (SEQ, D)); dout("y_s", (NS * SL, D))
    dout("ssd_p", (DEPTH, 1024, 64)); dout("ssd_s", (DEPTH, NS, 1024, 64))
    dout("sconv_p", (DEPTH, 3, 1280)); dout("sconv_s", (DEPTH, NS, 3, 1280))
    dout("c_p", (DEPTH, 4, 256, 256)); dout("c_s", (DEPTH, NS, 4, 256, 256))
    dout("n_p", (DEPTH, 4, 256)); dout("n_s", (DEPTH, NS * 4, 256))
    dout("m_p", (DEPTH, 1, 4)); dout("m_s", (DEPTH, NS, 4))
    dout("fconv_p", (DEPTH, 2, 2 * DFF)); dout("fconv_s", (DEPTH, NS, 2, 2 * DFF))

    wsc = nc.dram_tensor("wsc", [DEPTH * NWB, 128, WCAP], BF16, kind="Internal").ap()
    st = contextlib.ExitStack()
    with st:
        P = Prog(nc)
        ARENA_BYTES = 211456
        arena_t = st.enter_context(nc.sbuf_tensor("arena", [128, ARENA_BYTES // 4], F32))
        PS_t = st.enter_context(nc.psum_tensor("PS", [128, 4096], F32))
        astate = {"off": 0}

        def alloc(free, dt=F32):
            if isinstance(free, int):
                free = (free,)
            n = int(np.prod(free))
            nb = n * (4 if dt == F32 else 2)
            nb = (nb + 127) // 128 * 128
            o = astate["off"]
            astate["off"] = o + nb
            astate["peak"] = max(astate.get("peak", 0), astate["off"])
            assert astate["off"] <= ARENA_BYTES, ("arena overflow", astate["off"])
            base = arena_t[:, o // 4:(o + nb) // 4]
            ap = base if dt == F32 else base.bitcast(BF16)
            ap = ap[:, 0:n]
            if len(free) == 2:
                ap = ap.rearrange("p (a b) -> p a b", a=free[0])
            elif len(free) == 3:
                ap = ap.rearrange("p (a b c) -> p a b c", a=free[0], b=free[1])
            elif len(free) == 4:
                ap = ap.rearrange("p (a b c d) -> p a b c d", a=free[0], b=free[1], c=free[2])
            toks = [("sb", k) for k in range(o // 128, (o + nb) // 128)]
            return Buf(ap, toks)

        def mark():
            return astate["off"]

        def release(m):
            astate["off"] = m

        ps_use = [0] * 8
        ps_clock = [0]
        ps_gen = [0] * 8

        ring_gen = {}

        def ps_check(b):
            for kind, idx, g in b.meta:
                if kind == "ps":
                    assert ps_gen[idx] == g, ("stale PSUM buffer used", idx, g, ps_gen[idx])
                else:
                    assert ring_gen.get(idx, 0) == g, ("stale ring buffer used", idx, g, ring_gen.get(idx, 0))

        def make_ring(nbytes):
            base = (astate["off"] + 511) // 512 * 512
            nbytes = nbytes // 512 * 512
            astate["off"] = base + nbytes
            astate["peak"] = max(astate.get("peak", 0), astate["off"])
            assert astate["off"] <= ARENA_BYTES, ("arena overflow (ring)", astate["off"])
            assert base % 512 == 0 or True
            st_ = {"o": 0}

            def ralloc(free, dt=F32):
                if isinstance(free, int):
                    free = (free,)
                n = int(np.prod(free))
                nb = n * (4 if dt == F32 else 2)
                nb = (nb + 511) // 512 * 512
                assert nb <= nbytes
                if st_["o"] + nb > nbytes:
                    st_["o"] = 0
                o = base + st_["o"]
                st_["o"] += nb
                o4 = (o + 3) // 4 * 4
                bs_ = arena_t[:, o4 // 4:(o4 + nb) // 4 if o4 + nb <= ARENA_BYTES else ARENA_BYTES // 4]
                ap = bs_ if dt == F32 else bs_.bitcast(BF16)
                ap = ap[:, 0:n]
                if len(free) == 2:
                    ap = ap.rearrange("p (a b) -> p a b", a=free[0])
                elif len(free) == 3:
                    ap = ap.rearrange("p (a b c) -> p a b c", a=free[0], b=free[1])
                slots = list(range(o // 512, (o + nb) // 512))
                meta = []
                for k in slots:
                    ring_gen[k] = ring_gen.get(k, 0) + 1
                    meta.append(("ring", k, ring_gen[k]))
                return Buf(ap, [("sb", k) for k in range(o // 128, (o + nb) // 128)], tuple(meta))

            return ralloc

        P.checker = ps_check

        ps_allowed = [list(range(8))]

        def run_streams(streams):
            alive = list(streams)
            while alive:
                for item in list(alive):
                    g, banks = item
                    ps_allowed[0] = banks
                    try:
                        next(g)
                    except StopIteration:
                        alive.remove(item)
            ps_allowed[0] = list(range(8))

        def psum(nb=1):
            best, bs = None, None
            for s in range(0, 8, nb):
                if any(b not in ps_allowed[0] for b in range(s, s + nb)):
                    continue
                sc = max(ps_use[s:s + nb])
                if best is None or sc < best:
                    best, bs = sc, s
            ps_clock[0] += 1
            for b in range(bs, bs + nb):
                ps_use[b] = ps_clock[0]
                ps_gen[b] += 1
            return Buf(PS_t[:, bs * 512:(bs + nb) * 512], [("ps", b) for b in range(bs, bs + nb)],
                       tuple(("ps", b, ps_gen[b]) for b in range(bs, bs + nb)))

        def MM(out, lhsT, rhs, start=True, stop=True):
            P.rec("pe", lambda e: e.matmul(out.ap, lhsT.ap, rhs.ap, start=start, stop=stop),
                  reads=[lhsT, rhs], writes=[out])

        def TR(out, in_, ident):
            P.rec("pe", lambda e: e.transpose(out.ap, in_.ap, ident.ap), reads=[in_, ident], writes=[out])

        def ACT(out, in_, func, scale=1.0, bias=None, accum=None):
            kw = {}
            if bias is not None:
                kw["bias"] = _ap(bias)
            if accum is not None:
                kw["accum_out"] = accum.ap
            sc = _ap(scale)
            P.rec("act", lambda e: e.activation(out=out.ap, in_=in_.ap, func=func, scale=sc, **kw),
                  reads=[in_, scale, bias], writes=[out, accum])

        def TT(eng, out, in0, in1, op):
            P.rec(eng, lambda e: e.tensor_tensor(out=out.ap, in0=in0.ap, in1=in1.ap, op=op),
                  reads=[in0, in1], writes=[out])

        def TS(eng, out, in0, s1, s2=None, op0=ALU.mult, op1=None):
            a1, a2 = _ap(s1), _ap(s2)
            if op1 is None:
                P.rec(eng, lambda e: e.tensor_scalar(out=out.ap, in0=in0.ap, scalar1=a1, scalar2=None, op0=op0),
                      reads=[in0, s1], writes=[out])
            else:
                P.rec(eng, lambda e: e.tensor_scalar(out=out.ap, in0=in0.ap, scalar1=a1, scalar2=a2, op0=op0, op1=op1),
                      reads=[in0, s1, s2], writes=[out])

        def STT(out, in0, scalar, in1, op0, op1):
            sc = _ap(scalar)
            P.rec("dve", lambda e: e.scalar_tensor_tensor(out=out.ap, in0=in0.ap, scalar=sc, in1=in1.ap, op0=op0, op1=op1),
                  reads=[in0, scalar, in1], writes=[out])

        def CP(eng, out, in_):
            if eng == "act":
                ACT(out, in_, AF.Copy)
            else:
                P.rec(eng, lambda e: e.tensor_copy(out=out.ap, in_=in_.ap), reads=[in_], writes=[out])

        def MEMSET(eng, out, v):
            P.rec(eng, lambda e: e.memset(out.ap, v), writes=[out])

        def DMA(q, out, in_, reads=(), writes=()):
            oa, ia = _ap(out), _ap(in_)
            P.rec(q, lambda e: e.dma_start(out=oa, in_=ia), reads=list(reads) + ([in_] if isinstance(in_, Buf) else []),
                  writes=list(writes) + ([out] if isinstance(out, Buf) else []), dma=True)

        def RED(out, in_, op):
            P.rec("dve", lambda e: e.tensor_reduce(out=out.ap, in_=in_.ap, axis=AX.X, op=op), reads=[in_], writes=[out])

        cp_rr = [0]

        def CPX(out, in_):
            cp_rr[0] ^= 1
            CP("act" if cp_rr[0] else "dve", out, in_)

        cf = alloc(NCONST, F32)
        cb_ = alloc(NCONST, BF16)
        DMA("sp", cf, dr["consts"])
        DMA("pool", cb_, dr["consts"])

        def CF(name, rows=128, sub=None):
            o, n = CL[name]
            if sub is not None:
                o, n = o + sub[0], sub[1]
            return cf[0:rows, o:o + n]

        def CB(name, rows=128, sub=None):
            o, n = CL[name]
            if sub is not None:
                o, n = o + sub[0], sub[1]
            return cb_[0:rows, o:o + n]

        mhalf = alloc(1, F32)
        MEMSET("pool", mhalf, -0.5)
        wbufs = [alloc(WCAP, BF16) for _ in range(NWBUF)]
        x_tok = alloc((4, D), F32)
        prm_small = alloc(64, F32)
        cws = alloc((10, 4), F32); cbs = alloc(10, F32); cwf = alloc((44, 3), F32); cbf = alloc(44, F32)
        a_bc = alloc(16, F32)
        pst = []
        for l in range(DEPTH):
            s = {"hT_f": alloc(512, F32), "hT_b": alloc(512, BF16), "C_f": alloc((2, 4, 256), F32),
                 "C_b": alloc((2, 4, 256), BF16), "nT_f": alloc((2, 4, 1), F32), "nT_b": alloc((2, 4, 1), BF16),
                 "m_st": alloc(4, F32), "tl_s": alloc((10, 3), F32), "tl_f": alloc((44, 2), F32)}
            for k in ("hT_f", "hT_b", "C_f", "C_b", "nT_f", "nT_b", "m_st", "tl_s", "tl_f"):
                MEMSET("pool", s[k], 0.0)
            pst.append(s)

        wstate = {"issued": 0, "gs": None, "lst": None}

        def wbuf_of(g):
            if wstate["gs"] is not None and g >= wstate["gs"]:
                lst = wstate["lst"]
                return lst[(g - wstate["gs"]) % len(lst)]
            return wbufs[g % NWBUF]

        def wdepth(g):
            if wstate["gs"] is not None and g >= wstate["gs"]:
                return len(wstate["lst"]) - 1
            return 1

        def wissue(g):
            l = (g // NWB) % DEPTH
            name, row0, nk, segs = WBLK[g % NWB]
            wb = wbuf_of(g)
            tot = sum(n for _, n in segs)
            view = wb[:, 0:nk * tot].r("p (k n) -> p k n", k=nk)
            lk = g % (DEPTH * NWB)
            if g >= DEPTH * NWB and USE_WSC:
                DMA("pool", wb[:, 0:nk * tot], wsc[lk, :, 0:nk * tot], reads=[("dram", lk)])
                return
            src = dr[name][l]
            o = 0
            for c0, n in segs:
                sap = src[row0:row0 + nk * 128, c0:c0 + n].rearrange("(k p) n -> p k n", p=128)
                DMA("pool", view[:, :, o:o + n], sap)
                o += n
            if USE_WSC:
                DMA("sp", wsc[lk, :, 0:nk * tot], wb[:, 0:nk * tot], writes=[("dram", lk)])

        def wget(g, total, prefetch=True):
            while wstate["issued"] <= min(g + (wdepth(g) if prefetch else 0), total - 1):
                nx = wstate["issued"]
                if nx >= SAMPLE_G0 and wstate["gs"] is None and nx > g:
                    break
                wissue(nx)
                wstate["issued"] += 1
            name, row0, nk, segs = WBLK[g % NWB]
            tot = sum(n for _, n in segs)
            return wbuf_of(g)[:, 0:nk * tot].r("p (k n) -> p k n", k=nk)

        tiles = [("p", i) for i in range(4)] + [("s", 0)]
        if KT:
            tiles = [(t[0], int(t[1:] or 0)) for t in KT.split(",")]
        total_blocks = len(tiles) * DEPTH * NWB
        SAMPLE_G0 = 10 ** 9
        for _i, (_k, _t) in enumerate(tiles):
            if _k == "s":
                SAMPLE_G0 = _i * DEPTH * NWB
        gctr = [0]

        def nextw(prefetch=True):
            g = gctr[0]
            gctr[0] += 1
            return wget(g, total_blocks, prefetch)

        identf = CF("ident")
        identb = CB("ident")
        onesf = CF("ones")
        onesb = CB("ones")

        def rsqrt_(out, in_, scale, rows, n):
            TS("dve", out, in_, scale, EPS, ALU.mult, ALU.add)
            TT("pool", out, out, mhalf[0:rows, 0:1].bc([rows, n]), ALU.pow)

        for kind, ti in tiles:
            prompt = kind == "p"
            Q = 128 if prompt else 64
            NCH = 4 if prompt else 1
            TTK = Q * NCH
            nseq = 1 if prompt else NS
            Lq = TTK if prompt else SL
            sfx = "p" if prompt else "s"
            Lm = CF("L" + sfx, Q); Um = CF("U" + sfx, Q); SELLAST = CF("SELLAST" + sfx, Q)
            NEGb = CB("NEG" + sfx, Q); NEGTb = CB("NEGT" + sfx, Q)
            SELROW = CF("SELROW" + sfx, Q)
            if prompt:
                bm = onesf[0:Q, 0:1]; bmb = onesb[0:Q, 0:1]; bmT = onesf[0:1, 0:Q]
            else:
                bm = CF("bms", Q); bmb = CB("bms", Q); bmT = CF("bmTs", NS)
            idq_f = identf[0:Q, 0:Q]; idq_b = identb[0:Q, 0:Q]
            last_tile = prompt and ti == 3
            want_rows = last_tile or not prompt

            if not prompt and tiles.index((kind, ti)) * DEPTH * NWB == SAMPLE_G0:
                extra = [alloc(WCAP, BF16)]
                wstate["gs"] = max(SAMPLE_G0, wstate["issued"])
                wstate["lst"] = extra + [wbufs[(wstate["gs"] + k) % NWBUF] for k in range(NWBUF)]
            if prompt:
                DMA("sp", x_tok, dr["xp"][ti * 512:(ti + 1) * 512, :].rearrange("(c p) d -> p c d", p=128))
            else:
                DMA("sp", x_tok[0:Q, 0, :], dr["xs"])

            def layer_body(l):
                S = pst[l]
                m0 = mark()
                DMA("sp", prm_small, dr["prow"][l, 0:64].partition_broadcast(128))
                DMA("sp", cws, dr["cw_s"][l].rearrange("p (a b) -> p a b", a=10))
                DMA("sp", cbs, dr["cb_s"][l])
                DMA("sp", cwf, dr["cw_f"][l].rearrange("p (a b) -> p a b", a=44))
                DMA("sp", cbf, dr["cb_f"][l])
                ACT(a_bc, prm_small[:, 16:32], AF.Exp)
                TS("dve", a_bc, a_bc, -1.0)
                dtb = prm_small[:, 0:16]; Dp = prm_small[:, 32:48]; gbp = prm_small[:, 48:56]

                def prow(name, rows):
                    o, n = PR[name]
                    b = alloc(n, F32)
                    DMA("sp", b, dr["prow"][l, o:o + n].partition_broadcast(128))
                    return b[0:rows, :]

                xT = alloc((8, TTK), BF16)
                yT = alloc((8, TTK), BF16)
                qT = alloc((8, TTK), BF16)
                kT = alloc((8, TTK), BF16)
                k_tok = [alloc(D, BF16) for _ in range(NCH)]
                gi = alloc((NCH, 4), F32)
                logf = alloc((NCH, 4), F32)
                mS = mark()

                def to_featmajor(src_tok, dstT):
                    for c in range(NCH):
                        for half in range(2):
                            ps = psum(1)
                            for j in range(4):
                                kc = half * 4 + j
                                TR(ps[:, j * Q:(j + 1) * Q], src_tok[0:Q, c, kc * 128:(kc + 1) * 128], idq_f)
                            CPX(dstT[:, half * 4:half * 4 + 4, c * Q:(c + 1) * Q],
                                ps[:, 0:4 * Q].r("p (j q) -> p j q", j=4))

                to_featmajor(x_tok, xT)

                def proj_tok(src, c, wb, col0, ncols, evac, rows=None):
                    r0, r1 = (c * Q, (c + 1) * Q) if rows is None else rows
                    M = r1 - r0
                    for cb0 in range(0, ncols, 512):
                        n = min(512, ncols - cb0)
                        ps = psum(1)
                        for kc in range(8):
                            MM(ps[0:M, 0:n], src[:, kc, r0:r1], wb[:, kc, col0 + cb0:col0 + cb0 + n], kc == 0, kc == 7)
                        evac(ps[0:M, 0:n], cb0, n)

                def proj_feat(src, wb, col0, evac, m=128):
                    ps = psum(1)
                    for kc in range(8):
                        MM(ps[0:m, 0:TTK], wb[:, kc, col0:col0 + m], src[:, kc, 0:TTK], kc == 0, kc == 7)
                    evac(ps[0:m, 0:TTK])

                if prompt:
                    row_rng = (TTK - 32, TTK)
                else:
                    row_rng = (0, 64)
                RM = row_rng[1] - row_rng[0]

                def emit_rows(src, wb, col0, ncols, dst_p, dst_s, nrow, dcol0):
                    def ev(ps, cb0, n):
                        t = cur_ring[0](512, F32)
                        CPX(t[0:RM, 0:n], ps)
                        if prompt:
                            DMA("sp", dst_p[l, :, dcol0 + cb0:dcol0 + cb0 + n], t[RM - nrow:RM, 0:n])
                        else:
                            for tt in range(nrow):
                                tok = SL - nrow + tt
                                DMA("sp", dst_s[l, :, tt, dcol0 + cb0:dcol0 + cb0 + n], t[tok:64:SL, 0:n])
                    proj_tok(src, 0, wb, col0, ncols, ev, rows=row_rng)

                cut('P0')
                z_tok = [alloc(D, BF16) for _ in range(NCH)]
                xsT = alloc((8, TTK), BF16)
                BT = alloc(TTK, BF16)
                CT = alloc(TTK, BF16)
                dt_ = alloc((NCH, 16), F32)
                dta = alloc((NCH, 16), F32)
                wb = nextw()
                for c in range(NCH):
                    proj_tok(xT, c, wb, 0, 1024,
                             lambda ps, cb0, n, c=c: ACT(z_tok[c][0:Q, cb0:cb0 + n], ps, AF.Silu))
                histS = None
                if not prompt:
                    histS = alloc((10, NS * 3), F32)
                    mm = mark()
                    nat = alloc(1280, F32)
                    DMA("sp", nat[0:NS * 3, :], dr["s_sconv"][l])
                    for cc in range(10):
                        ps = psum(1)
                        TR(ps[:, 0:NS * 3], nat[0:NS * 3, cc * 128:(cc + 1) * 128], identf[0:NS * 3, 0:NS * 3])
                        CPX(histS[:, cc, :], ps[:, 0:NS * 3])
                    release(mm)

                mR = mark()
                cur_ring = [make_ring(20 * 1024)]

                pend_silu = []

                def flush_silu():
                    while pend_silu:
                        d_, a_c = pend_silu.pop(0)
                        ACT(d_, a_c, AF.Silu)

                def conv_chunk(ps, cc):
                    xp = cur_ring[0]((nseq, 3 + Lq), F32)
                    acc = cur_ring[0]((nseq, Lq), F32)
                    if prompt:
                        CP("act", xp[:, 0, 0:3], S["tl_s"][:, cc, :])
                    else:
                        CP("act", xp[:, :, 0:3], histS[:, cc, :].r("p (b j) -> p b j", j=3))
                    CP("act", xp[:, :, 3:3 + Lq], ps.r("p (b t) -> p b t", b=nseq))
                    ACT(acc, ps.r("p (b t) -> p b t", b=nseq), AF.Identity, scale=cws[:, cc, 3:4], bias=cbs[:, cc:cc + 1])
                    for j in (2, 1, 0):
                        STT(acc, xp[:, :, j:j + Lq], cws[:, cc, j:j + 1], acc, ALU.mult, ALU.add)
                    if cc < 8:
                        dst = xsT[:, cc, :]
                    elif cc == 8:
                        dst = BT
                    else:
                        dst = CT
                    if prompt:
                        CP("act", S["tl_s"][:, cc, :], xp[:, 0, Lq:Lq + 3])
                    flush_silu()
                    pend_silu.append((dst.r("p (b t) -> p b t", b=nseq), acc))

                wb = nextw()
                for cc in range(8):
                    proj_feat(xT, wb, cc * 128, lambda ps, cc=cc: conv_chunk(ps, cc))
                if want_rows:
                    emit_rows(xT, wb, 0, 1024, dr["sconv_p"], dr["sconv_s"], 3, 0)
                wb = nextw()
                for cc in (8, 9):
                    proj_feat(xT, wb, (cc - 8) * 128, lambda ps, cc=cc: conv_chunk(ps, cc))
                flush_silu()
                if want_rows:
                    emit_rows(xT, wb, 0, 256, dr["sconv_p"], dr["sconv_s"], 3, 1024)
                psd = psum(1)
                for c in range(NCH):
                    for kc in range(8):
                        MM(psd[0:Q, c * 16:(c + 1) * 16], xT[:, kc, c * Q:(c + 1) * Q], wb[:, kc, 256:272], kc == 0, kc == 7)
                mm = mark()
                tx = alloc((NCH, 16), F32); ta = alloc((NCH, 16), F32)
                TT("dve", tx[0:Q], psd[0:Q, 0:NCH * 16].r("p (c h) -> p c h", c=NCH), dtb[0:Q].us(1).bc([Q, NCH, 16]), ALU.add)
                STT(ta[0:Q], tx[0:Q], -1.0, tx[0:Q], ALU.mult, ALU.max)
                ACT(ta[0:Q], ta[0:Q], AF.Exp, scale=-1.0)
                ACT(ta[0:Q], ta[0:Q], AF.Ln, bias=1.0)
                STT(dt_[0:Q], tx[0:Q], 0.0, ta[0:Q], ALU.max, ALU.add)
                TT("dve", dta[0:Q], dt_[0:Q], a_bc[0:Q].us(1).bc([Q, NCH, 16]), ALU.mult)
                release(mm)

                release(mR)
                cut('S1')
                snw = prow("snw", Q)
                gx = alloc((NCH, 8), F32); gta = alloc((NCH, 4), F32)
                s2b = []
                for _p in range(2 if NCH > 1 else 1):
                    s2b.append({"xs_tok": alloc(D, BF16), "xdt": alloc(D, BF16), "B_tok": alloc(128, BF16),
                                "GT": alloc((16, Q), BF16), "eac": alloc(32, F32), "cdb": alloc((nseq, 8), F32),
                                "xdtw": alloc(D, BF16)})
                R = alloc((16, Q), F32); expT = alloc((16, Q), BF16); CBs = alloc((2, Q), BF16)
                Xs = None if prompt else alloc((2, NS, 8), F32)
                yo = alloc(D, F32); xsD = alloc(D, F32); ss = alloc(2, F32); y_n = alloc(D, BF16)
                if not prompt:
                    sq = [{"hT_f": alloc(512, F32), "hT_b": alloc(512, BF16),
                           "xw": alloc(D, BF16), "hn": alloc(512, F32)} for _ in range(2)]
                    nat_l = [alloc((8, 64), F32) for _ in range(6)]
                    natn_l = [alloc((8, 64), F32) for _ in range(6)]

                def s2_front(c, B):
                    cols = slice(c * Q, (c + 1) * Q)
                    psx = psum(1).bitcast(BF16)
                    for kc in range(8):
                        TR(psx[0:Q, kc * 128:(kc + 1) * 128], xsT[:, kc, cols], identb)
                    CP("act", B["xs_tok"][0:Q], psx[0:Q, :])
                    TT("dve", B["xdt"][0:Q].r("p (h e) -> p h e", h=16), psx[0:Q, :].r("p (h e) -> p h e", h=16),
                       dt_[0:Q, c, :].us(2).bc([Q, 16, 64]), ALU.mult)
                    psb = psum(1).bitcast(BF16)
                    TR(psb[0:Q, 0:128], BT[:, cols], identb)
                    CP("act", B["B_tok"][0:Q], psb[0:Q, 0:128])
                    TT("pool", R[0:Q], Lm.us(1).bc([Q, 16, Q]), dta[0:Q, c, :].us(2).bc([Q, 16, Q]), ALU.mult)
                    yield
                    nbk = 16 * Q // 512
                    SEG = psum(nbk)
                    Rf = R[0:Q].r("p h l -> p (h l)")
                    for bk in range(nbk):
                        MM(SEG[0:Q, bk * 512:(bk + 1) * 512], Um, Rf[:, bk * 512:(bk + 1) * 512], True, False)
                        MM(SEG[0:Q, bk * 512:(bk + 1) * 512], idq_b, NEGb, False, True)
                    ACT(expT[0:Q].r("p h l -> p (h l)"), SEG[0:Q, 0:16 * Q], AF.Exp)
                    psc = psum(2)
                    for g in range(2):
                        MM(psc[0:Q, g * 512:g * 512 + Q], BT[g * 64:(g + 1) * 64, cols], CT[g * 64:(g + 1) * 64, cols])
                    pt = psum(1)
                    MM(pt[0:Q, 0:16], Lm, dta[0:Q, c, :])
                    MM(pt[0:Q, 16:32], Um, dta[0:Q, c, :])
                    pcd = psum(1)
                    if prompt:
                        for g in range(2):
                            MM(pcd[g * 64:(g + 1) * 64, 0:8], onesf[0:Q, 0:64], dta[0:Q, c, g * 8:(g + 1) * 8])
                    else:
                        for g in range(2):
                            TT("dve", Xs[0:Q, g], bm.us(2).bc([Q, NS, 8]),
                               dta[0:Q, c, g * 8:(g + 1) * 8].us(1).bc([Q, NS, 8]), ALU.mult)
                            MM(pcd[g * 64:(g + 1) * 64, 0:NS * 8], onesf[0:Q, 0:64], Xs[0:Q, g].r("p b h -> p (b h)"))
                    yield
                    CP("dve", CBs[0:Q], psc[0:Q, :].r("p (g x) -> p g x", g=2)[:, :, 0:Q])
                    ACT(B["eac"][0:Q], pt[0:Q, 0:32], AF.Exp)
                    ACT(B["cdb"].r("p b h -> p (b h)"), pcd[:, 0:nseq * 8], AF.Exp)
                    TT("dve", B["GT"][0:Q].r("p (g k) l -> p g k l", g=2), expT[0:Q].r("p (g k) l -> p g k l", g=2),
                       CBs[0:Q].us(2).bc([Q, 2, 8, Q]), ALU.mult)
                    TT("dve", B["xdtw"][0:Q].r("p (h e) -> p h e", h=16), B["xdt"][0:Q].r("p (h e) -> p h e", h=16),
                       B["eac"][0:Q, 16:32].us(2).bc([Q, 16, 64]), ALU.mult)
                    yield

                def s2_back(c, B):
                    cols = slice(c * Q, (c + 1) * Q)
                    eac, cdb, xdtw, xdt, GT, xs_tok, B_tok = (B["eac"], B["cdb"], B["xdtw"], B["xdt"], B["GT"],
                                                             B["xs_tok"], B["B_tok"])
                    def load_nat(b2):
                        nv = nat_l[b2 % 6].r("p (j g) n -> p j g n", g=2)
                        for g in range(2):
                            DMA("sp", nv[:, :, g, :],
                                dr["s_ssd"][l, b2][g * 512:(g + 1) * 512, :].rearrange("(j q) n -> q j n", q=128))

                    if not prompt:
                        for b2 in range(3):
                            load_nat(b2)
                    for b in range(nseq):
                        if prompt:
                            hT_f, hT_b = S["hT_f"], S["hT_b"]
                        else:
                            if b + 3 < nseq:
                                load_nat(b + 3)
                            sb_ = sq[b % 2]
                            nat, hT_f, hT_b = nat_l[b % 6], sb_["hT_f"], sb_["hT_b"]
                            natv = nat.r("p (j g) n -> p j g n", g=2)
                            pn = psum(1)
                            for jj in range(4):
                                TR(pn[:, jj * 128:(jj + 1) * 128], natv[:, jj].r("p g n -> p (g n)"), identf)
                            CP("dve", hT_f, pn)
                            CP("act", hT_b, pn)
                        YO = psum(2)
                        for g in range(2):
                            MM(YO[0:Q, g * 512:(g + 1) * 512], CT[g * 64:(g + 1) * 64, cols], hT_b[g * 64:(g + 1) * 64, :])
                        if prompt:
                            TT("dve", yo[0:Q].r("p (h e) -> p h e", h=16), YO[0:Q, :].r("p (h e) -> p h e", h=16),
                               eac[0:Q, 0:16].us(2).bc([Q, 16, 64]), ALU.mult)
                            xw = xdtw
                        else:
                            if b == 0:
                                TS("dve", yo[0:Q], YO[0:Q, :], bm[:, b:b + 1])
                            else:
                                STT(yo[0:Q], YO[0:Q, :], bm[:, b:b + 1], yo[0:Q], ALU.mult, ALU.add)
                            xw = sb_["xw"]
                            TS("dve", xw[0:Q], xdtw[0:Q], bm[:, b:b + 1])
                        HL = psum(1)
                        for g in range(2):
                            MM(HL[g * 64:(g + 1) * 64, :], B_tok[0:Q, g * 64:(g + 1) * 64], xw[0:Q, g * 512:(g + 1) * 512])
                        hn = hT_f if prompt else sb_["hn"]
                        TT("dve", hn.r("p (h e) -> p h e", h=8), hT_f.r("p (h e) -> p h e", h=8),
                           cdb[:, b, :].us(2).bc([128, 8, 64]), ALU.mult)
                        TT("dve", hn, hn, HL, ALU.add)
                        if prompt:
                            CP("act", hT_b, hn)
                        else:
                            po = psum(1)
                            for jj in range(4):
                                TR(po[:, jj * 128:(jj + 1) * 128], hn[:, jj * 128:(jj + 1) * 128], identf)
                            natn = natn_l[b % 6]
                            CP("act", natn.r("p k n -> p (k n)"), po)
                            natnv = natn.r("p (j g) n -> p j g n", g=2)
                            for g in range(2):
                                DMA("sp", dr["ssd_s"][l, b][g * 512:(g + 1) * 512, :].rearrange("(j q) n -> q j n", q=128),
                                    natnv[:, :, g, :])
                        yield
                    if not prompt:
                        TT("dve", yo[0:Q].r("p (h e) -> p h e", h=16), yo[0:Q].r("p (h e) -> p h e", h=16),
                           eac[0:Q, 0:16].us(2).bc([Q, 16, 64]), ALU.mult)
                    TT("pool", xsD[0:Q].r("p (h e) -> p h e", h=16), xs_tok[0:Q].r("p (h e) -> p h e", h=16),
                       Dp[0:Q].us(2).bc([Q, 16, 64]), ALU.mult)
                    Y = psum(2)
                    for h in range(16):
                        MM(Y[0:Q, h * 64:(h + 1) * 64], GT[0:Q, h, :], xdt[0:Q, h * 64:(h + 1) * 64])
                    TT("dve", yo[0:Q], yo[0:Q], Y[0:Q, :], ALU.add)
                    TT("dve", yo[0:Q], yo[0:Q], xsD[0:Q], ALU.add)
                    TT("dve", yo[0:Q], yo[0:Q], z_tok[c][0:Q], ALU.mult)
                    ACT(xsD[0:Q], yo[0:Q], AF.Square, accum=ss[0:Q, 0:1])
                    rsqrt_(ss[0:Q, 1:2], ss[0:Q, 0:1], 1.0 / 1024, Q, 1)
                    STT(y_n[0:Q], yo[0:Q], ss[0:Q, 1:2], snw, ALU.mult, ALU.mult)
                    yield
                    pyt = psum(1).bitcast(BF16)
                    for kc in range(8):
                        TR(pyt[:, kc * Q:(kc + 1) * Q], y_n[0:Q, kc * 128:(kc + 1) * 128], idq_b)
                    CP("act", yT[:, :, cols], pyt[:, 0:8 * Q].r("p (k q) -> p k q", k=8))
                    yield

                def s2_stream():
                    yield from s2_front(0, s2b[0])
                    for c in range(NCH):
                        if c + 1 < NCH:
                            yield from s2_front(c + 1, s2b[(c + 1) % 2])
                        yield from s2_back(c, s2b[c % 2])
                    if last_tile:
                        po = psum(2)
                        for blk in range(8):
                            g, jj = blk // 4, blk % 4
                            TR(po[:, g * 512 + jj * 64:g * 512 + (jj + 1) * 64], S["hT_f"][g * 64:(g + 1) * 64, jj * 128:(jj + 1) * 128],
                               identf[g * 64:(g + 1) * 64, g * 64:(g + 1) * 64])
                        natn = yo.r("p (k n) -> p k n", k=16)[:, 0:8, :]
                        CP("act", natn.r("p (g j) n -> p g (j n)", g=2), po.r("p (g x) -> p g x", g=2)[:, :, 0:256])
                        DMA("sp", dr["ssd_p"][l].rearrange("(k q) n -> q k n", q=128), natn)

                def m1_stream():
                    wb = nextw()
                    for cc in range(8):
                        proj_feat(xT, wb, cc * 128, lambda ps, cc=cc: CP("act", qT[:, cc, :], ps))
                        yield
                    wb = nextw()
                    for cc in range(8):
                        proj_feat(xT, wb, cc * 128, lambda ps, cc=cc: ACT(kT[:, cc, :], ps, AF.Copy, scale=0.0625))
                        yield
                    for c in range(NCH):
                        for hf in range(2):
                            proj_tok(xT, c, wb, hf * 512, 512,
                                     lambda ps, cb0, n, c=c, hf=hf: ACT(k_tok[c][0:Q, hf * 512:hf * 512 + n], ps, AF.Copy, scale=0.0625))
                            yield
                    wbv[0] = nextw()
                    wb = wbv[0]
                    psg = psum(1)
                    for c in range(NCH):
                        for kc in range(8):
                            MM(psg[0:Q, c * 8:(c + 1) * 8], xT[:, kc, c * Q:(c + 1) * Q], wb[:, kc, 1024:1032], kc == 0, kc == 7)
                    TT("dve", gx[0:Q], psg[0:Q, 0:NCH * 8].r("p (c h) -> p c h", c=NCH), gbp[0:Q].us(1).bc([Q, NCH, 8]), ALU.add)
                    yield
                    CP("dve", gi[0:Q], gx[0:Q, :, 0:4])
                    STT(gta[0:Q], gx[0:Q, :, 4:8], -1.0, gx[0:Q, :, 4:8], ALU.mult, ALU.max)
                    ACT(gta[0:Q], gta[0:Q], AF.Exp, scale=-1.0)
                    ACT(gta[0:Q], gta[0:Q], AF.Ln, bias=1.0)
                    STT(logf[0:Q], gx[0:Q, :, 4:8], 0.0, gta[0:Q], ALU.min, ALU.subtract)
                    yield

                wbv = [None]
                run_streams([(s2_stream(), list(range(5))), (m1_stream(), [5, 6, 7])])
                release(mS)
                cut('M1')
                hmT = alloc((8, TTK), BF16)
                v_tok = [alloc(D, BF16) for _ in range(NCH)]
                o_tok = [alloc(D, BF16) for _ in range(NCH)]
                mnw = prow("mnw", Q)
                TS("dve", mnw, mnw, 0.5)

                mT = alloc((8, TTK), BF16)
                sgA = [alloc(TTK, BF16) for _ in range(2)]
                mM2 = mark()
                if prompt:
                    nT_f, nT_b, m_st = S["nT_f"], S["nT_b"], S["m_st"][0:1, :]
                else:
                    nT_f = alloc((2, 4, NS), F32); nT_b = alloc((2, 4, NS), BF16); m_st = alloc(4, F32)[0:NS, :]
                    mm = mark()
                    nat = alloc(256, F32)
                    DMA("sp", nat[0:64, :], dr["s_n"][l])
                    DMA("sp", m_st, dr["s_m"][l])
                    for dc in range(2):
                        ps = psum(1)
                        TR(ps[:, 0:64], nat[0:64, dc * 128:(dc + 1) * 128], identf[0:64, 0:64])
                        CP("dve", nT_f[:, dc].r("p h b -> p b h"), ps[:, 0:64].r("p (b h) -> p b h", h=4))
                    CP("act", nT_b.r("p a h b -> p (a h b)"), nT_f.r("p a h b -> p (a h b)"))
                    release(mm)
                m2b = []
                for _p in range(2 if NCH > 1 else 1):
                    m2b.append({"sm": alloc(48, F32), "gs": alloc(8, F32), "SwT": alloc((4, Q), BF16), "kw": alloc(D, BF16),
                                "wcb": alloc((nseq, 4), F32), "mprev": alloc(4, F32), "pes": alloc(12, F32)})
                R2 = alloc((4, Q), F32); Wt = alloc((4, Q), BF16); Sw = alloc((4, Q), BF16)
                ws = alloc(4, F32); wcs = alloc(4, F32); Z = alloc((nseq, 4), F32)
                ni = alloc(D, F32); denI = alloc(4, F32); emt = alloc(4, F32); junk = alloc(256, BF16)
                ow = alloc(D, F32); hmn = alloc(D, BF16)
                tmpd = None if prompt else alloc((4, NS), F32)
                if not prompt:
                    cq = [{"C_b": alloc((2, 4, 256), BF16), "kwm": alloc(D, BF16)} for _ in range(2)]
                    Cf_l = [alloc((2, 4, 256), F32) for _ in range(5)]

                def m2_front(c, B):
                    cols = slice(c * Q, (c + 1) * Q)
                    sm, gs, SwT, kw, wcb, mprev, pes = B["sm"], B["gs"], B["SwT"], B["kw"], B["wcb"], B["mprev"], B["pes"]
                    pg = psum(1)
                    MM(pg[0:Q, 0:4], Lm, logf[0:Q, c, :])
                    MM(pg[0:Q, 4:8], bmT, m_st)
                    CP("dve", gs[0:Q], pg[0:Q, 0:8])
                    bcum = gs[0:Q, 0:4]; m_tok = gs[0:Q, 4:8]
                    a_ = sm[0:Q, 0:4]; mloc = sm[0:Q, 4:8]; mxx = sm[0:Q, 8:12]; nmxx = sm[0:Q, 12:16]
                    wint = sm[0:Q, 16:20]; den = sm[0:Q, 20:24]; mt = sm[0:Q, 24:28]
                    TT("dve", a_, gi[0:Q, c, :], bcum, ALU.subtract)
                    TT("pool", R2[0:Q], idq_f.us(1).bc([Q, 4, Q]), a_.us(2).bc([Q, 4, Q]), ALU.mult)
                    yield
                    A = psum(1)
                    MM(A[0:Q, 0:4 * Q], onesf[0:Q, 0:Q], R2[0:Q].r("p h s -> p (h s)"), True, False)
                    MM(A[0:Q, 0:4 * Q], idq_b, NEGTb, False, True)
                    RED(mloc, A[0:Q, 0:4 * Q].r("p (h s) -> p h s", h=4), ALU.max)
                    TT("dve", mxx, mloc, m_tok, ALU.max)
                    TS("dve", nmxx, mxx, -1.0)
                    TT("dve", mt, bcum, mxx, ALU.add)
                    for h in range(4):
                        ACT(Wt[0:Q, h, :], A[0:Q, h * Q:(h + 1) * Q], AF.Exp, bias=nmxx[:, h:h + 1])
                    pe_ = psum(1)
                    MM(pe_[0:Q, 0:4], SELLAST, mxx)
                    MM(pe_[0:nseq, 4:8], SELROW, mt)
                    MM(pe_[0:nseq, 8:12], SELROW, mxx)
                    CP("dve", pes[0:Q, 0:4], pe_[0:Q, 0:4])
                    CP("dve", pes[0:nseq, 4:12], pe_[0:nseq, 4:12])
                    CP("dve", mprev[0:nseq], m_st)
                    CP("dve", m_st, pes[0:nseq, 4:8])
                    yield
                    QK = psum(1)
                    for h in range(4):
                        for dc in range(2):
                            MM(QK[0:Q, h * Q:(h + 1) * Q], qT[:, h * 2 + dc, cols], kT[:, h * 2 + dc, cols], dc == 0, dc == 1)
                    TT("dve", Sw[0:Q].r("p h s -> p (h s)"), Wt[0:Q].r("p h s -> p (h s)"), QK[0:Q, 0:4 * Q], ALU.mult)
                    TT("dve", wint, m_tok, mxx, ALU.subtract)
                    ACT(wint, wint, AF.Exp)
                    TT("dve", ws[0:Q], a_, pes[0:Q, 0:4], ALU.subtract)
                    ACT(ws[0:Q], ws[0:Q], AF.Exp)
                    TT("dve", kw[0:Q].r("p (h e) -> p h e", h=4), k_tok[c][0:Q].r("p (h e) -> p h e", h=4),
                       ws[0:Q].us(2).bc([Q, 4, 256]), ALU.mult)
                    TT("dve", wcs[0:nseq], mprev[0:nseq], pes[0:nseq, 8:12], ALU.subtract)
                    ACT(wcs[0:nseq], wcs[0:nseq], AF.Exp)
                    TT("dve", Z[0:nseq], identf[0:nseq, 0:nseq].us(2).bc([nseq, nseq, 4]),
                       wcs[0:nseq].us(1).bc([nseq, nseq, 4]), ALU.mult)
                    yield
                    pw = psum(1).bitcast(BF16)
                    for h in range(4):
                        TR(pw[0:Q, h * Q:(h + 1) * Q], Sw[0:Q, h, :], idq_b)
                    CP("act", SwT[0:Q].r("p h s -> p (h s)"), pw[0:Q, 0:4 * Q])
                    pwc = psum(1)
                    MM(pwc[:, 0:nseq * 4], onesf[0:nseq, 0:128], Z[0:nseq].r("p b h -> p (b h)"))
                    CP("act", wcb.r("p b h -> p (b h)"), pwc[:, 0:nseq * 4])
                    yield

                def m2_back(c, B):
                    cols = slice(c * Q, (c + 1) * Q)
                    sm, gs, SwT, kw, wcb = B["sm"], B["gs"], B["SwT"], B["kw"], B["wcb"]
                    bcum = gs[0:Q, 0:4]
                    wint = sm[0:Q, 16:20]; den = sm[0:Q, 20:24]; mt = sm[0:Q, 24:28]; r_ = sm[0:Q, 28:32]
                    ssq = sm[0:Q, 32:36]; rstd = sm[0:Q, 36:40]
                    pden = psum(1)
                    for h in range(4):
                        MM(pden[0:Q, h:h + 1], SwT[0:Q, h, :], onesb[0:Q, 0:1])
                    for h in range(4):
                        for dc in range(2):
                            MM(pden[0:Q, 8 + h * nseq:8 + (h + 1) * nseq], qT[:, h * 2 + dc, cols], nT_b[:, dc, h, :], dc == 0, dc == 1)
                    if prompt:
                        CP("dve", denI[0:Q], pden[0:Q, 8:12])
                    else:
                        TT("dve", tmpd[0:Q], pden[0:Q, 8:8 + 4 * NS].r("p (h b) -> p h b", h=4), bm.us(1).bc([Q, 4, NS]), ALU.mult)
                        RED(denI[0:Q], tmpd[0:Q], ALU.add)
                    TT("dve", denI[0:Q], denI[0:Q], wint, ALU.mult)
                    TT("dve", den, denI[0:Q], pden[0:Q, 0:4], ALU.add)
                    def load_C(b2):
                        for a2 in range(2):
                            DMA("sp", Cf_l[b2 % 5][:, a2], dr["s_c"][l, b2][:, a2 * 128:(a2 + 1) * 128, :].rearrange("h q e -> q h e"))

                    if not prompt:
                        for b2 in range(3):
                            load_C(b2)
                    for b in range(nseq):
                        if prompt:
                            C_f, C_b = S["C_f"], S["C_b"]
                            kwm = kw
                        else:
                            if b + 3 < nseq:
                                load_C(b + 3)
                            cb2 = cq[b % 2]
                            C_f, C_b, kwm = Cf_l[b % 5], cb2["C_b"], cb2["kwm"]
                            CP("act", C_b.r("p a h e -> p (a h e)"), C_f.r("p a h e -> p (a h e)"))
                            TS("dve", kwm[0:Q], kw[0:Q], bm[:, b:b + 1])
                        NI = psum(2)
                        for h in range(4):
                            for dc in range(2):
                                MM(NI[0:Q, h * 256:(h + 1) * 256], qT[:, h * 2 + dc, cols], C_b[:, dc, h, :], dc == 0, dc == 1)
                        if prompt:
                            TT("dve", ni[0:Q].r("p (h e) -> p h e", h=4), NI[0:Q, :].r("p (h e) -> p h e", h=4),
                               wint.us(2).bc([Q, 4, 256]), ALU.mult)
                        elif b == 0:
                            TS("dve", ni[0:Q], NI[0:Q, :], bm[:, b:b + 1])
                        else:
                            STT(ni[0:Q], NI[0:Q, :], bm[:, b:b + 1], ni[0:Q], ALU.mult, ALU.add)
                        for dc in range(2):
                            Cn = psum(2)
                            for h in range(4):
                                MM(Cn[:, h * 256:(h + 1) * 256], kwm[0:Q, h * 256 + dc * 128:h * 256 + (dc + 1) * 128],
                                   v_tok[c][0:Q, h * 256:(h + 1) * 256])
                            for h in range(4):
                                STT(C_f[:, dc, h, :], C_f[:, dc, h, :], wcb[:, b, h:h + 1], Cn[:, h * 256:(h + 1) * 256],
                                    ALU.mult, ALU.add)
                        if prompt:
                            CP("act", C_b.r("p a h e -> p (a h e)"), C_f.r("p a h e -> p (a h e)"))
                        else:
                            for a2 in range(2):
                                DMA("sp", dr["c_s"][l, b][:, a2 * 128:(a2 + 1) * 128, :].rearrange("h q e -> q h e"), C_f[:, a2])
                        yield
                    if not prompt:
                        TT("dve", ni[0:Q].r("p (h e) -> p h e", h=4), ni[0:Q].r("p (h e) -> p h e", h=4),
                           wint.us(2).bc([Q, 4, 256]), ALU.mult)
                    pn2 = psum(1)
                    for dc in range(2):
                        for h in range(4):
                            MM(pn2[:, (dc * 4 + h) * nseq:(dc * 4 + h + 1) * nseq],
                               kw[0:Q, h * 256 + dc * 128:h * 256 + (dc + 1) * 128], bmb)
                    for dc in range(2):
                        TT("dve", nT_f[:, dc], nT_f[:, dc], wcb.r("p b h -> p h b"), ALU.mult)
                    TT("dve", nT_f.r("p a h b -> p (a h b)"), nT_f.r("p a h b -> p (a h b)"), pn2[:, 0:8 * nseq], ALU.add)
                    CP("act", nT_b.r("p a h b -> p (a h b)"), nT_f.r("p a h b -> p (a h b)"))
                    NUM = psum(2)
                    for h in range(4):
                        MM(NUM[0:Q, h * 256:(h + 1) * 256], SwT[0:Q, h, :], v_tok[c][0:Q, h * 256:(h + 1) * 256])
                    TT("dve", ni[0:Q], ni[0:Q], NUM[0:Q, :], ALU.add)
                    ACT(emt[0:Q], mt, AF.Exp, scale=-1.0)
                    STT(den, den, -1.0, den, ALU.mult, ALU.max)
                    TT("dve", den, den, emt[0:Q], ALU.max)
                    P.rec("dve", lambda e, o=r_.ap, i=den.ap: e.reciprocal(out=o, in_=i), reads=[den], writes=[r_])
                    TT("dve", ni[0:Q].r("p (h e) -> p h e", h=4), ni[0:Q].r("p (h e) -> p h e", h=4),
                       r_.us(2).bc([Q, 4, 256]), ALU.mult)
                    for h in range(4):
                        ACT(junk[0:Q], ni[0:Q, h * 256:(h + 1) * 256], AF.Square, accum=ssq[:, h:h + 1])
                    rsqrt_(rstd, ssq, 1.0 / 256, Q, 4)
                    STT(ow[0:Q], o_tok[c][0:Q], 1.0, mnw, ALU.add, ALU.mult)
                    for h in range(4):
                        STT(hmn[0:Q, h * 256:(h + 1) * 256], ni[0:Q, h * 256:(h + 1) * 256], rstd[:, h:h + 1],
                            ow[0:Q, h * 256:(h + 1) * 256], ALU.mult, ALU.mult)
                    yield
                    pht = psum(1).bitcast(BF16)
                    for kc in range(8):
                        TR(pht[:, kc * Q:(kc + 1) * Q], hmn[0:Q, kc * 128:(kc + 1) * 128], idq_b)
                    CP("act", hmT[:, :, cols], pht[:, 0:8 * Q].r("p (k q) -> p k q", k=8))
                    yield

                def m2_stream():
                    yield from m2_front(0, m2b[0])
                    for c in range(NCH):
                        if c + 1 < NCH:
                            yield from m2_front(c + 1, m2b[(c + 1) % 2])
                        yield from m2_back(c, m2b[c % 2])
                    if last_tile or not prompt:
                        nrow = 4 * nseq
                        po = psum(1)
                        tmpn = ni.r("p (a x) -> p a x", a=2)[:, :, 0:nseq * 4].r("p a (b h) -> p a b h", h=4)
                        for dc in range(2):
                            CP("dve", tmpn[:, dc], nT_f[:, dc].r("p h b -> p b h"))
                            TR(po[0:nrow, dc * 128:(dc + 1) * 128], tmpn[:, dc].r("p b h -> p (b h)"), identf)
                        natn = ow[:, 0:256]
                        CP("act", natn[0:nrow], po[0:nrow, 0:256])
                        DMA("sp", dr["n_p"][l] if prompt else dr["n_s"][l], natn[0:nrow])
                        DMA("sp", dr["m_p"][l] if prompt else dr["m_s"][l], m_st)
                        if prompt:
                            for a2 in range(2):
                                DMA("sp", dr["c_p"][l][:, a2 * 128:(a2 + 1) * 128, :].rearrange("h q e -> q h e"), S["C_f"][:, a2])

                def ga_stream():
                    wbo = None
                    for c in range(NCH):
                        for hf in range(2):
                            proj_tok(xT, c, wbv[0], hf * 512, 512,
                                     lambda ps, cb0, n, c=c, hf=hf: CP("act", v_tok[c][0:Q, hf * 512:hf * 512 + n], ps))
                            yield
                        if wbo is None:
                            wbo = nextw(prefetch=False)
                        for hf in range(2):
                            proj_tok(xT, c, wbo, hf * 512, 512,
                                     lambda ps, cb0, n, c=c, hf=hf: ACT(o_tok[c][0:Q, hf * 512:hf * 512 + n], ps, AF.Tanh, scale=0.5))
                            yield
                    wa = nextw()
                    wg = nextw(prefetch=False)
                    for j in range(8):
                        pb = psum(1); pgt = psum(1)
                        for kc in range(8):
                            MM(pgt[:, 0:TTK], wg[:, kc, j * 128:(j + 1) * 128], xT[:, kc, :], kc == 0, kc == 7)
                        for kc in range(8):
                            MM(pb[:, 0:TTK], wa[:, kc, j * 128:(j + 1) * 128], yT[:, kc, :], kc == 0, kc == 7)
                        sg = sgA[j % 2]
                        ACT(sg, pgt[:, 0:TTK], AF.Tanh, scale=0.5)
                        STT(mT[:, j, :], sg, 1.0, pb[:, 0:TTK], ALU.add, ALU.mult)
                        yield

                run_streams([(m2_stream(), list(range(5))), (ga_stream(), [5, 6, 7])])
                release(mM2)
                cut('M2')
                cur_ring[0] = make_ring(12 * 1024)
                wa = nextw()
                wg = nextw(prefetch=False)
                for j in range(8):
                    pb = psum(1); pgt = psum(1)
                    for kc in range(8):
                        MM(pgt[:, 0:TTK], wg[:, kc, j * 128:(j + 1) * 128], xT[:, kc, :], kc == 0, kc == 7)
                    for kc in range(8):
                        MM(pb[:, 0:TTK], wa[:, kc, j * 128:(j + 1) * 128], hmT[:, kc, :], kc == 0, kc == 7)
                    sg = sgA[j % 2]
                    ACT(sg, pgt[:, 0:TTK], AF.Tanh, scale=0.5)
                    t2 = cur_ring[0](TTK, F32)
                    STT(t2, sg, 1.0, pb[:, 0:TTK], ALU.add, ALU.mult)
                    TT("dve", mT[:, j, :], mT[:, j, :], t2, ALU.add)

                def layer_norm(c, pss, g_, b_, mixscale=1.0):
                    st6 = cur_ring[0](12, F32); mv = cur_ring[0](4, F32)
                    for hf in range(2):
                        xv = x_tok[0:Q, c, hf * 512:(hf + 1) * 512]
                        if mixscale == 1.0:
                            STT(xv, xv, ALPHA, pss[hf], ALU.mult, ALU.add)
                        else:
                            TS("dve", xv, xv, ALPHA)
                            STT(xv, pss[hf], mixscale, xv, ALU.mult, ALU.add)
                        P.rec("dve", lambda e, o=st6[0:Q, hf * 6:(hf + 1) * 6].ap, i=xv.ap: e.bn_stats(out=o, in_=i),
                              reads=[xv], writes=[st6])
                    P.rec("dve", lambda e, o=mv[0:Q, 0:2].ap, i=st6[0:Q].ap: e.bn_aggr(out=o, in_=i), reads=[st6], writes=[mv])
                    rsqrt_(mv[0:Q, 2:3], mv[0:Q, 1:2], 1.0, Q, 1)
                    xv = x_tok[0:Q, c, :]
                    TS("dve", xv, xv, mv[0:Q, 0:1], mv[0:Q, 2:3], ALU.subtract, ALU.mult)
                    TT("dve", xv, xv, g_, ALU.mult)
                    TT("dve", xv, xv, b_, ALU.add)

                wb = nextw()
                l1g = prow("l1g", Q); l1b = prow("l1b", Q)
                for c in range(NCH):
                    pss = []
                    for hf in range(2):
                        ps = psum(1)
                        for kc in range(8):
                            MM(ps[0:Q, :], mT[:, kc, c * Q:(c + 1) * Q], wb[:, kc, hf * 512:(hf + 1) * 512], kc == 0, kc == 7)
                        pss.append(ps[0:Q, :])
                    layer_norm(c, pss, l1g, l1b, mixscale=0.5)

                cut('G')
                release(m0)
                m0b = mark()
                x1T = alloc((8, TTK), BF16)
                to_featmajor(x_tok, x1T)
                hT = alloc((22, TTK), BF16)
                histF = None
                if not prompt:
                    histF = alloc((44, NS * 2), F32)
                    for piece in range(4):
                        mm = mark()
                        nat = alloc(1408, F32)
                        DMA("sp", nat[0:NS * 2, :], dr["s_fconv"][l, :, piece * 1408:(piece + 1) * 1408])
                        for k in range(11):
                            cc = piece * 11 + k
                            ps = psum(1)
                            TR(ps[:, 0:NS * 2], nat[0:NS * 2, k * 128:(k + 1) * 128], identf[0:NS * 2, 0:NS * 2])
                            CPX(histF[:, cc, :], ps[:, 0:NS * 2])
                        release(mm)

                cur_ring[0] = make_ring(40 * 1024)

                def ffn_conv(ps, cc, silu):
                    xp = cur_ring[0]((nseq, 2 + Lq), F32)
                    acc = cur_ring[0]((nseq, Lq), F32)
                    if prompt:
                        CP("act", xp[:, 0, 0:2], S["tl_f"][:, cc, :])
                    else:
                        CP("act", xp[:, :, 0:2], histF[:, cc, :].r("p (b j) -> p b j", j=2))
                    CP("act", xp[:, :, 2:2 + Lq], ps.r("p (b t) -> p b t", b=nseq))
                    ACT(acc, ps.r("p (b t) -> p b t", b=nseq), AF.Identity, scale=cwf[:, cc, 2:3], bias=cbf[:, cc:cc + 1])
                    for j in (1, 0):
                        STT(acc, xp[:, :, j:j + Lq], cwf[:, cc, j:j + 1], acc, ALU.mult, ALU.add)
                    if prompt:
                        CP("act", S["tl_f"][:, cc, :], xp[:, 0, Lq:Lq + 2])
                    return acc

                for bb in range(6):
                    nj = 4 if bb < 5 else 2
                    n = nj * 128
                    wb = nextw()
                    for jj in range(nj):
                        j = bb * 4 + jj
                        pg_ = psum(1); pv_ = psum(1)
                        for kc in range(8):
                            MM(pg_[:, 0:TTK], wb[:, kc, jj * 128:(jj + 1) * 128], x1T[:, kc, :], kc == 0, kc == 7)
                        for kc in range(8):
                            MM(pv_[:, 0:TTK], wb[:, kc, n + jj * 128:n + (jj + 1) * 128], x1T[:, kc, :], kc == 0, kc == 7)
                        ag = ffn_conv(pg_[:, 0:TTK], j, True)
                        av = ffn_conv(pv_[:, 0:TTK], 22 + j, False)
                        sg = cur_ring[0]((nseq, Lq), BF16)
                        ACT(sg, ag, AF.Silu)
                        TT("dve", hT[:, j, :].r("p (b t) -> p b t", b=nseq), sg, av, ALU.mult)
                    if want_rows:
                        emit_rows(x1T, wb, 0, n, dr["fconv_p"], dr["fconv_s"], 2, bb * 512)
                        emit_rows(x1T, wb, n, n, dr["fconv_p"], dr["fconv_s"], 2, DFF + bb * 512)
                l2g = prow("l2g", Q); l2b = prow("l2b", Q)
                for cb0 in range(2):
                    pss = [psum(1) for _ in range(NCH)]
                    for kh in range(2):
                        wb = nextw()
                        for c in range(NCH):
                            for k in range(11):
                                kc = kh * 11 + k
                                MM(pss[c][0:Q, :], hT[:, kc, c * Q:(c + 1) * Q], wb[:, k, :], kc == 0, kc == 21)
                    if cb0 == 0:
                        keep = []
                        for c in range(NCH):
                            t = alloc(512, F32)
                            CPX(t[0:Q], pss[c][0:Q, :])
                            keep.append(t[0:Q])
                    else:
                        for c in range(NCH):
                            layer_norm(c, [keep[c], pss[c][0:Q, :]], l2g, l2b)
                release(m0b)
                release(m0)

            for l in range(DEPTH):
                if l >= KL:
                    continue
                base_m = mark()
                try:
                    layer_body(l)
                except _Cut:
                    pass
                release(base_m)
                gctr[0] = (tiles.index((kind, ti)) * DEPTH + l + 1) * NWB
                wstate["issued"] = max(wstate["issued"], gctr[0])
            if prompt:
                DMA("sp", dr["y_p"][ti * 512:(ti + 1) * 512, :].rearrange("(c p) d -> p c d", p=128), x_tok)
            else:
                DMA("sp", dr["y_s"], x_tok[0:Q, 0, :])

        print("final peak", astate.get("peak"))
        P.emit()
        build_program.stats = P.stats
    return nc


_NC_CACHE = {}


def _get_nc():
    if "nc" not in _NC_CACHE:
        _NC_CACHE["nc"] = build_program()
    return _NC_CACHE["nc"]


def kernel(x_prompt, x_sample, state_ssd, state_ssd_conv, state_mlstm_c, state_mlstm_n, state_mlstm_m,
           state_ffn_conv, w_in, ssd_conv_w, ssd_conv_b, ssd_dt_bias, ssd_a_log, ssd_d, ssd_norm_w,
           mlstm_gate_b, mlstm_norm_w, w_branch_a, w_branch_b, w_out, ln1_g, ln1_b, ffn_w_up, ffn_conv_w,
           ffn_conv_b, ffn_w_down, ln2_g, ln2_b):
    f = lambda a: np.ascontiguousarray(np.asarray(a, dtype=np.float32))
    nc = _get_nc()
    prow = np.zeros((DEPTH, NPR), np.float32)
    for name, arr in (("dtb", ssd_dt_bias), ("alog", ssd_a_log), ("D", ssd_d), ("gb", mlstm_gate_b),
                      ("snw", ssd_norm_w), ("mnw", mlstm_norm_w), ("l1g", ln1_g), ("l1b", ln1_b),
                      ("l2g", ln2_g), ("l2b", ln2_b)):
        o, n = PR[name]
        prow[:, o:o + n] = f(arr)
    cw_s = f(ssd_conv_w).reshape(DEPTH, 4, 10, 128).transpose(0, 3, 2, 1).reshape(DEPTH, 128, 40)
    cb_s = f(ssd_conv_b).reshape(DEPTH, 10, 128).transpose(0, 2, 1)
    cw_f = f(ffn_conv_w).reshape(DEPTH, 3, 44, 128).transpose(0, 3, 2, 1).reshape(DEPTH, 128, 132)
    cb_f = f(ffn_conv_b).reshape(DEPTH, 44, 128).transpose(0, 2, 1)
    consts = _make_consts()
    shared = {"w_in": f(w_in), "w_a": f(w_branch_a), "w_b": f(w_branch_b), "w_out": f(w_out), "w_up": f(ffn_w_up),
              "w_down": f(ffn_w_down), "prow": prow, "cw_s": f(cw_s), "cb_s": f(cb_s), "cw_f": f(cw_f),
              "cb_f": f(cb_f), "consts": consts}
    xp = f(x_prompt); xs = f(x_sample)
    s_ssd = f(state_ssd); s_sc = f(state_ssd_conv); s_c = f(state_mlstm_c); s_n = f(state_mlstm_n)
    s_m = f(state_mlstm_m); s_fc = f(state_ffn_conv)
    in_maps = []
    for i in range(NCORES):
        sl = slice(i * NS, (i + 1) * NS)
        m = dict(shared)
        m["xp"] = xp[i]
        m["xs"] = np.ascontiguousarray(xs[sl].reshape(NS * SL, D))
        m["s_ssd"] = np.ascontiguousarray(s_ssd[:, sl].reshape(DEPTH, NS, 1024, 64))
        m["s_sconv"] = np.ascontiguousarray(s_sc[:, sl].reshape(DEPTH, NS * 3, 1280))
        m["s_c"] = np.ascontiguousarray(s_c[:, sl])
        m["s_n"] = np.ascontiguousarray(s_n[:, sl].reshape(DEPTH, NS * 4, 256))
        m["s_m"] = np.ascontiguousarray(s_m[:, sl])
        m["s_fconv"] = np.ascontiguousarray(s_fc[:, sl].reshape(DEPTH, NS * 2, 2 * DFF))
        in_maps.append(m)
    res = run_bass_kernel_spmd(nc, in_maps, core_ids=list(range(NCORES)))
    R = res.results
    cat0 = lambda k: np.stack([R[i][k] for i in range(NCORES)], 0)
    y_p = cat0("y_p")
    y_s = cat0("y_s").reshape(NCORES * NS, SL, D)
    ssd_p = np.stack([R[i]["ssd_p"] for i in range(NCORES)], 1).reshape(DEPTH, NCORES, 16, 64, 64)
    ssd_s = np.concatenate([R[i]["ssd_s"] for i in range(NCORES)], 1).reshape(DEPTH, NCORES * NS, 16, 64, 64)
    sconv_p = np.stack([R[i]["sconv_p"] for i in range(NCORES)], 1)
    sconv_s = np.concatenate([R[i]["sconv_s"] for i in range(NCORES)], 1)
    c_p = np.stack([R[i]["c_p"] for i in range(NCORES)], 1)
    c_s = np.concatenate([R[i]["c_s"] for i in range(NCORES)], 1)
    n_p = np.stack([R[i]["n_p"] for i in range(NCORES)], 1)
    n_s = np.concatenate([R[i]["n_s"].reshape(DEPTH, NS, 4, 256) for i in range(NCORES)], 1)
    m_p = np.stack([R[i]["m_p"].reshape(DEPTH, 4) for i in range(NCORES)], 1)
    m_s = np.concatenate([R[i]["m_s"] for i in range(NCORES)], 1)
    fconv_p = np.stack([R[i]["fconv_p"] for i in range(NCORES)], 1)
    fconv_s = np.concatenate([R[i]["fconv_s"] for i in range(NCORES)], 1)
    outs = (y_p, y_s, ssd_p, ssd_s, sconv_p, sconv_s, c_p, c_s, n_p, n_s, m_p, m_s, fconv_p, fconv_s)
    return tuple(np.ascontiguousarray(o, dtype=np.float32) for o in outs)
```
